# Optimizing a Trainium2 kernel written in Bass

```python
import math
import jax, jax.numpy as jnp
from jax import lax
import numpy as np

D_MODEL = 1024
BATCH = 16
SEQ = 2048
DEPTH = 2
DEC_BATCH = 32
DEC_SEQ = 64
PAST_LEN = 2048

CHUNK = 64
N_A = DEPTH // 2
N_B = DEPTH - N_A
EPS = 1e-6

SSM_EXPAND = 2
D_INNER = SSM_EXPAND * D_MODEL
SSM_HEADDIM = 64
SSM_HEADS = D_INNER // SSM_HEADDIM
SSM_GROUPS = 4
SSM_HPG = SSM_HEADS // SSM_GROUPS
D_STATE = 128
SSM_CONV = 4
CONV_DIM = D_INNER + 2 * SSM_GROUPS * D_STATE
IN_PROJ_DIM = D_INNER + CONV_DIM + SSM_HEADS

MLA_HEADS = 16
Q_RANK = 384
KV_RANK = 256
QK_NOPE = 128
QK_ROPE = 64
V_DIM = 128
ROPE_THETA = 10000.0
Q_BLOCK = 128

D_FF = 2816
FFN_CONV = 3

kernel_name = 'streaming_ssd_mla_yoco_step'

F32 = jnp.float32


def rmsnorm(x, g):
    xf = x.astype(F32)
    y = xf * lax.rsqrt(jnp.mean(xf * xf, axis=-1, keepdims=True) + EPS)
    return (y * g.astype(F32)).astype(x.dtype)


def causal_dwconv(u, buf, w, b):
    k = w.shape[0]
    t = u.shape[1]
    ext = jnp.concatenate([buf.astype(u.dtype), u], axis=1)
    out = ext[:, 0:t] * w[0]
    for i in range(1, k):
        out = out + ext[:, i:i + t] * w[i]
    return out + b, ext[:, t:]


def rope(x, pos):
    half = QK_ROPE // 2
    inv = jnp.exp(-math.log(ROPE_THETA) * jnp.arange(half, dtype=F32) / half)
    ang = pos.astype(F32)[:, None] * inv[None, :]
    shp = (pos.shape[0],) + (1,) * (x.ndim - 3) + (half,)
    cos = jnp.cos(ang).reshape(shp)
    sin = jnp.sin(ang).reshape(shp)
    xf = x.astype(F32)
    x1, x2 = xf[..., :half], xf[..., half:]
    return jnp.concatenate([x1 * cos - x2 * sin, x2 * cos + x1 * sin], axis=-1).astype(x.dtype)


def ssd_scan(x, dt, a, bmat, cmat, h0):
    bsz, t = x.shape[0], x.shape[1]
    l = min(CHUNK, t)
    nc = t // l
    xc = x.astype(F32).reshape(bsz, nc, l, SSM_GROUPS, SSM_HPG, SSM_HEADDIM)
    dtc = dt.astype(F32).reshape(bsz, nc, l, SSM_GROUPS, SSM_HPG)
    bc = bmat.astype(F32).reshape(bsz, nc, l, SSM_GROUPS, D_STATE)
    cc = cmat.astype(F32).reshape(bsz, nc, l, SSM_GROUPS, D_STATE)
    da = dtc * a.astype(F32).reshape(SSM_GROUPS, SSM_HPG)
    acs = jnp.cumsum(da, axis=2)
    seg = acs[:, :, :, None] - acs[:, :, None]
    causal = jnp.tril(jnp.ones((l, l), dtype=bool))[:, :, None, None]
    decay = jnp.exp(jnp.where(causal, seg, -jnp.inf))
    cb = jnp.einsum('bclgn,bcsgn->bclsg', cc, bc)
    wts = cb[..., None] * decay * dtc[:, :, None]
    y_diag = jnp.einsum('bclsge,bcsgep->bclgep', wts, xc)
    decay_end = jnp.exp(acs[:, :, -1:] - acs)
    states = jnp.einsum('bclgn,bclge,bclgep->bcgepn', bc, decay_end * dtc, xc)
    chunk_decay = jnp.exp(acs[:, :, -1])

    def step(h, inp):
        s_c, d_c = inp
        return h * d_c[..., None, None] + s_c, h

    h_init = h0.astype(F32).reshape(bsz, SSM_GROUPS, SSM_HPG, SSM_HEADDIM, D_STATE)
    h_last, h_prev = lax.scan(step, h_init, (jnp.moveaxis(states, 1, 0), jnp.moveaxis(chunk_decay, 1, 0)))
    h_prev = jnp.moveaxis(h_prev, 0, 1)
    y_off = jnp.einsum('bclgn,bcgepn,bclge->bclgep', cc, h_prev, jnp.exp(acs))
    y = (y_diag + y_off).reshape(bsz, t, SSM_HEADS, SSM_HEADDIM)
    return y.astype(x.dtype), h_last.reshape(bsz, SSM_HEADS, SSM_HEADDIM, D_STATE).astype(h0.dtype)


def mamba2_mixer(h, conv_buf, ssm_state, w_in, conv_w, conv_b, dt_bias, a_log, d_skip, g_norm, w_out):
    bsz, t, _ = h.shape
    zxbcdt = h @ w_in
    z, xbc, dt = jnp.split(zxbcdt, [D_INNER, D_INNER + CONV_DIM], axis=-1)
    xbc_c, new_buf = causal_dwconv(xbc, conv_buf, conv_w, conv_b)
    xbc_c = jax.nn.silu(xbc_c)
    xs, bm, cm = jnp.split(xbc_c, [D_INNER, D_INNER + SSM_GROUPS * D_STATE], axis=-1)
    xs = xs.reshape(bsz, t, SSM_HEADS, SSM_HEADDIM)
    bm = bm.reshape(bsz, t, SSM_GROUPS, D_STATE)
    cm = cm.reshape(bsz, t, SSM_GROUPS, D_STATE)
    dtp = jax.nn.softplus(dt.astype(F32) + dt_bias.astype(F32))
    a = -jnp.exp(a_log.astype(F32))
    y, new_state = ssd_scan(xs, dtp, a, bm, cm, ssm_state)
    y = y + xs * d_skip[:, None]
    yg = (y.reshape(bsz, t, D_INNER) * jax.nn.silu(z)).reshape(bsz, t, SSM_GROUPS, D_INNER // SSM_GROUPS)
    yg = rmsnorm(yg, g_norm.reshape(SSM_GROUPS, D_INNER // SSM_GROUPS)).reshape(bsz, t, D_INNER)
    return yg @ w_out, new_buf, new_state


def conv_ffn(h, buf, w_up, conv_w, conv_b, w_down):
    u = h @ w_up
    u, new_buf = causal_dwconv(u, buf, conv_w, conv_b)
    val, gate = jnp.split(u, 2, axis=-1)
    return (jax.nn.gelu(gate, approximate=True) * val) @ w_down, new_buf


def mla_shared_kv(x, pos, g_in, w_dkv, g_kv, w_kr):
    h = rmsnorm(x, g_in)
    c_kv = rmsnorm(h @ w_dkv, g_kv)
    k_pe = rope(h @ w_kr, pos)
    return c_kv, k_pe


def mla_attend(h, pos_q, c_kv, k_pe, pos_k, w_dq, g_q, w_uq, w_uk, w_uv, w_o):
    bsz, t, _ = h.shape
    c_q = rmsnorm(h @ w_dq, g_q)
    q = (c_q @ w_uq).reshape(bsz, t, MLA_HEADS, QK_NOPE + QK_ROPE)
    q_nope, q_pe = q[..., :QK_NOPE], q[..., QK_NOPE:]
    q_pe = rope(q_pe, pos_q)
    q_lat = jnp.einsum('bthd,rhd->bthr', q_nope, w_uk)
    scale = (QK_NOPE + QK_ROPE) ** -0.5
    blk = min(Q_BLOCK, t)
    nb = t // blk
    ql_b = q_lat.reshape(bsz, nb, blk, MLA_HEADS, KV_RANK).swapaxes(0, 1)
    qp_b = q_pe.reshape(bsz, nb, blk, MLA_HEADS, QK_ROPE).swapaxes(0, 1)
    pq_b = pos_q.reshape(nb, blk)
    k_chunk = pos_k // CHUNK

    def attend_block(args):
        ql, qp, pq = args
        s = jnp.einsum('bqhr,bkr->bhqk', ql, c_kv).astype(F32) + jnp.einsum('bqhd,bkd->bhqk', qp, k_pe).astype(F32)
        s = s * scale
        mask = (pq // CHUNK)[:, None] >= k_chunk[None, :]
        s = jnp.where(mask[None, None], s, -jnp.inf)
        p = jax.nn.softmax(s, axis=-1).astype(c_kv.dtype)
        return jnp.einsum('bhqk,bkr->bqhr', p, c_kv)

    o_lat = lax.map(attend_block, (ql_b, qp_b, pq_b))
    o_lat = o_lat.swapaxes(0, 1).reshape(bsz, t, MLA_HEADS, KV_RANK)
    o = jnp.einsum('bthr,rhv->bthv', o_lat, w_uv).reshape(bsz, t, MLA_HEADS * V_DIM)
    return o @ w_o


def trunk(x, ssm_conv_buf, ssm_state, ffn_buf, past_lat, past_kpe, weights):
    (norm_mix_pre, norm_mix_post, norm_ffn_pre, norm_ffn_post,
     ssm_w_in, ssm_conv_w, ssm_conv_b, ssm_dt_bias, ssm_a_log, ssm_d, ssm_norm, ssm_w_out,
     kv_norm_in, kv_w_dkv, kv_norm, kv_w_kr, kv_w_uk, kv_w_uv,
     mla_w_dq, mla_q_norm, mla_w_uq, mla_w_o,
     ffn_w_up, ffn_conv_w, ffn_conv_b, ffn_w_down) = weights
    t = x.shape[1]
    past = past_lat.shape[1]
    pos_q = past + jnp.arange(t, dtype=jnp.int32)
    pos_k = jnp.arange(past + t, dtype=jnp.int32)
    new_ssm_conv, new_ssm_state, new_ffn = [], [], []
    c_kv = k_pe = new_lat = new_kpe = None
    for layer in range(DEPTH):
        h = rmsnorm(x, norm_mix_pre[layer])
        if layer < N_A:
            mix, cbuf, st = mamba2_mixer(h, ssm_conv_buf[layer], ssm_state[layer], ssm_w_in[layer], ssm_conv_w[layer],
                                         ssm_conv_b[layer], ssm_dt_bias[layer], ssm_a_log[layer], ssm_d[layer],
                                         ssm_norm[layer], ssm_w_out[layer])
            new_ssm_conv.append(cbuf)
            new_ssm_state.append(st)
        else:
            j = layer - N_A
            mix = mla_attend(h, pos_q, c_kv, k_pe, pos_k, mla_w_dq[j], mla_q_norm[j], mla_w_uq[j], kv_w_uk, kv_w_uv, mla_w_o[j])
        x = x + rmsnorm(mix, norm_mix_post[layer])
        f, fbuf = conv_ffn(rmsnorm(x, norm_ffn_pre[layer]), ffn_buf[layer], ffn_w_up[layer], ffn_conv_w[layer],
                           ffn_conv_b[layer], ffn_w_down[layer])
        x = x + rmsnorm(f, norm_ffn_post[layer])
        new_ffn.append(fbuf)
        if layer == N_A - 1:
            new_lat, new_kpe = mla_shared_kv(x, pos_q, kv_norm_in, kv_w_dkv, kv_norm, kv_w_kr)
            c_kv = jnp.concatenate([past_lat.astype(new_lat.dtype), new_lat], axis=1)
            k_pe = jnp.concatenate([past_kpe.astype(new_kpe.dtype), new_kpe], axis=1)
    return (x, jnp.stack(new_ssm_state), jnp.stack(new_ssm_conv), jnp.stack(new_ffn), new_lat, new_kpe)


def setup_inputs(seed: int = 0) -> dict:
    key = jax.random.key(seed)
    ks = iter(jax.random.split(key, 48))

    def nrm(shape, scale):
        return jax.random.normal(next(ks), shape, F32) * scale

    def gain(shape):
        return 1.0 + nrm(shape, 0.02)

    dt0 = jnp.exp(jax.random.uniform(next(ks), (N_A, SSM_HEADS), F32, math.log(1e-3), math.log(1e-1)))
    return {
        'x_prompt': nrm((BATCH, SEQ, D_MODEL), 1.0),
        'x_sample': nrm((DEC_BATCH, DEC_SEQ, D_MODEL), 1.0),
        'state_ssm': nrm((N_A, DEC_BATCH, SSM_HEADS, SSM_HEADDIM, D_STATE), 0.1),
        'state_ssm_conv': nrm((N_A, DEC_BATCH, SSM_CONV - 1, CONV_DIM), 1.0),
        'state_ffn_conv': nrm((DEPTH, DEC_BATCH, FFN_CONV - 1, 2 * D_FF), 1.0),
        'cache_kv_latent': nrm((DEC_BATCH, PAST_LEN, KV_RANK), 1.0),
        'cache_k_rope': nrm((DEC_BATCH, PAST_LEN, QK_ROPE), 1.0),
        'norm_mix_pre': gain((DEPTH, D_MODEL)),
        'norm_mix_post': gain((DEPTH, D_MODEL)),
        'norm_ffn_pre': gain((DEPTH, D_MODEL)),
        'norm_ffn_post': gain((DEPTH, D_MODEL)),
        'ssm_w_in': nrm((N_A, D_MODEL, IN_PROJ_DIM), D_MODEL ** -0.5),
        'ssm_conv_w': nrm((N_A, SSM_CONV, CONV_DIM), SSM_CONV ** -0.5),
        'ssm_conv_b': nrm((N_A, CONV_DIM), 0.02),
        'ssm_dt_bias': dt0 + jnp.log(-jnp.expm1(-dt0)),
        'ssm_a_log': jnp.log(jax.random.uniform(next(ks), (N_A, SSM_HEADS), F32, 1.0, 16.0)),
        'ssm_d': gain((N_A, SSM_HEADS)),
        'ssm_norm': gain((N_A, D_INNER)),
        'ssm_w_out': nrm((N_A, D_INNER, D_MODEL), D_INNER ** -0.5),
        'kv_norm_in': gain((D_MODEL,)),
        'kv_w_dkv': nrm((D_MODEL, KV_RANK), D_MODEL ** -0.5),
        'kv_norm': gain((KV_RANK,)),
        'kv_w_kr': nrm((D_MODEL, QK_ROPE), D_MODEL ** -0.5),
        'kv_w_uk': nrm((KV_RANK, MLA_HEADS, QK_NOPE), QK_NOPE ** -0.5),
        'kv_w_uv': nrm((KV_RANK, MLA_HEADS, V_DIM), KV_RANK ** -0.5),
        'mla_w_dq': nrm((N_B, D_MODEL, Q_RANK), D_MODEL ** -0.5),
        'mla_q_norm': gain((N_B, Q_RANK)),
        'mla_w_uq': nrm((N_B, Q_RANK, MLA_HEADS * (QK_NOPE + QK_ROPE)), Q_RANK ** -0.5),
        'mla_w_o': nrm((N_B, MLA_HEADS * V_DIM, D_MODEL), (MLA_HEADS * V_DIM) ** -0.5),
        'ffn_w_up': nrm((DEPTH, D_MODEL, 2 * D_FF), D_MODEL ** -0.5),
        'ffn_conv_w': nrm((DEPTH, FFN_CONV, 2 * D_FF), FFN_CONV ** -0.5),
        'ffn_conv_b': nrm((DEPTH, 2 * D_FF), 0.02),
        'ffn_w_down': nrm((DEPTH, D_FF, D_MODEL), D_FF ** -0.5),
    }


def reference(x_prompt, x_sample, state_ssm, state_ssm_conv, state_ffn_conv, cache_kv_latent, cache_k_rope,
              norm_mix_pre, norm_mix_post, norm_ffn_pre, norm_ffn_post,
              ssm_w_in, ssm_conv_w, ssm_conv_b, ssm_dt_bias, ssm_a_log, ssm_d, ssm_norm, ssm_w_out,
              kv_norm_in, kv_w_dkv, kv_norm, kv_w_kr, kv_w_uk, kv_w_uv,
              mla_w_dq, mla_q_norm, mla_w_uq, mla_w_o,
              ffn_w_up, ffn_conv_w, ffn_conv_b, ffn_w_down):
    weights = (norm_mix_pre, norm_mix_post, norm_ffn_pre, norm_ffn_post,
               ssm_w_in, ssm_conv_w, ssm_conv_b, ssm_dt_bias, ssm_a_log, ssm_d, ssm_norm, ssm_w_out,
               kv_norm_in, kv_w_dkv, kv_norm, kv_w_kr, kv_w_uk, kv_w_uv,
               mla_w_dq, mla_q_norm, mla_w_uq, mla_w_o,
               ffn_w_up, ffn_conv_w, ffn_conv_b, ffn_w_down)
    bp = x_prompt.shape[0]
    dtp = x_prompt.dtype
    y_prompt, p_ssm, p_ssm_conv, p_ffn_conv, p_lat, p_kpe = trunk(
        x_prompt,
        jnp.zeros((N_A, bp, SSM_CONV - 1, CONV_DIM), dtp),
        jnp.zeros((N_A, bp, SSM_HEADS, SSM_HEADDIM, D_STATE), dtp),
        jnp.zeros((DEPTH, bp, FFN_CONV - 1, 2 * D_FF), dtp),
        jnp.zeros((bp, 0, KV_RANK), dtp),
        jnp.zeros((bp, 0, QK_ROPE), dtp),
        weights)
    y_sample, s_ssm, s_ssm_conv, s_ffn_conv, s_lat, s_kpe = trunk(
        x_sample, state_ssm_conv, state_ssm, state_ffn_conv, cache_kv_latent, cache_k_rope, weights)
    return (y_prompt, y_sample, p_ssm, p_ssm_conv, p_ffn_conv, p_lat, p_kpe, s_ssm, s_ssm_conv, s_ffn_conv, s_lat, s_kpe)
```

```python
import math
import numpy as np
import concourse.bass as bass
import concourse.mybir as mybir
from concourse.bass_utils import run_bass_kernel_spmd

F32 = mybir.dt.float32
BF16 = mybir.dt.bfloat16
I32 = mybir.dt.int32
ALU = mybir.AluOpType
AF = mybir.ActivationFunctionType
AX = mybir.AxisListType


class V:
    __slots__ = ("buf", "ap")

    def __init__(self, buf, ap):
        self.buf = buf
        self.ap = ap

    def __getitem__(self, key):
        return V(self.buf, self.ap[key])

    def bitcast(self, dt):
        return V(self.buf, self.ap.bitcast(dt))

    def rr(self, pattern, **kw):
        return V(self.buf, self.ap.rearrange(pattern, **kw))

    def bc(self, shape):
        return V(self.buf, self.ap.broadcast_to(list(shape)))

    def unsq(self, axis):
        return V(self.buf, self.ap.unsqueeze(axis))

    def pbc(self, n):
        return V(self.buf, self.ap.partition_broadcast(n))


class Buf:
    def __init__(self, name, t, tracked=True):
        self.name = name
        self.t = t
        self.tracked = tracked
        self.w = {}
        self.r = {}
        self.dsem = None
        self.psum = False

    def __getitem__(self, key):
        return V(self, self.t[key])

    @property
    def v(self):
        return V(self, self.t.ap())


class Eng:
    def __init__(self, name, handle, sem):
        self.name = name
        self.h = handle
        self.sem = sem
        self.count = 0
        self.items = []
        self.waited = {}


class Prog:
    def __init__(self):
        self.nc = bass.Bass("TRN2", target_bir_lowering=False)
        nc = self.nc
        self.sems = {}
        self.E = {}
        for name, h in (("pe", nc.tensor), ("act", nc.scalar), ("dve", nc.vector),
                        ("pool", nc.gpsimd), ("sp", nc.sync)):
            self.E[name] = Eng(name, h, self._sem("e_" + name))
        self.dma_tot = {}
        self.nbuf = 0

    def _sem(self, name):
        s = self.nc.alloc_semaphore(name)
        self.sems[name] = s
        return s

    def sbuf(self, name, shape, dt):
        return Buf(name, self.nc.alloc_sbuf_tensor(name, list(shape), dt))

    def psum(self, name, shape, dt=F32):
        b = Buf(name, self.nc.alloc_psum_tensor(name, list(shape), dt))
        b.psum = True
        return b

    def dram(self, name, shape, dt, kind="Internal"):
        t = self.nc.dram_tensor(name, list(shape), dt, kind=kind)
        return Buf(name, t, tracked=(kind == "Internal"))

    def _need(self, eng, deps):
        for key, val in deps.items():
            if eng.waited.get(key, 0) >= val:
                continue
            if key == "e_" + eng.name:
                assert val <= eng.count, "same-engine wait on a pending (non-inc) op"
            eng.waited[key] = val
            eng.items.append(("wait", key, val))

    def op(self, en, fn, reads=(), writes=(), inc=True):
        eng = self.E[en]
        me = "e_" + en
        deps = {}

        def add(d, skip=None):
            for k, v in d.items():
                if k != skip and v > deps.get(k, 0):
                    deps[k] = v
        wset = set(id(b) for b in writes)
        for b in reads:
            if b.tracked:
                add(b.w)
                if b.psum:
                    add(b.r, me)
        skip = me if en == "pe" else None
        for b in writes:
            if b.tracked:
                add(b.w, skip)
                add(b.r, skip)
        self._need(eng, deps)
        if inc:
            eng.count += 1
            tok = eng.count
        else:
            tok = eng.count + 1
        eng.items.append(("ins", fn, inc))
        for b in reads:
            if b.tracked and id(b) not in wset:
                b.r[me] = tok
        for b in writes:
            if b.tracked:
                b.w = {me: tok}
                b.r = {}
        return tok

    def dma(self, qn, out, in_):
        eng = self.E[qn]
        src, dst = in_.buf, out.buf
        deps = {}

        def add(d):
            for k, v in d.items():
                if v > deps.get(k, 0):
                    deps[k] = v
        if src.tracked:
            add(src.w)
        if dst.tracked:
            add(dst.w)
            add(dst.r)
        self._need(eng, deps)
        owner = dst if dst.tracked else src
        if owner.dsem is None:
            owner.dsem = "d_%d" % self.nbuf
            self.nbuf += 1
            self._sem(owner.dsem)
            self.dma_tot[owner.dsem] = 0
        key = owner.dsem
        self.dma_tot[key] += 16
        val = self.dma_tot[key]
        eng.items.append(("dma", out.ap, in_.ap, key))
        if src.tracked:
            src.r[key] = val
        if dst.tracked:
            dst.w = {key: val}
            dst.r = {}

    def finish(self):
        sp = self.E["sp"]
        deps = dict(self.dma_tot)
        for n, e in self.E.items():
            if n != "sp" and e.count:
                deps["e_" + n] = e.count
        self._need(sp, deps)
        nc = self.nc
        with nc.allow_non_contiguous_dma(reason="small strided parameter/state loads"), nc.Block() as block:
            for n, deco in (("pe", block.tensor), ("act", block.scalar), ("dve", block.vector),
                            ("pool", block.gpsimd), ("sp", block.sync)):
                eng = self.E[n]

                def body(e, eng=eng):
                    for it in eng.items:
                        if it[0] == "wait":
                            e.wait_ge(self.sems[it[1]], it[2])
                        elif it[0] == "ins":
                            ins = it[1](e)
                            if it[2]:
                                ins.then_inc(eng.sem, 1)
                        else:
                            _, oap, iap, key = it
                            e.dma_start(out=oap, in_=iap).then_inc(self.sems[key], 16)
                deco(body)
        return nc

    def mm(self, out, lhsT, rhs, start=True, stop=True, inc=None):
        if inc is None:
            inc = stop
        return self.op("pe", lambda e: e.matmul(out.ap, lhsT=lhsT.ap, rhs=rhs.ap, start=start, stop=stop),
                       reads=[lhsT.buf, rhs.buf], writes=[out.buf], inc=inc)

    def tr(self, out, in_, ident, inc=True):
        return self.op("pe", lambda e: e.transpose(out.ap, in_.ap, ident.ap),
                       reads=[in_.buf, ident.buf], writes=[out.buf], inc=inc)

    def act(self, out, in_, func, bias=None, scale=None, accum=None):
        reads = [in_.buf]
        writes = [out.buf]
        kw = {}
        if bias is not None:
            if isinstance(bias, V):
                reads.append(bias.buf)
                kw["bias"] = bias.ap
            else:
                kw["bias"] = bias
        if scale is not None:
            if isinstance(scale, V):
                reads.append(scale.buf)
                kw["scale"] = scale.ap
            else:
                kw["scale"] = scale
        if accum is not None:
            writes.append(accum.buf)
            kw["accum_out"] = accum.ap
        return self.op("act", lambda e: e.activation(out.ap, in_.ap, func, **kw), reads=reads, writes=writes)

    def tt(self, en, out, a, b, op):
        return self.op(en, lambda e: e.tensor_tensor(out.ap, a.ap, b.ap, op),
                       reads=[a.buf, b.buf], writes=[out.buf])

    def ts(self, en, out, a, s1, op0, s2=None, op1=None):
        reads = [a.buf]
        s1a = s1.ap if isinstance(s1, V) else s1
        s2a = s2.ap if isinstance(s2, V) else s2
        if isinstance(s1, V):
            reads.append(s1.buf)
        if isinstance(s2, V):
            reads.append(s2.buf)
        kw = {}
        if op1 is not None:
            kw["op1"] = op1
        return self.op(en, lambda e: e.tensor_scalar(out.ap, a.ap, s1a, s2a, op0, **kw),
                       reads=reads, writes=[out.buf])

    def stt(self, out, a, s, b, op0, op1):
        reads = [a.buf, b.buf]
        sa = s.ap if isinstance(s, V) else s
        if isinstance(s, V):
            reads.append(s.buf)
        return self.op("dve", lambda e: e.scalar_tensor_tensor(out.ap, a.ap, sa, b.ap, op0, op1),
                       reads=reads, writes=[out.buf])

    def copy(self, en, out, in_):
        if en == "act":
            return self.op(en, lambda e: e.copy(out.ap, in_.ap), reads=[in_.buf], writes=[out.buf])
        return self.op(en, lambda e: e.tensor_copy(out.ap, in_.ap), reads=[in_.buf], writes=[out.buf])

    def memset(self, en, out, val):
        return self.op(en, lambda e: e.memset(out.ap, val), reads=[], writes=[out.buf])

    def recip(self, out, in_):
        return self.op("dve", lambda e: e.reciprocal(out.ap, in_.ap), reads=[in_.buf], writes=[out.buf])

    def reduce(self, out, in_, op):
        return self.op("dve", lambda e: e.tensor_reduce(out.ap, in_.ap, AX.X, op), reads=[in_.buf], writes=[out.buf])


class Rot:
    def __init__(self, bufs):
        self.bufs = bufs
        self.i = 0

    def get(self):
        b = self.bufs[self.i % len(self.bufs)]
        self.i += 1
        return b


D = 1024
KC = 8
SEQ = 2048
DEC_SEQ = 64
PAST = 2048
DI = 2048
NH = 32
HD = 64
NG = 4
DS = 128
CONVD = 3072
NXC = 24
INP = 5152
DFF = 2816
NFC = 44
MH = 16
QR = 384
KVR = 256
NOPE = 128
ROPE = 64
VD = 128
EPS = 1e-6
THETA = 10000.0
T = 256
ST = T // 128
NCH = T // 64
NPS = 2
NSS = 4
NSEQ = NPS + NSS
NEGBIG = -30000.0
SCALE = (NOPE + ROPE) ** -0.5
NSLOT = 3
SLOT_E = 4096


class _Stop(Exception):
    pass


def build_program(stop_after=None):
    P = Prog()
    nc = P.nc
    stop_tag = (stop_after or {}).get("tag")

    def chk(tag):
        if tag == stop_tag:
            raise _Stop()

    def din(name, shape):
        return P.dram(name, shape, F32, kind="ExternalInput")

    def dout(name, shape):
        return P.dram(name, shape, F32, kind="ExternalOutput")

    xp = din("xp", [NPS, SEQ, D])
    xs = din("xs", [NSS * DEC_SEQ, D])
    st_ssm = din("st_ssm", [NSS, DI, DS])
    st_sconv = din("st_sconv", [NSS, 3, CONVD])
    st_fconv = din("st_fconv", [2, NSS, 2, 2 * DFF])
    c_lat = din("c_lat", [NSS, PAST, KVR])
    c_kr = din("c_kr", [NSS, PAST, ROPE])
    g_mix_pre = din("g_mix_pre", [2, D])
    g_mix_post = din("g_mix_post", [2, D])
    g_ffn_pre = din("g_ffn_pre", [2, D])
    g_ffn_post = din("g_ffn_post", [2, D])
    w_in = din("w_in", [D, INP])
    cw_ssm = din("cw_ssm", [4, CONVD])
    cb_ssm = din("cb_ssm", [CONVD])
    dt_bias = din("dt_bias", [NH])
    a_log = din("a_log", [NH])
    d_skip = din("d_skip", [NH])
    g_ssm = din("g_ssm", [DI])
    w_out = din("w_out", [DI, D])
    g_kvin = din("g_kvin", [D])
    w_dkv = din("w_dkv", [D, KVR])
    g_kv = din("g_kv", [KVR])
    w_kr = din("w_kr", [D, ROPE])
    w_uk = din("w_uk", [KVR, MH * NOPE])
    w_uv = din("w_uv", [KVR, MH * VD])
    w_dq = din("w_dq", [D, QR])
    g_q = din("g_q", [QR])
    w_uq = din("w_uq", [QR, MH * (NOPE + ROPE)])
    w_o = din("w_o", [MH * VD, D])
    w_up = din("w_up", [2, D, 2 * DFF])
    cw_ffn = din("cw_ffn", [2, 3, 2 * DFF])
    cb_ffn = din("cb_ffn", [2, 2 * DFF])
    w_dn = din("w_dn", [2, DFF, D])
    k_ident = din("k_ident", [128, 128])
    k_u2 = din("k_u2", [128, 64])
    k_delta = din("k_delta", [128, 64])
    k_pos = din("k_pos", [128, 17])
    k_j = din("k_j", [128, 32])
    k_mrow = din("k_mrow", [2, 128])

    y_p = dout("y_p", [NPS, SEQ, D])
    y_s = dout("y_s", [NSS * DEC_SEQ, D])
    o_ssm = dout("o_ssm", [NSEQ, DI, DS])
    o_sconv = dout("o_sconv", [NSEQ, 3, CONVD])
    o_fconv = dout("o_fconv", [2, NSEQ, 2, 2 * DFF])
    o_lat_p = dout("o_lat_p", [NPS, SEQ, KVR])
    o_kr_p = dout("o_kr_p", [NPS, SEQ, ROPE])
    o_lat_s = dout("o_lat_s", [NSS * DEC_SEQ, KVR])
    o_kr_s = dout("o_kr_s", [NSS * DEC_SEQ, ROPE])

    class Chunked:
        def __init__(self, bufs):
            self.bufs = bufs

        def __getitem__(self, key):
            p, c, t = key
            return self.bufs[c][p, t]

    xres = Chunked([P.sbuf("xres%d" % i, [128, T], F32) for i in range(KC)])
    mixb = Chunked([P.sbuf("mixb%d" % i, [128, T], F32) for i in range(KC)])
    hn = Chunked([P.sbuf("hn%d" % i, [128, T], BF16) for i in range(KC)])
    bufA = Chunked([P.sbuf("bufA%d" % i, [128, T], BF16) for i in range(NXC)])
    G = [P.sbuf("G%d" % i, [128, 2048], F32) for i in range(8)]
    Hs = P.sbuf("Hs", [128, DI], F32)
    Hb = P.sbuf("Hb", [128, DI], BF16)
    kv_fm = P.sbuf("kv_fm", [128, 2, PAST + 64], BF16)
    kpe_fm = P.sbuf("kpe_fm", [128, PAST + 64], BF16)
    kv_tm = P.sbuf("kv_tm", [128, 17, KVR], BF16)
    ring = [P.sbuf("ring%d" % i, [128, SLOT_E], BF16) for i in range(NSLOT)]
    extp = Rot([P.sbuf("ext%d" % i, [128, T + 16], F32) for i in range(2)])
    caccp = Rot([P.sbuf("cacc%d" % i, [128, T], F32) for i in range(2)])
    sqbp = Rot([P.sbuf("sqb%d" % i, [128, T], BF16) for i in range(2)])
    rstdp = Rot([P.sbuf("rstd%d" % i, [128, T], F32) for i in range(2)])
    tmpp = Rot([P.sbuf("tmp%d" % i, [128, 512], F32) for i in range(6)])
    smallp = Rot([P.sbuf("small%d" % i, [128, 64], F32) for i in range(12)])
    ident_f = P.sbuf("ident_f", [128, 128], F32)
    ident_b = P.sbuf("ident_b", [128, 128], BF16)
    ones_b = P.sbuf("ones_b", [128, 128], BF16)
    ones_f = P.sbuf("ones_f", [128, 128], F32)
    u2 = P.sbuf("u2", [128, 64], F32)
    negu2 = P.sbuf("negu2", [128, 64], F32)
    delta = P.sbuf("delta", [128, 64], F32)
    mrow = P.sbuf("mrow", [1, 2, 128], BF16)
    mrow_f = P.sbuf("mrow_f", [1, 2, 128], F32)
    diagD = P.sbuf("diagD", [128, 16, 128], BF16)
    dcol = P.sbuf("dcol", [128, 16], F32)
    cos_tm = P.sbuf("cos_tm", [128, 17, 32], F32)
    sin_tm = P.sbuf("sin_tm", [128, 17, 32], F32)
    gcols = P.sbuf("gcols", [128, 9, KC], F32)
    gq_col = P.sbuf("gq_col", [128, 3], F32)
    gssm_col = P.sbuf("gssm_col", [128, 16], F32)
    gkv_bc = P.sbuf("gkv_bc", [128, KVR], F32)
    dtb_bc = P.sbuf("dtb_bc", [128, NH], F32)
    a_bc = P.sbuf("a_bc", [128, NH], F32)
    cws = P.sbuf("cws", [128, NXC, 4], F32)
    cbs = P.sbuf("cbs", [128, NXC], F32)
    cwf = P.sbuf("cwf", [128, 2, NFC, 3], F32)
    cbf = P.sbuf("cbf", [128, 2, NFC], F32)
    tail_s = P.sbuf("tail_s", [128, NXC, 4, 3], F32)
    tail_f = P.sbuf("tail_f", [128, 2, NFC, 4, 2], F32)
    dt_tm = P.sbuf("dt_tm", [128, ST, NH], F32)
    PS = Rot([P.psum("ps%d" % i, [128, 512], F32) for i in range(8)])

    P.dma("sp", ident_f.v, k_ident.v)
    P.copy("dve", ident_b.v, ident_f.v)
    P.memset("pool", ones_b.v, 1.0)
    P.memset("pool", ones_f.v, 1.0)
    onesH = P.sbuf("onesH", [128, 2, 128], F32)
    P.memset("pool", onesH.v, 0.0)
    P.memset("pool", onesH[0:64, 0, :], 1.0)
    P.memset("pool", onesH[64:128, 1, :], 1.0)
    P.dma("sp", u2.v, k_u2.v)
    P.ts("dve", negu2.v, u2.v, -1.0, ALU.mult)
    P.dma("sp", delta.v, k_delta.v)
    P.dma("sp", mrow_f.v, k_mrow.v.unsq(0))
    P.copy("dve", mrow.v, mrow_f.v)
    for i, gsrc in enumerate((g_mix_pre, g_mix_post, g_ffn_pre, g_ffn_post)):
        for l in range(2):
            P.dma("sp", gcols[:, 2 * i + l, :], gsrc[l].rr("(c p) -> p c", p=128))
    P.dma("sp", gcols[:, 8, :], g_kvin.v.rr("(c p) -> p c", p=128))
    P.dma("sp", gq_col.v, g_q.v.rr("(c p) -> p c", p=128))
    P.dma("sp", gssm_col.v, g_ssm.v.rr("(c p) -> p c", p=128))
    P.dma("sp", gkv_bc.v, g_kv.v.pbc(128))
    P.dma("sp", dtb_bc.v, dt_bias.v.pbc(128))
    P.dma("sp", a_bc.v, a_log.v.pbc(128))
    P.act(a_bc.v, a_bc.v, AF.Exp)
    P.ts("dve", a_bc.v, a_bc.v, -1.0, ALU.mult)
    for i in range(4):
        P.dma("sp", cws[:, :, i], cw_ssm[i].rr("(c p) -> p c", p=128))
    P.dma("sp", cbs.v, cb_ssm.v.rr("(c p) -> p c", p=128))
    for l in range(2):
        for i in range(3):
            P.dma("sp", cwf[:, l, :, i], cw_ffn[l, i].rr("(c p) -> p c", p=128))
        P.dma("sp", cbf[:, l], cb_ffn[l].rr("(c p) -> p c", p=128))
    for hf in range(2):
        P.dma("sp", dcol[hf * 64:(hf + 1) * 64, :], d_skip.v[hf::2].pbc(64))
    for j in range(16):
        P.ts("dve", diagD[:, j, :], ident_f.v, dcol[:, j:j + 1], ALU.mult)
    posb = P.sbuf("posb", [128, 17], F32)
    invb = P.sbuf("invb", [128, 32], F32)
    angb = V(G[2], G[2].t[:, 0:544].rearrange("p (b j) -> p b j", b=17))
    angi = V(G[3], G[3].t[:, 0:544].bitcast(I32).rearrange("p (b j) -> p b j", b=17))
    angk = V(G[4], G[4].t[:, 0:544].rearrange("p (b j) -> p b j", b=17))
    P.dma("sp", posb.v, k_pos.v)
    P.dma("sp", invb.v, k_j.v)
    P.act(invb.v, invb.v, AF.Exp, scale=-math.log(THETA) / 32.0)
    for b in range(17):
        P.ts("dve", angb[:, b, :], invb.v, posb[:, b:b + 1], ALU.mult)
    for tab, shift in ((sin_tm, 0.0), (cos_tm, math.pi / 2)):
        P.ts("dve", angk, angb, shift, ALU.add, 1.0 / (2 * math.pi), ALU.mult)
        P.copy("dve", angi, angk)
        P.copy("dve", angk, angi)
        P.stt(angk, angk, -2 * math.pi, angb, ALU.mult, ALU.add)
        if shift != 0.0:
            P.ts("dve", angk, angk, shift, ALU.add)
        P.ts("dve", angk, angk, -3.1415925, ALU.max, 3.1415925, ALU.min)
        P.act(tab.v, angk, AF.Sin)

    wsc = {}
    wshape = {}

    def wdef(name, nblk, kc, m):
        wsc[name] = P.dram("wsc_" + name, [nblk, 128, kc * m], BF16)
        wshape[name] = (nblk, kc, m)

    wdef("wz", 4, 8, 512)
    wdef("wx", 6, 8, 512)
    wdef("wdt", 1, 8, 32)
    wdef("wout", 4, 16, 256)
    for l in range(2):
        wdef("wup%d" % l, 11, 8, 512)
        wdef("wdn%d" % l, 8, 22, 128)
    wdef("wkv", 1, 8, 320)
    wdef("wdq", 1, 8, 384)
    wdef("wuqn", 4, 3, 512)
    wdef("wuqr", 1, 3, 1024)
    wdef("wo", 4, 16, 256)
    wdef("wuv", 1, 2, 2048)

    stgp = Rot([G[0], G[1], G[2], G[3], G[4], G[5]])
    cast_eng = Rot(["dve", "pool", "act"])
    ring_i = [0]

    def kview(w, c0, c1):
        return w.rr("(c p) m -> p c m", p=128)[:, :, c0:c1]

    def prep(name, j, parts, gain=None):
        nblk, kc, m = wshape[name]
        slot = ring[ring_i[0] % NSLOT]
        ring_i[0] += 1
        bv = slot[:, 0:kc * m].rr("p (c m) -> p c m", c=kc)
        cper = max(1, 2048 // m)
        for c0 in range(0, kc, cper):
            c1 = min(kc, c0 + cper)
            stg = stgp.get()
            sv = stg[:, 0:(c1 - c0) * m].rr("p (c m) -> p c m", c=c1 - c0)
            for src, off, mi in parts:
                P.dma("sp", sv[:, :, off:off + mi], src[:, c0:c1, :])
            if gain is None:
                P.copy(cast_eng.get(), bv[:, c0:c1, :], sv)
            else:
                for c in range(c0, c1):
                    P.ts("dve" if c % 2 else "pool", bv[:, c, :], sv[:, c - c0, :], gain[:, c:c + 1], ALU.mult)
        P.dma("sp", wsc[name][j], slot[:, 0:kc * m])

    w_in_v = w_in.v
    _skip_prologue = (stop_after or {}).get("phase", 99) < 1
    for j in range(4):
        prep("wz", j, [(kview(w_in_v, j * 512, (j + 1) * 512), 0, 512)])
    prep("wdt", 0, [(kview(w_in_v, 5120, 5152), 0, 32)])
    for j in range(6):
        prep("wx", j, [(kview(w_in_v, 2048 + j * 512, 2048 + (j + 1) * 512), 0, 512)])
    for j in range(4):
        prep("wout", j, [(kview(w_out.v, j * 256, (j + 1) * 256), 0, 256)], gain=gssm_col.v)
    prep("wkv", 0, [(kview(w_dkv.v, 0, 256), 0, 256), (kview(w_kr.v, 0, 64), 256, 64)])
    for l in range(2):
        for j in range(11):
            prep("wup%d" % l, j, [(kview(w_up[l], j * 256, (j + 1) * 256), 0, 256),
                                  (kview(w_up[l], DFF + j * 256, DFF + (j + 1) * 256), 256, 256)])
        for j in range(8):
            prep("wdn%d" % l, j, [(kview(w_dn[l], j * 128, (j + 1) * 128), 0, 128)])
    prep("wdq", 0, [(kview(w_dq.v, 0, 384), 0, 384)])
    uq4 = w_uq.v.rr("(c p) (h e) -> p c h e", p=128, e=192)
    for j in range(4):
        stg = stgp.get()
        sv = stg[:, 0:1536].rr("p (c h e) -> p c h e", c=3, h=4)
        for c in range(3):
            P.dma("sp", sv[:, c], uq4[:, c, 4 * j:4 * j + 4, 0:128])
        slot = ring[ring_i[0] % NSLOT]
        ring_i[0] += 1
        P.copy(cast_eng.get(), slot[:, 0:1536], stg[:, 0:1536])
        P.dma("sp", wsc["wuqn"][j], slot[:, 0:1536])
    slot = ring[ring_i[0] % NSLOT]
    ring_i[0] += 1
    for c in range(3):
        stg = stgp.get()
        sv = stg[:, 0:1024].rr("p (h e) -> p h e", h=16)
        P.dma("sp", sv, uq4[:, c, :, 128:192])
        P.copy(cast_eng.get(), slot[:, c * 1024:(c + 1) * 1024], stg[:, 0:1024])
    P.dma("sp", wsc["wuqr"][0], slot[:, 0:3072])
    for j in range(4):
        prep("wo", j, [(kview(w_o.v, j * 256, (j + 1) * 256), 0, 256)])
    stg = stgp.get()
    stg2 = stgp.get()
    P.dma("sp", stg.v, w_uv.v[0:128, :])
    P.dma("sp", stg2.v, w_uv.v[128:256, :])
    slot = ring[ring_i[0] % NSLOT]
    ring_i[0] += 1
    P.copy("dve", slot[:, 0:2048], stg.v)
    P.copy("pool", slot[:, 2048:4096], stg2.v)
    P.dma("sp", wsc["wuv"][0], slot.v)
    stg = stgp.get()
    stg2 = stgp.get()
    P.dma("sp", stg.v, w_uk.v[0:128, :])
    P.dma("sp", stg2.v, w_uk.v[128:256, :])
    wukT_sb = P.sbuf("wukT_sb", [128, 16, 256], BF16)
    sl4 = wukT_sb.v
    for rc, sg in enumerate((stg, stg2)):
        for h4 in range(4):
            ps = PS.get()
            for hh in range(4):
                h = h4 * 4 + hh
                P.tr(ps[:, hh * 128:(hh + 1) * 128], sg[:, h * 128:(h + 1) * 128], ident_f.v, inc=(hh == 3))
            P.copy("act" if h4 % 2 else "dve", sl4[:, h4 * 4:h4 * 4 + 4, rc * 128:(rc + 1) * 128],
                   ps.v.rr("p (h r) -> p h r", h=4))

    class WStream:
        def __init__(self):
            self.order = []
            self.pos = 0
            self.issued = 0
            self.slot_of = {}

        def plan(self, lst):
            self.order = lst

        def _issue(self):
            name, j = self.order[self.issued]
            nblk, kc, m = wshape[name]
            slot = ring[(ring_i[0] + self.issued) % NSLOT]
            P.dma("sp", slot[:, 0:kc * m], wsc[name][j])
            self.slot_of[self.issued] = slot
            self.issued += 1

        def get(self, name, j):
            assert self.order[self.pos] == (name, j), (self.order[self.pos], name, j)
            while self.issued < min(len(self.order), self.pos + NSLOT):
                self._issue()
            slot = self.slot_of.pop(self.pos)
            self.pos += 1
            nblk, kc, m = wshape[name]
            return slot[:, 0:kc * m].rr("p (c m) -> p c m", c=kc)

    WS = WStream()

    def pass_blocks(kind):
        lst = [("wz", j) for j in range(4)] + [("wdt", 0)] + [("wx", j) for j in range(6)]
        lst += [("wout", j) for j in range(4)]
        lst += [("wup0", j) for j in range(11)] + [("wdn0", j) for j in range(8)]
        lst += [("wkv", 0), ("wdq", 0), ("wuqr", 0)]
        lst += [("wuqn", j) for j in range(4)] * (NSS if kind == "s" else 1)
        lst += [("wo", j) for j in range(4)]
        lst += [("wup1", j) for j in range(11)] + [("wdn1", j) for j in range(8)]
        return lst

    tiles = []
    for s in range(NPS):
        for t0 in range(0, SEQ, T):
            tiles.append(dict(kind="p", segs=[(s, t0, T)]))
    tiles.append(dict(kind="s", segs=[(NPS + i, 0, DEC_SEQ) for i in range(NSS)]))
    if stop_after is not None and stop_after.get("ntiles"):
        tiles = tiles[:stop_after["ntiles"]]
    if stop_after is not None and stop_after.get("tiles"):
        tiles = [tiles[i] for i in stop_after["tiles"]]
    order = []
    for tl in tiles:
        order += pass_blocks(tl["kind"])
    WS.plan(order)

    def norm_to_hn(src, gidx, nk=KC, dim=D, out=None, gcol=None):
        out = hn if out is None else out
        ps = PS.get()
        for kc in range(nk):
            sq = sqbp.get()
            P.act(sq.v, src[:, kc, :], AF.Square)
            P.mm(ps[:, 0:T], ones_b.v, sq.v, start=(kc == 0), stop=(kc == nk - 1), inc=True)
        rs = rstdp.get()
        P.act(rs.v, ps[:, 0:T], AF.Sqrt, bias=EPS, scale=1.0 / dim)
        P.recip(rs.v, rs.v)
        for kc in range(nk):
            g = gcols[:, gidx, kc:kc + 1] if gcol is None else gcol[:, kc:kc + 1]
            P.stt(out[:, kc, :], src[:, kc, :], g, rs.v, ALU.mult, ALU.mult)

    def postnorm_add(gidx):
        ps = PS.get()
        for kc in range(KC):
            sq = sqbp.get()
            P.act(sq.v, mixb[:, kc, :], AF.Square)
            P.mm(ps[:, 0:T], ones_b.v, sq.v, start=(kc == 0), stop=(kc == KC - 1), inc=True)
        rs = rstdp.get()
        P.act(rs.v, ps[:, 0:T], AF.Sqrt, bias=EPS, scale=1.0 / D)
        P.recip(rs.v, rs.v)
        for kc in range(KC):
            P.tt("pool" if kc % 2 else "dve", mixb[:, kc, :], mixb[:, kc, :], rs.v, ALU.mult)
            P.stt(xres[:, kc, :], mixb[:, kc, :], gcols[:, gidx, kc:kc + 1], xres[:, kc, :], ALU.mult, ALU.add)

    def conv_chunk(ps, K, wv, bv, tail, segs, acc=None):
        nseg = len(segs)
        L = segs[0][2]
        ext = extp.get()
        ev = ext[:, 0:nseg * (L + K - 1)].rr("p (s l) -> p s l", s=nseg)
        P.copy("pool", ev[:, :, 0:K - 1], tail[:, 0:nseg, :])
        P.copy("act", ev[:, :, K - 1:], ps[:, 0:T].rr("p (s l) -> p s l", s=nseg))
        if acc is None:
            acc = caccp.get()
        av = acc[:, 0:T].rr("p (s l) -> p s l", s=nseg)
        P.act(av, ev[:, :, 0:L], AF.Identity, bias=bv, scale=wv[:, 0:1])
        for i in range(1, K):
            P.stt(av, ev[:, :, i:i + L], wv[:, i:i + 1], av, ALU.mult, ALU.add)
        P.copy("pool", tail[:, 0:nseg, :], ev[:, :, L:L + K - 1])
        return acc

    def tail_init(tile):
        segs = tile["segs"]
        if tile["kind"] == "p":
            if segs[0][1] == 0:
                P.memset("pool", tail_s.v, 0.0)
                P.memset("pool", tail_f.v, 0.0)
        else:
            for i, (seq, t0, L) in enumerate(segs):
                for r in range(3):
                    P.dma("sp", tail_s[:, :, i, r], st_sconv[seq - NPS, r].rr("(c p) -> p c", p=128))
                for l in range(2):
                    for r in range(2):
                        P.dma("sp", tail_f[:, l, :, i, r], st_fconv[l, seq - NPS, r].rr("(c p) -> p c", p=128))

    def tail_out(tile):
        segs = tile["segs"]
        for i, (seq, t0, L) in enumerate(segs):
            if tile["kind"] == "p" and t0 + L < SEQ:
                continue
            for r in range(3):
                P.dma("sp", o_sconv[seq, r].rr("(c p) -> p c", p=128), tail_s[:, :, i, r])
            for l in range(2):
                for r in range(2):
                    P.dma("sp", o_fconv[l, seq, r].rr("(c p) -> p c", p=128), tail_f[:, l, :, i, r])

    def load_x(tile):
        stage = G[0].v[:, 0:ST * D].rr("p (s d) -> p s d", s=ST)
        if tile["kind"] == "p":
            seq, t0, L = tile["segs"][0]
            P.dma("sp", stage, xp[seq, t0:t0 + T, :].rr("(s p) d -> p s d", p=128))
        else:
            P.dma("sp", stage, xs.v.rr("(s p) d -> p s d", p=128))
        for kc in range(KC):
            ps = PS.get()
            for st in range(ST):
                P.tr(ps[:, st * 128:(st + 1) * 128], stage[:, st, kc * 128:(kc + 1) * 128], ident_f.v,
                     inc=(st == ST - 1))
            P.copy("act" if kc % 2 else "dve", xres[:, kc, :], ps[:, 0:T])

    def store_y(tile):
        stage = G[1].v[:, 0:ST * D].rr("p (s d) -> p s d", s=ST)
        for st in range(ST):
            for k4 in range(2):
                ps = PS.get()
                for kk in range(4):
                    kc = k4 * 4 + kk
                    P.tr(ps[:, kk * 128:(kk + 1) * 128], xres[:, kc, st * 128:(st + 1) * 128], ident_f.v,
                         inc=(kk == 3))
                P.copy("act" if k4 % 2 else "dve", stage[:, st, k4 * 512:(k4 + 1) * 512], ps.v)
        if tile["kind"] == "p":
            seq, t0, L = tile["segs"][0]
            P.dma("sp", y_p[seq, t0:t0 + T, :].rr("(s p) d -> p s d", p=128), stage)
        else:
            P.dma("sp", y_s.v.rr("(s p) d -> p s d", p=128), stage)

    zs_tm = G[0].v.bitcast(BF16).rr("p (s c) -> p s c", s=ST)
    yg_fm = G[1].v.bitcast(BF16).rr("p (c t) -> p c t", c=16)

    def state_load(seq):
        if seq < NPS:
            P.memset("pool", Hs.v, 0.0)
            P.memset("pool", Hb.v, 0.0)
            return
        stg = G[2]
        sv = stg.v.rr("p (j n) -> p j n", j=16)
        P.dma("sp", sv, st_ssm[seq - NPS].rr("(j p) n -> p j n", p=128))
        for j4 in range(4):
            ps = PS.get()
            for jj in range(4):
                j = j4 * 4 + jj
                P.tr(ps[:, jj * 128:(jj + 1) * 128], sv[:, j, :], ident_f.v, inc=(jj == 3))
            P.copy("act" if j4 % 2 else "dve", Hs[:, j4 * 512:(j4 + 1) * 512], ps.v)
        P.copy("pool", Hb.v, Hs.v)

    def state_store(seq):
        stg = G[2]
        sv = stg.v.rr("p (j n) -> p j n", j=16)
        for j4 in range(4):
            ps = PS.get()
            for jj in range(4):
                j = j4 * 4 + jj
                P.tr(ps[:, jj * 128:(jj + 1) * 128], Hs[:, j * 128:(j + 1) * 128], ident_f.v, inc=(jj == 3))
            P.copy("act" if j4 % 2 else "dve", sv[:, j4 * 4:j4 * 4 + 4, :], ps.v.rr("p (j n) -> p j n", j=4))
        P.dma("sp", o_ssm[seq].rr("(j p) n -> p j n", p=128), sv)

    def mamba(tile):
        segs = tile["segs"]
        norm_to_hn(xres, 0)
        for zb in range(4):
            wv = WS.get("wz", zb)
            for st in range(ST):
                ps = PS.get()
                for kc in range(KC):
                    P.mm(ps.v, hn[:, kc, st * 128:(st + 1) * 128], wv[:, kc, :], start=(kc == 0), stop=(kc == KC - 1))
                P.act(zs_tm[:, st, zb * 512:(zb + 1) * 512], ps.v, AF.Silu)
        chk("m_z")
        wv = WS.get("wdt", 0)
        for st in range(ST):
            ps = PS.get()
            for kc in range(KC):
                P.mm(ps[:, 0:NH], hn[:, kc, st * 128:(st + 1) * 128], wv[:, kc, :], start=(kc == 0), stop=(kc == KC - 1))
            xr = smallp.get()[:, 0:NH]
            ab = smallp.get()[:, 0:NH]
            P.tt("dve", xr, ps[:, 0:NH], dtb_bc.v, ALU.add)
            P.act(ab, xr, AF.Abs)
            P.act(ab, ab, AF.Exp, scale=-1.0)
            P.act(ab, ab, AF.Ln, bias=1.0)
            P.stt(dt_tm[:, st, :], xr, 0.0, ab, ALU.max, ALU.add)
        chk("m_dt")
        for cb in range(6):
            wv = WS.get("wx", cb)
            for mc in range(4):
                ch = cb * 4 + mc
                ps = PS.get()
                for kc in range(KC):
                    P.mm(ps[:, 0:T], wv[:, kc, mc * 128:(mc + 1) * 128], hn[:, kc, :], start=(kc == 0), stop=(kc == KC - 1))
                acc = conv_chunk(ps, 4, cws[:, ch, :], cbs[:, ch:ch + 1], tail_s[:, ch], segs, None)
                P.act(bufA[:, ch, :], acc.v, AF.Silu)
        chk("m_x")
        for st in range(ST):
            tk = slice(st * 128, (st + 1) * 128)
            x_tm = G[7].v.bitcast(BF16)[:, 0:2048]
            B_tm = G[7].v.bitcast(BF16)[:, 2048:2560]
            xdt = G[6].v.bitcast(BF16)[:, 0:2048]
            xw = G[6].v.bitcast(BF16)[:, 2048:4096]
            dec = G[5].v.bitcast(BF16)[:, 0:2048]
            wts = G[5].v.bitcast(BF16)[:, 2048:4096]
            Rm = G[2].v
            DAB = G[3].v
            y_sb = G[4].v
            for j8 in range(2):
                ps = PS.get()
                pv = ps.v.bitcast(BF16)
                for jj in range(8):
                    j = j8 * 8 + jj
                    P.tr(pv[:, jj * 128:(jj + 1) * 128], bufA[:, j, tk], ident_b.v, inc=(jj == 7))
                P.copy("act" if j8 else "dve", x_tm[:, j8 * 1024:(j8 + 1) * 1024], pv)
            ps = PS.get()
            pv = ps.v.bitcast(BF16)
            for g in range(NG):
                P.tr(pv[:, g * 128:(g + 1) * 128], bufA[:, 16 + g, tk], ident_b.v, inc=(g == NG - 1))
            P.copy("act", B_tm, pv[:, 0:512])
            chk("m_tr")
            da = smallp.get()[:, 0:NH]
            P.tt("dve", da, dt_tm[:, st, :], a_bc.v, ALU.mult)
            P.tt("pool", xdt.rr("p (h e) -> p h e", h=NH), x_tm.rr("p (h e) -> p h e", h=NH),
                 dt_tm[:, st, :].unsq(2).bc([128, NH, HD]), ALU.mult)
            dab3 = da.unsq(2).bc([128, NH, 64])
            P.tt("dve", Rm.rr("p (h l) -> p h l", h=NH), dab3, u2.v.unsq(1).bc([128, NH, 64]), ALU.mult)
            P.tt("pool", DAB.rr("p (h l) -> p h l", h=NH), dab3, delta.v.unsq(1).bc([128, NH, 64]), ALU.add)
            chk("m_rd")
            for g in range(NG):
                ps = PS.get()
                for hf in range(2):
                    rows = slice(hf * 64, (hf + 1) * 64)
                    P.mm(ps[rows, :], ones_f[rows, 0:64], Rm[rows, g * 512:(g + 1) * 512], start=True, stop=False)
                    P.mm(ps[rows, :], negu2[rows, :], DAB[rows, g * 512:(g + 1) * 512], start=False, stop=True,
                         inc=(hf == 1))
                P.act(dec[:, g * 512:(g + 1) * 512], ps.v, AF.Exp)
            P.tt("pool", xw.rr("p (h e) -> p h e", h=NH), xdt.rr("p (h e) -> p h e", h=NH),
                 dec.rr("p (h l) -> p h l", h=NH)[:, :, 63:64].bc([128, NH, HD]), ALU.mult)
            chk("m_seg")
            ps = PS.get()
            for hf in range(2):
                rows = slice(hf * 64, (hf + 1) * 64)
                P.mm(ps[rows, 0:NH], u2[rows, :], da[rows, :], inc=False)
            for hf in range(2):
                rows = slice(hf * 64, (hf + 1) * 64)
                P.mm(ps[:, 64 + hf * NH:64 + (hf + 1) * NH], onesH[:, hf, :], da, inc=(hf == 1))
            eac = smallp.get()
            P.act(eac[:, 0:NH], ps[:, 0:NH], AF.Exp)
            cd = smallp.get()
            P.act(cd.v, ps[:, 64:128], AF.Exp)
            chk("m_eac")
            psc = PS.get()
            for hf in range(2):
                rows = slice(hf * 64, (hf + 1) * 64)
                tok = slice(st * 128 + hf * 64, st * 128 + (hf + 1) * 64)
                for g in range(NG):
                    P.mm(psc[rows, g * 64:(g + 1) * 64], bufA[:, 16 + g, tok], bufA[:, 20 + g, tok],
                         inc=(hf == 1 and g == NG - 1))
            P.tt("dve", wts.rr("p (g e l) -> p g e l", g=NG, e=8), dec.rr("p (g e l) -> p g e l", g=NG, e=8),
                 psc[:, 0:256].rr("p (g l) -> p g l", g=NG).unsq(2).bc([128, NG, 8, 64]), ALU.mult)
            chk("m_cb")
            for hf in range(2):
                c = st * 2 + hf
                rows = slice(hf * 64, (hf + 1) * 64)
                tok = slice(c * 64, (c + 1) * 64)
                if tile["kind"] == "p":
                    seq, t0, L = segs[0]
                    first = (t0 == 0 and c == 0)
                    last = (t0 + T == SEQ and c == NCH - 1)
                else:
                    seq = segs[c][0]
                    first = last = True
                if first:
                    state_load(seq)
                for g in range(NG):
                    pso = PS.get()
                    P.mm(pso[rows, :], bufA[:, 20 + g, tok], Hb[:, g * 512:(g + 1) * 512])
                    psd = PS.get()
                    for e in range(8):
                        h = g * 8 + e
                        P.mm(psd[rows, e * 64:(e + 1) * 64], wts[rows, h * 64:(h + 1) * 64], xdt[rows, h * 64:(h + 1) * 64],
                             start=(e == 0), stop=False, inc=False)
                    for jj in range(4):
                        j = g * 4 + jj
                        P.mm(psd[rows, jj * 128:(jj + 1) * 128], bufA[:, j, tok], diagD[:, j, :], start=False,
                             stop=(jj == 3), inc=(jj == 3))
                    tmp = tmpp.get()
                    P.tt("dve", tmp[rows, :].rr("p (e d) -> p e d", e=8), pso[rows, :].rr("p (e d) -> p e d", e=8),
                         eac[rows, g * 8:(g + 1) * 8].unsq(2).bc([64, 8, HD]), ALU.mult)
                    P.tt("dve", y_sb[rows, g * 512:(g + 1) * 512], psd[rows, :], tmp[rows, :], ALU.add)
                hh2 = NH // 2
                P.tt("pool", Hs[:, 0:1024].rr("p (h e) -> p h e", h=hh2), Hs[:, 0:1024].rr("p (h e) -> p h e", h=hh2),
                     cd[:, hf * NH:hf * NH + hh2].unsq(2).bc([128, hh2, HD]), ALU.mult)
                P.tt("dve", Hs[:, 1024:2048].rr("p (h e) -> p h e", h=hh2), Hs[:, 1024:2048].rr("p (h e) -> p h e", h=hh2),
                     cd[:, hf * NH + hh2:(hf + 1) * NH].unsq(2).bc([128, hh2, HD]), ALU.mult)
                for g in range(NG):
                    pss = PS.get()
                    P.mm(pss.v, B_tm[rows, g * 128:(g + 1) * 128], xw[rows, g * 512:(g + 1) * 512])
                    P.tt("dve", Hs[:, g * 512:(g + 1) * 512], Hs[:, g * 512:(g + 1) * 512], pss.v, ALU.add)
                P.copy("act", Hb.v, Hs.v)
                if last:
                    state_store(seq)
            chk("m_chunk")
            P.tt("dve", y_sb, y_sb, zs_tm[:, st, :], ALU.mult)
            ssq = smallp.get()
            for g in range(NG):
                junk = tmpp.get()
                P.act(junk.v, y_sb[:, g * 512:(g + 1) * 512], AF.Square, accum=ssq[:, g:g + 1])
            P.act(ssq[:, 0:NG], ssq[:, 0:NG], AF.Sqrt, bias=EPS, scale=1.0 / 512)
            P.recip(ssq[:, 0:NG], ssq[:, 0:NG])
            ygn = G[2].v.bitcast(BF16)[:, 0:2048]
            for g in range(NG):
                P.act(ygn[:, g * 512:(g + 1) * 512], y_sb[:, g * 512:(g + 1) * 512], AF.Identity, scale=ssq[:, g:g + 1])
            for j8 in range(2):
                ps = PS.get()
                pv = ps.v.bitcast(BF16)
                for jj in range(8):
                    j = j8 * 8 + jj
                    P.tr(pv[:, jj * 128:(jj + 1) * 128], ygn[:, j * 128:(j + 1) * 128], ident_b.v, inc=(jj == 7))
                P.copy("act" if j8 else "dve", yg_fm[:, j8 * 8:(j8 + 1) * 8, tk], pv.rr("p (j t) -> p j t", j=8))
        chk("m_gate")
        for mb in range(4):
            wv = WS.get("wout", mb)
            for mc in range(2):
                ps = PS.get()
                for kc in range(16):
                    P.mm(ps[:, 0:T], wv[:, kc, mc * 128:(mc + 1) * 128], yg_fm[:, kc, :], start=(kc == 0), stop=(kc == 15))
                P.copy("act", mixb[:, mb * 2 + mc, :], ps[:, 0:T])
        postnorm_add(2)

    def ffn(tile, l):
        segs = tile["segs"]
        norm_to_hn(xres, 4 + l)
        for blk in range(11):
            wv = WS.get("wup%d" % l, blk)
            accs = []
            for q in range(4):
                ch = (blk * 2 + q) if q < 2 else (22 + blk * 2 + q - 2)
                ps = PS.get()
                for kc in range(KC):
                    P.mm(ps[:, 0:T], wv[:, kc, q * 128:(q + 1) * 128], hn[:, kc, :], start=(kc == 0), stop=(kc == KC - 1))
                acc = conv_chunk(ps, 3, cwf[:, l, ch, :], cbf[:, l, ch:ch + 1], tail_f[:, l, ch], segs,
                                 acc=(tmpp.get() if q < 2 else None))
                if q < 2:
                    accs.append(acc)
                else:
                    P.act(acc.v, acc.v, AF.Gelu_apprx_tanh)
                    P.tt("dve", bufA[:, blk * 2 + q - 2, :], acc[:, 0:T], accs[q - 2][:, 0:T], ALU.mult)
        chk("f_up")
        for mb in range(8):
            wv = WS.get("wdn%d" % l, mb)
            ps = PS.get()
            for kc in range(22):
                P.mm(ps[:, 0:T], wv[:, kc, :], bufA[:, kc, :], start=(kc == 0), stop=(kc == 21))
            P.copy("act", mixb[:, mb, :], ps[:, 0:T])
        chk("f_dn")
        postnorm_add(6 + l)

    def rowgroups(tile):
        if tile["kind"] == "p":
            seq, t0, L = tile["segs"][0]
            return [(st * 128, 128, seq, t0 + st * 128) for st in range(ST)]
        return [(i * 64, 64, seq, PAST) for i, (seq, t0, L) in enumerate(tile["segs"])]

    def cache_load(seq):
        s = seq - NPS
        stg = G[2]
        for half in range(2):
            sv = stg.v.rr("p (k r) -> p k r", k=8)
            P.dma("sp", sv, c_lat[s, half * 1024:(half + 1) * 1024, :].rr("(k p) r -> p k r", p=128))
            P.copy("pool", kv_tm[:, half * 8:(half + 1) * 8, :], sv)
        sk = G[3].v[:, 0:1024].rr("p (k r) -> p k r", k=16)
        P.dma("sp", sk, c_kr[s].rr("(k p) r -> p k r", p=128))
        kb2 = G[3].v.bitcast(BF16)[:, 2048:4096].rr("p (k r) -> p k r", k=16)
        P.copy("pool", kb2[:, :, 0:64], sk)
        P.copy("pool", kb2[:, :, 64:128], sk)
        for k4 in range(4):
            for rc in range(2):
                ps = PS.get()
                pv = ps.v.bitcast(BF16)
                for kk in range(4):
                    kb = k4 * 4 + kk
                    P.tr(pv[:, kk * 128:(kk + 1) * 128], kv_tm[:, kb, rc * 128:(rc + 1) * 128], ident_b.v, inc=(kk == 3))
                P.copy("act" if rc else "dve", kv_fm[:, rc, k4 * 512:(k4 + 1) * 512], pv[:, 0:512])
            ps = PS.get()
            pv = ps.v.bitcast(BF16)
            for kk in range(4):
                kb = k4 * 4 + kk
                P.tr(pv[:, kk * 128:(kk + 1) * 128], kb2[:, kb, :], ident_b.v, inc=(kk == 3))
            P.copy("act", kpe_fm[:, k4 * 512:(k4 + 1) * 512], pv[:, 0:512])

    nkv = P.sbuf("nkv", [64, NSS, 384], BF16)

    def shared_kv(tile):
        norm_to_hn(xres, 8)
        wv = WS.get("wkv", 0)
        for (c0, nr, seq, pos0) in rowgroups(tile):
            ps = PS.get()
            for kc in range(KC):
                P.mm(ps[0:nr, 0:320], hn[:, kc, c0:c0 + nr], wv[:, kc, :], start=(kc == 0), stop=(kc == KC - 1))
            junk = tmpp.get()
            ssq = smallp.get()
            P.act(junk[0:nr, 0:KVR], ps[0:nr, 0:KVR], AF.Square, accum=ssq[0:nr, 0:1])
            P.act(ssq[0:nr, 0:1], ssq[0:nr, 0:1], AF.Sqrt, bias=EPS, scale=1.0 / KVR)
            P.recip(ssq[0:nr, 0:1], ssq[0:nr, 0:1])
            ck = tmpp.get()
            P.stt(ck[0:nr, 0:KVR], ps[0:nr, 0:KVR], ssq[0:nr, 0:1], gkv_bc[0:nr, :], ALU.mult, ALU.mult)
            chk("k_norm")
            kr = tmpp.get()
            P.copy("act", kr[0:nr, 0:64], ps[0:nr, 256:320])
            rope_any(ck[:, 256:320], kr[:, 0:64], slice(0, nr), cos_tm[0:nr, pos0 // 128, :], sin_tm[0:nr, pos0 // 128, :], 1)
            chk("k_rope")
            if seq < NPS:
                P.dma("sp", o_lat_p[seq, pos0:pos0 + nr, :], ck[0:nr, 0:KVR])
                P.dma("sp", o_kr_p[seq, pos0:pos0 + nr, :], ck[0:nr, 256:320])
            else:
                r0 = (seq - NPS) * DEC_SEQ
                P.dma("sp", o_lat_s[r0:r0 + nr, :], ck[0:nr, 0:KVR])
                P.dma("sp", o_kr_s[r0:r0 + nr, :], ck[0:nr, 256:320])
            chk("k_out")
            if seq < NPS:
                append_keys(ck, nr, pos0)
            else:
                P.copy("pool", nkv[0:nr, seq - NPS, 0:320], ck[0:nr, 0:320])
                P.copy("pool", nkv[0:nr, seq - NPS, 320:384], ck[0:nr, 256:320])

    def append_keys(ck, nr, pos0, bf=None):
        kb = pos0 // 128
        if bf is None:
            t = tmpp.get()
            bf = t.v.bitcast(BF16)
            P.copy("pool", bf[0:nr, 0:320], ck[0:nr, 0:320])
            P.copy("pool", bf[0:nr, 320:384], ck[0:nr, 256:320])
        P.copy("pool", kv_tm[0:nr, kb, :], bf[0:nr, 0:KVR])
        chk("a_cp")
        ps = PS.get()
        pv = ps.v.bitcast(BF16)
        for rc in range(2):
            P.tr(pv[:, rc * 128:rc * 128 + nr], bf[0:nr, rc * 128:(rc + 1) * 128], ident_b[0:nr, 0:nr], inc=False)
        P.tr(pv[:, 256:256 + nr], bf[0:nr, 256:384], ident_b[0:nr, 0:nr], inc=True)
        chk("a_tr")
        for rc in range(2):
            P.copy("dve", kv_fm[:, rc, pos0:pos0 + nr], pv[:, rc * 128:rc * 128 + nr])
        P.copy("dve", kpe_fm[:, pos0:pos0 + nr], pv[:, 256:256 + nr])

    def mla(tile):
        segs = tile["segs"]
        norm_to_hn(xres, 1)
        cq_raw = G[7].v[:, 0:3 * T].rr("p (c t) -> p c t", c=3)
        cq = G[7].v.bitcast(BF16)[:, 2048:2048 + 3 * T].rr("p (c t) -> p c t", c=3)
        wv = WS.get("wdq", 0)
        for mc in range(3):
            ps = PS.get()
            for kc in range(KC):
                P.mm(ps[:, 0:T], wv[:, kc, mc * 128:(mc + 1) * 128], hn[:, kc, :], start=(kc == 0), stop=(kc == KC - 1))
            P.copy("act", cq_raw[:, mc, :], ps[:, 0:T])
        norm_to_hn(cq_raw, None, nk=3, dim=QR, out=cq, gcol=gq_col.v)
        qpe_fm = G[6].v.bitcast(BF16)[:, 0:8 * T].rr("p (j t) -> p j t", j=8)
        wv = WS.get("wuqr", 0)
        for st in range(ST):
            qr = G[2].v[:, 0:1024]
            for half in range(2):
                ps = PS.get()
                for c in range(3):
                    P.mm(ps.v, cq[:, c, st * 128:(st + 1) * 128], wv[:, c, half * 512:(half + 1) * 512],
                         start=(c == 0), stop=(c == 2))
                P.copy("act", qr[:, half * 512:(half + 1) * 512], ps.v)
            qrr = G[3].v.bitcast(BF16)[:, 0:1024]
            if tile["kind"] == "p":
                pos0 = segs[0][1] + st * 128
                rope_any(qrr, qr, slice(0, 128), cos_tm[:, pos0 // 128, :], sin_tm[:, pos0 // 128, :], MH)
            else:
                for hf in range(2):
                    rows = slice(hf * 64, (hf + 1) * 64)
                    rope_any(qrr, qr, rows, cos_hi[rows, :], sin_hi[rows, :], MH)
            ps = PS.get()
            pv = ps.v.bitcast(BF16)
            for j in range(8):
                P.tr(pv[:, j * 128:(j + 1) * 128], qrr[:, j * 128:(j + 1) * 128], ident_b.v, inc=(j == 7))
            P.copy("dve", qpe_fm[:, :, st * 128:(st + 1) * 128], pv.rr("p (j t) -> p j t", j=8))
        wuv = G[1].v.bitcast(BF16).rr("p (c m) -> p c m", c=2)
        P.dma("sp", G[1].v.bitcast(BF16), wsc["wuv"][0])
        wukT = wukT_sb.v
        if tile["kind"] == "p":
            seq, t0, L = segs[0]
            groups = [(0, T, seq, [(st * 128, 128, t0 + (st + 1) * 128, True) for st in range(ST)])]
        else:
            groups = [(i * 64, 64, sq, [(i * 64, 64, PAST + 64, False)]) for i, (sq, _, _) in enumerate(segs)]
        for (g0, gn, seq, qgs) in groups:
            gc = slice(g0, g0 + gn)
            if tile["kind"] == "s":
                cache_load(seq)
                append_keys(None, 64, PAST, bf=nkv[:, seq - NPS, :])
            qsets = [(G[2].v.bitcast(BF16)[:, 0:T], G[2].v.bitcast(BF16)[:, 1024:1024 + 2 * T].rr("p (c t) -> p c t", c=2)),
                     (G[7].v.bitcast(BF16)[:, 2048 + 3 * T:2048 + 4 * T],
                      G[7].v.bitcast(BF16)[:, 2048 + 4 * T:2048 + 6 * T].rr("p (c t) -> p c t", c=2))]
            wqs = {}
            s0_done = set()

            def att_s0(h):
                if h in s0_done or h >= MH:
                    return
                s0_done.add(h)
                h4, hh = divmod(h, 4)
                if hh == 0:
                    wqs[h4] = WS.get("wuqn", h4)
                wq = wqs[h4]
                qn, qlat = qsets[h % 2]
                ps = MISC
                for c in range(3):
                    P.mm(ps[:, 0:gn], wq[:, c, hh * 128:(hh + 1) * 128], cq[:, c, gc], start=(c == 0), stop=(c == 2))
                P.copy("act", qn[:, gc], ps[:, 0:gn])
                for rc in range(2):
                    P.mm(ps[:, 256 * rc:256 * rc + gn], wukT[:, h, rc * 128:(rc + 1) * 128], qn[:, gc])
                P.copy("act", qlat[:, :, gc], ps.v.rr("p (c t) -> p c t", c=2)[:, :, 0:gn])

            units = []
            gi = 0
            for h in range(MH):
                pr = slice((h % 2) * 64, (h % 2) * 64 + 64)
                for qi, (c0, nq, nk, diag) in enumerate(qgs):
                    nsb = (nk + 511) // 512
                    ob = OBS if tile["kind"] == "s" else (OBA if gi % 2 == 0 else OBB)
                    gi += 1
                    grp = dict(h=h, c0=c0, nq=nq, nk=nk, nsb=nsb, ob=ob)
                    for j in range(nsb):
                        units.append(dict(grp=grp, j=j, k0=j * 512, kn=min(512, nk - j * 512), last=(j == nsb - 1),
                                          diag=(diag and j == nsb - 1), qlat=qsets[h % 2][1], qpe=qpe_fm[pr, h // 2, :],
                                          pr=pr, h=h, head_first=(qi == 0 and j == 0)))

            def emit_s1(u):
                att_s0(u["h"])
                g = u["grp"]
                if "b1" not in g:
                    g["b1"], g["b2"], g["b3"] = smallp.get(), smallp.get(), smallp.get()
                att_s1(u)
                if u["head_first"]:
                    att_s0(u["h"] + 1)

            n = len(units)
            pend_b = [None]
            for t in range(n + 3):
                if 0 <= t - 2 < n:
                    att_s3a(units[t - 2])
                if t < n:
                    emit_s1(units[t])
                fin = None
                if 0 <= t - 3 < n:
                    att_s3b(units[t - 3])
                    if pend_b[0] is not None:
                        att_combine_b(pend_b[0], wuv)
                        pend_b[0] = None
                    if units[t - 3]["last"]:
                        fin = units[t - 3]["grp"]
                if 0 <= t - 1 < n:
                    att_s2(units[t - 1])
                if 0 <= t - 2 < n:
                    att_s3e(units[t - 2])
                if fin is not None:
                    att_combine_a(fin)
                    pend_b[0] = fin
            if pend_b[0] is not None:
                att_combine_b(pend_b[0], wuv)
                pend_b[0] = None
        for mb in range(4):
            wv = WS.get("wo", mb)
            for mc in range(2):
                ps = PS.get()
                for kc in range(16):
                    P.mm(ps[:, 0:T], wv[:, kc, mc * 128:(mc + 1) * 128], bufA[:, kc, :], start=(kc == 0), stop=(kc == 15))
                P.copy("act", mixb[:, mb * 2 + mc, :], ps[:, 0:T])
        postnorm_add(3)

    def rope_any(dst, srcv, rows, cs_t, sn_t, nh):
        nr = rows.stop - rows.start
        src = srcv[rows, :].rr("p (h e) -> p h e", h=nh)
        out = dst[rows, :].rr("p (h e) -> p h e", h=nh)
        cs = cs_t.unsq(1).bc([nr, nh, 32])
        sn = sn_t.unsq(1).bc([nr, nh, 32])
        t1 = tmpp.get()[rows, 0:nh * 32].rr("p (h e) -> p h e", h=nh)
        t2 = tmpp.get()[rows, 0:nh * 32].rr("p (h e) -> p h e", h=nh)
        x1 = src[:, :, 0:32]
        x2 = src[:, :, 32:64]
        P.tt("dve", t1, x1, cs, ALU.mult)
        P.tt("dve", t2, x2, sn, ALU.mult)
        P.tt("dve", out[:, :, 0:32], t1, t2, ALU.subtract)
        P.tt("dve", t1, x2, cs, ALU.mult)
        P.tt("dve", t2, x1, sn, ALU.mult)
        P.tt("dve", out[:, :, 32:64], t1, t2, ALU.add)

    cos_hi = P.sbuf("cos_hi", [128, 32], F32)
    sin_hi = P.sbuf("sin_hi", [128, 32], F32)
    P.dma("sp", cos_hi[0:64, :], cos_tm[0:64, 16, :])
    P.dma("sp", cos_hi[64:128, :], cos_tm[0:64, 16, :])
    P.dma("sp", sin_hi[0:64, :], sin_tm[0:64, 16, :])
    P.dma("sp", sin_hi[64:128, :], sin_tm[0:64, 16, :])

    PSB = PS.bufs
    SCp = Rot([PSB[0], PSB[1]])
    TRB = PSB[2]
    OBA = [PSB[3], PSB[4]]
    OBB = [PSB[5], PSB[6]]
    OBS = [PSB[3], PSB[4], PSB[5]]
    MISC = PSB[7]
    pbp = Rot([G[3], G[4]])
    ptp = Rot([G[5], G[0]])

    def att_s1(u):
        g = u["grp"]
        nq, c0, k0, kn, pr = g["nq"], g["c0"], u["k0"], u["kn"], u["pr"]
        ps = SCp.get()
        u["bank"] = ps
        qlat = u["qlat"]
        P.mm(ps[0:nq, 0:kn], qlat[:, 0, c0:c0 + nq], kv_fm[:, 0, k0:k0 + kn], start=True, stop=False)
        P.mm(ps[0:nq, 0:kn], qlat[:, 1, c0:c0 + nq], kv_fm[:, 1, k0:k0 + kn], start=False, stop=False)
        P.mm(ps[0:nq, 0:kn], u["qpe"][:, c0:c0 + nq], kpe_fm[pr, k0:k0 + kn], start=False, stop=not u["diag"],
             inc=not u["diag"])
        if u["diag"]:
            P.mm(ps[0:nq, kn - 128:kn], mrow[0:1, 0, :], mrow[0:1, 1, :], start=False, stop=True)

    def att_s2(u):
        g = u["grp"]
        nq, j, kn = g["nq"], u["j"], u["kn"]
        ps = u["bank"]
        mx = g["b1"][0:nq, j:j + 1]
        nb = g["b1"][0:nq, 8 + j:9 + j]
        P.reduce(mx, ps[0:nq, 0:kn], ALU.max)
        P.ts("dve", nb, mx, -SCALE, ALU.mult)
        pb = pbp.get().v.bitcast(BF16)
        u["pb"] = pb
        P.act(pb[0:nq, 0:kn], ps[0:nq, 0:kn], AF.Exp, bias=nb, scale=SCALE, accum=g["b2"][0:nq, j:j + 1])

    def att_s3a(u):
        g = u["grp"]
        nq, j, k0, kn = g["nq"], u["j"], u["k0"], u["kn"]
        pb = u["pb"]
        nkb = (kn + 127) // 128
        pv = TRB.v.bitcast(BF16).rr("p (k q) -> p k q", k=8)
        pt = ptp.get().v.bitcast(BF16)[:, 0:512].rr("p (k q) -> p k q", k=4)
        for kk in range(nkb):
            kc = min(128, kn - kk * 128)
            P.tr(pv[0:kc, kk, 0:nq], pb[0:nq, kk * 128:kk * 128 + kc], ident_b[0:nq, 0:nq], inc=(kk == nkb - 1))
        u["pv"] = pv
        u["pt"] = pt

    def att_s3e(u):
        g = u["grp"]
        nq, kn = g["nq"], u["kn"]
        nkb = (kn + 127) // 128
        pv, pt = u["pv"], u["pt"]
        full = kn // 128
        if full > 0:
            P.copy("dve", pt[:, 0:full, 0:nq], pv[:, 0:full, 0:nq])
        if full < nkb:
            kc = kn - full * 128
            P.copy("dve", pt[0:kc, full, 0:nq], pv[0:kc, full, 0:nq])
        u["pt"] = pt

    def att_s3b(u):
        g = u["grp"]
        nq, j, k0, kn = g["nq"], u["j"], u["k0"], u["kn"]
        nkb = (kn + 127) // 128
        pt = u["pt"]
        ob = g["ob"][j // 2]
        oc = (j % 2) * 256
        for kk in range(nkb):
            kc = min(128, kn - kk * 128)
            kb = k0 // 128 + kk
            P.mm(ob[0:nq, oc:oc + KVR], pt[0:kc, kk, 0:nq], kv_tm[0:kc, kb, :], start=(kk == 0), stop=(kk == nkb - 1))

    def att_combine_a(g):
        nq, nsb, c0, h = g["nq"], g["nsb"], g["c0"], g["h"]
        b1, b2, b3 = g["b1"], g["b2"], g["b3"]
        on = tmpp.get().v.bitcast(BF16)
        rl = b3[0:nq, 40:41]
        if nsb == 1:
            P.recip(rl, b2[0:nq, 0:1])
            P.act(on[0:nq, 0:KVR], g["ob"][0][0:nq, 0:KVR], AF.Identity, scale=rl)
        else:
            m = b3[0:nq, 41:42]
            P.reduce(m, b1[0:nq, 0:nsb], ALU.max)
            P.ts("dve", m, m, -SCALE, ALU.mult)
            ws = b3[0:nq, 0:nsb]
            P.act(ws, b1[0:nq, 0:nsb], AF.Exp, bias=m, scale=SCALE)
            lw = b3[0:nq, 16:16 + nsb]
            P.tt("dve", lw, ws, b2[0:nq, 0:nsb], ALU.mult)
            P.reduce(rl, lw, ALU.add)
            P.recip(rl, rl)
            acc = tmpp.get()
            for j in range(nsb):
                oj = g["ob"][j // 2][0:nq, (j % 2) * 256:(j % 2) * 256 + KVR]
                if j == 0:
                    P.ts("dve", acc[0:nq, 0:KVR], oj, ws[:, 0:1], ALU.mult)
                else:
                    P.stt(acc[0:nq, 0:KVR], oj, ws[:, j:j + 1], acc[0:nq, 0:KVR], ALU.mult, ALU.add)
            P.act(on[0:nq, 0:KVR], acc[0:nq, 0:KVR], AF.Identity, scale=rl)
        g["on"] = on

    def att_combine_b(g, wuv):
        nq, c0, h = g["nq"], g["c0"], g["h"]
        on = g["on"]
        ps = TRB
        pv = ps.v.bitcast(BF16)[:, 512:768]
        for rc in range(2):
            P.tr(pv[:, rc * 128:rc * 128 + nq], on[0:nq, rc * 128:(rc + 1) * 128], ident_b[0:nq, 0:nq], inc=(rc == 1))
        ol = tmpp.get().v.bitcast(BF16)
        P.copy("dve", ol[:, 0:256].rr("p (c q) -> p c q", c=2)[:, :, 0:nq], pv.rr("p (c q) -> p c q", c=2)[:, :, 0:nq])
        for rc in range(2):
            P.mm(ps[:, 384:384 + nq], wuv[:, rc, h * 128:(h + 1) * 128], ol[:, rc * 128:rc * 128 + nq], start=(rc == 0), stop=(rc == 1))
        P.copy("act", bufA[:, h, c0:c0 + nq], ps[:, 384:384 + nq])

    try:
        phase = (stop_after or {}).get("phase", 99)
        for ti, tile in enumerate(tiles):
            if phase < 2:
                break
            load_x(tile)
            tail_init(tile)
            if phase < 3:
                break
            mamba(tile)
            if phase < 4:
                break
            ffn(tile, 0)
            if phase < 5:
                break
            shared_kv(tile)
            if phase < 6:
                break
            mla(tile)
            if phase < 7:
                break
            ffn(tile, 1)
            tail_out(tile)
            store_y(tile)
    except _Stop:
        pass
    P.finish()
    return P


_CACHE = {}


def _consts():
    k = np.arange(128)
    l = np.arange(64)
    u2 = ((k[:, None] % 64) <= l[None, :]).astype(np.float32)
    delta = np.where((k[:, None] % 64) == (l[None, :] + 1), -NEGBIG, 0.0).astype(np.float32)
    pos = (np.arange(17)[None, :] * 128 + k[:, None]).astype(np.float32)
    jj = np.broadcast_to(np.arange(32, dtype=np.float32)[None, :], (128, 32)).copy()
    mrow = np.zeros((2, 128), np.float32)
    mrow[0, :64] = 1.0
    mrow[1, 64:] = NEGBIG
    return dict(k_ident=np.eye(128, dtype=np.float32), k_u2=u2, k_delta=delta, k_pos=pos, k_j=jj, k_mrow=mrow)


def kernel(x_prompt, x_sample, state_ssm, state_ssm_conv, state_ffn_conv, cache_kv_latent, cache_k_rope,
           norm_mix_pre, norm_mix_post, norm_ffn_pre, norm_ffn_post,
           ssm_w_in, ssm_conv_w, ssm_conv_b, ssm_dt_bias, ssm_a_log, ssm_d, ssm_norm, ssm_w_out,
           kv_norm_in, kv_w_dkv, kv_norm, kv_w_kr, kv_w_uk, kv_w_uv,
           mla_w_dq, mla_q_norm, mla_w_uq, mla_w_o,
           ffn_w_up, ffn_conv_w, ffn_conv_b, ffn_w_down, _stop_after=None):
    f = lambda a: np.ascontiguousarray(np.asarray(a, dtype=np.float32))
    if "prog" not in _CACHE or _stop_after is not None:
        _CACHE["prog"] = build_program(_stop_after)
    prog = _CACHE["prog"]
    shared = dict(
        g_mix_pre=f(norm_mix_pre), g_mix_post=f(norm_mix_post), g_ffn_pre=f(norm_ffn_pre), g_ffn_post=f(norm_ffn_post),
        w_in=f(ssm_w_in)[0], cw_ssm=f(ssm_conv_w)[0], cb_ssm=f(ssm_conv_b)[0], dt_bias=f(ssm_dt_bias)[0],
        a_log=f(ssm_a_log)[0], d_skip=f(ssm_d)[0], g_ssm=f(ssm_norm)[0], w_out=f(ssm_w_out)[0],
        g_kvin=f(kv_norm_in), w_dkv=f(kv_w_dkv), g_kv=f(kv_norm), w_kr=f(kv_w_kr),
        w_uk=f(kv_w_uk).reshape(KVR, MH * NOPE), w_uv=f(kv_w_uv).reshape(KVR, MH * VD),
        w_dq=f(mla_w_dq)[0], g_q=f(mla_q_norm)[0], w_uq=f(mla_w_uq)[0], w_o=f(mla_w_o)[0],
        w_up=f(ffn_w_up), cw_ffn=f(ffn_conv_w), cb_ffn=f(ffn_conv_b), w_dn=f(ffn_w_down),
    )
    shared.update(_consts())
    xp = f(x_prompt)
    xs_ = f(x_sample)
    sst = f(state_ssm)[0]
    ssc = f(state_ssm_conv)[0]
    sfc = f(state_ffn_conv)
    cl = f(cache_kv_latent)
    ck = f(cache_k_rope)
    in_maps = []
    for c in range(8):
        m = dict(shared)
        m["xp"] = xp[NPS * c:NPS * (c + 1)]
        m["xs"] = xs_[NSS * c:NSS * (c + 1)].reshape(NSS * DEC_SEQ, D)
        m["st_ssm"] = sst[NSS * c:NSS * (c + 1)].reshape(NSS, DI, DS)
        m["st_sconv"] = ssc[NSS * c:NSS * (c + 1)]
        m["st_fconv"] = np.ascontiguousarray(sfc[:, NSS * c:NSS * (c + 1)])
        m["c_lat"] = cl[NSS * c:NSS * (c + 1)]
        m["c_kr"] = ck[NSS * c:NSS * (c + 1)]
        in_maps.append(m)
    res = run_bass_kernel_spmd(prog.nc, in_maps, core_ids=list(range(8)))
    R = res.results
    cat = lambda name: [np.asarray(r[name]) for r in R]
    y_prompt = np.concatenate(cat("y_p"), axis=0)
    y_sample = np.concatenate([a.reshape(NSS, DEC_SEQ, D) for a in cat("y_s")], axis=0)
    ossm = cat("o_ssm")
    osc = cat("o_sconv")
    ofc = cat("o_fconv")
    p_ssm = np.concatenate([a[:NPS] for a in ossm], axis=0).reshape(1, 8 * NPS, NH, HD, DS)
    s_ssm = np.concatenate([a[NPS:] for a in ossm], axis=0).reshape(1, 8 * NSS, NH, HD, DS)
    p_sc = np.concatenate([a[:NPS] for a in osc], axis=0)[None]
    s_sc = np.concatenate([a[NPS:] for a in osc], axis=0)[None]
    p_fc = np.concatenate([a[:, :NPS] for a in ofc], axis=1)
    s_fc = np.concatenate([a[:, NPS:] for a in ofc], axis=1)
    p_lat = np.concatenate(cat("o_lat_p"), axis=0)
    p_kr = np.concatenate(cat("o_kr_p"), axis=0)
    s_lat = np.concatenate([a.reshape(NSS, DEC_SEQ, KVR) for a in cat("o_lat_s")], axis=0)
    s_kr = np.concatenate([a.reshape(NSS, DEC_SEQ, ROPE) for a in cat("o_kr_s")], axis=0)
    outs = (y_prompt, y_sample, p_ssm, p_sc, p_fc, p_lat, p_kr, s_ssm, s_sc, s_fc, s_lat, s_kr)
    return tuple(np.ascontiguousarray(o, dtype=np.float32) for o in outs)
```

```python
import math
import numpy as np
import concourse.bass as bass
import concourse.mybir as mybir
from concourse.bass_utils import run_bass_kernel_spmd

F32 = mybir.dt.float32
BF16 = mybir.dt.bfloat16
I32 = mybir.dt.int32
ALU = mybir.AluOpType
AF = mybir.ActivationFunctionType
AX = mybir.AxisListType


class V:
    __slots__ = ("buf", "ap")

    def __init__(self, buf, ap):
        self.buf = buf
        self.ap = ap

    def __getitem__(self, key):
        return V(self.buf, self.ap[key])

    def bitcast(self, dt):
        return V(self.buf, self.ap.bitcast(dt))

    def rr(self, pattern, **kw):
        return V(self.buf, self.ap.rearrange(pattern, **kw))

    def bc(self, shape):
        return V(self.buf, self.ap.broadcast_to(list(shape)))

    def unsq(self, axis):
        return V(self.buf, self.ap.unsqueeze(axis))

    def pbc(self, n):
        return V(self.buf, self.ap.partition_broadcast(n))


class Buf:
    def __init__(self, name, t, tracked=True):
        self.name = name
        self.t = t
        self.tracked = tracked
        self.w = {}
        self.r = {}
        self.dsem = None
        self.psum = False

    def __getitem__(self, key):
        return V(self, self.t[key])

    @property
    def v(self):
        return V(self, self.t.ap())


class Eng:
    def __init__(self, name, handle, sem):
        self.name = name
        self.h = handle
        self.sem = sem
        self.count = 0
        self.items = []
        self.waited = {}


class Prog:
    def __init__(self):
        self.nc = bass.Bass("TRN2", target_bir_lowering=False)
        nc = self.nc
        self.sems = {}
        self.E = {}
        for name, h in (("pe", nc.tensor), ("act", nc.scalar), ("dve", nc.vector),
                        ("pool", nc.gpsimd), ("sp", nc.sync)):
            self.E[name] = Eng(name, h, self._sem("e_" + name))
        self.dma_tot = {}
        self.nbuf = 0

    def _sem(self, name):
        s = self.nc.alloc_semaphore(name)
        self.sems[name] = s
        return s

    def sbuf(self, name, shape, dt):
        return Buf(name, self.nc.alloc_sbuf_tensor(name, list(shape), dt))

    def psum(self, name, shape, dt=F32):
        b = Buf(name, self.nc.alloc_psum_tensor(name, list(shape), dt))
        b.psum = True
        return b

    def dram(self, name, shape, dt, kind="Internal"):
        t = self.nc.dram_tensor(name, list(shape), dt, kind=kind)
        return Buf(name, t, tracked=(kind == "Internal"))

    def _need(self, eng, deps):
        for key, val in deps.items():
            if eng.waited.get(key, 0) >= val:
                continue
            if key == "e_" + eng.name:
                assert val <= eng.count, "same-engine wait on a pending (non-inc) op"
            eng.waited[key] = val
            eng.items.append(("wait", key, val))

    def op(self, en, fn, reads=(), writes=(), inc=True):
        eng = self.E[en]
        me = "e_" + en
        deps = {}

        def add(d, skip=None):
            for k, v in d.items():
                if k != skip and v > deps.get(k, 0):
                    deps[k] = v
        wset = set(id(b) for b in writes)
        for b in reads:
            if b.tracked:
                add(b.w)
                if b.psum:
                    add(b.r, me)
        skip = me if en == "pe" else None
        for b in writes:
            if b.tracked:
                add(b.w, skip)
                add(b.r, skip)
        self._need(eng, deps)
        if inc:
            eng.count += 1
            tok = eng.count
        else:
            tok = eng.count + 1
        eng.items.append(("ins", fn, inc))
        for b in reads:
            if b.tracked and id(b) not in wset:
                b.r[me] = tok
        for b in writes:
            if b.tracked:
                b.w = {me: tok}
                b.r = {}
        return tok

    def dma(self, qn, out, in_):
        eng = self.E[qn]
        src, dst = in_.buf, out.buf
        deps = {}

        def add(d):
            for k, v in d.items():
                if v > deps.get(k, 0):
                    deps[k] = v
        if src.tracked:
            add(src.w)
        if dst.tracked:
            add(dst.w)
            add(dst.r)
        self._need(eng, deps)
        owner = dst if dst.tracked else src
        if owner.dsem is None:
            owner.dsem = "d_%d" % self.nbuf
            self.nbuf += 1
            self._sem(owner.dsem)
            self.dma_tot[owner.dsem] = 0
        key = owner.dsem
        self.dma_tot[key] += 16
        val = self.dma_tot[key]
        eng.items.append(("dma", out.ap, in_.ap, key))
        if src.tracked:
            src.r[key] = val
        if dst.tracked:
            dst.w = {key: val}
            dst.r = {}

    def finish(self):
        sp = self.E["sp"]
        deps = dict(self.dma_tot)
        for n, e in self.E.items():
            if n != "sp" and e.count:
                deps["e_" + n] = e.count
        self._need(sp, deps)
        nc = self.nc
        with nc.allow_non_contiguous_dma(reason="small strided parameter/state loads"), nc.Block() as block:
            for n, deco in (("pe", block.tensor), ("act", block.scalar), ("dve", block.vector),
                            ("pool", block.gpsimd), ("sp", block.sync)):
                eng = self.E[n]

                def body(e, eng=eng):
                    for it in eng.items:
                        if it[0] == "wait":
                            e.wait_ge(self.sems[it[1]], it[2])
                        elif it[0] == "ins":
                            ins = it[1](e)
                            if it[2]:
                                ins.then_inc(eng.sem, 1)
                        else:
                            _, oap, iap, key = it
                            e.dma_start(out=oap, in_=iap).then_inc(self.sems[key], 16)
                deco(body)
        return nc

    def mm(self, out, lhsT, rhs, start=True, stop=True, inc=None):
        if inc is None:
            inc = stop
        return self.op("pe", lambda e: e.matmul(out.ap, lhsT=lhsT.ap, rhs=rhs.ap, start=start, stop=stop),
                       reads=[lhsT.buf, rhs.buf], writes=[out.buf], inc=inc)

    def tr(self, out, in_, ident, inc=True):
        return self.op("pe", lambda e: e.transpose(out.ap, in_.ap, ident.ap),
                       reads=[in_.buf, ident.buf], writes=[out.buf], inc=inc)

    def act(self, out, in_, func, bias=None, scale=None, accum=None):
        reads = [in_.buf]
        writes = [out.buf]
        kw = {}
        if bias is not None:
            if isinstance(bias, V):
                reads.append(bias.buf)
                kw["bias"] = bias.ap
            else:
                kw["bias"] = bias
        if scale is not None:
            if isinstance(scale, V):
                reads.append(scale.buf)
                kw["scale"] = scale.ap
            else:
                kw["scale"] = scale
        if accum is not None:
            writes.append(accum.buf)
            kw["accum_out"] = accum.ap
        return self.op("act", lambda e: e.activation(out.ap, in_.ap, func, **kw), reads=reads, writes=writes)

    def tt(self, en, out, a, b, op):
        return self.op(en, lambda e: e.tensor_tensor(out.ap, a.ap, b.ap, op),
                       reads=[a.buf, b.buf], writes=[out.buf])

    def ts(self, en, out, a, s1, op0, s2=None, op1=None):
        reads = [a.buf]
        s1a = s1.ap if isinstance(s1, V) else s1
        s2a = s2.ap if isinstance(s2, V) else s2
        if isinstance(s1, V):
            reads.append(s1.buf)
        if isinstance(s2, V):
            reads.append(s2.buf)
        kw = {}
        if op1 is not None:
            kw["op1"] = op1
        return self.op(en, lambda e: e.tensor_scalar(out.ap, a.ap, s1a, s2a, op0, **kw),
                       reads=reads, writes=[out.buf])

    def stt(self, out, a, s, b, op0, op1):
        reads = [a.buf, b.buf]
        sa = s.ap if isinstance(s, V) else s
        if isinstance(s, V):
            reads.append(s.buf)
        return self.op("dve", lambda e: e.scalar_tensor_tensor(out.ap, a.ap, sa, b.ap, op0, op1),
                       reads=reads, writes=[out.buf])

    def copy(self, en, out, in_):
        if en == "act":
            return self.op(en, lambda e: e.copy(out.ap, in_.ap), reads=[in_.buf], writes=[out.buf])
        return self.op(en, lambda e: e.tensor_copy(out.ap, in_.ap), reads=[in_.buf], writes=[out.buf])

    def memset(self, en, out, val):
        return self.op(en, lambda e: e.memset(out.ap, val), reads=[], writes=[out.buf])

    def recip(self, out, in_):
        return self.op("dve", lambda e: e.reciprocal(out.ap, in_.ap), reads=[in_.buf], writes=[out.buf])

    def reduce(self, out, in_, op):
        return self.op("dve", lambda e: e.tensor_reduce(out.ap, in_.ap, AX.X, op), reads=[in_.buf], writes=[out.buf])


class Rot:
    def __init__(self, bufs):
        self.bufs = bufs
        self.i = 0

    def get(self):
        b = self.bufs[self.i % len(self.bufs)]
        self.i += 1
        return b


D = 1024
KC = 8
SEQ = 2048
DEC_SEQ = 64
PAST = 2048
DI = 2048
NH = 32
HD = 64
NG = 4
DS = 128
CONVD = 3072
NXC = 24
INP = 5152
DFF = 2816
NFC = 44
MH = 16
QR = 384
KVR = 256
NOPE = 128
ROPE = 64
VD = 128
EPS = 1e-6
THETA = 10000.0
T = 256
ST = T // 128
NCH = T // 64
NPS = 2
NSS = 4
NSEQ = NPS + NSS
NEGBIG = -30000.0
SCALE = (NOPE + ROPE) ** -0.5
NSLOT = 3
SLOT_E = 4096


class _Stop(Exception):
    pass


def build_program(stop_after=None):
    P = Prog()
    nc = P.nc
    stop_tag = (stop_after or {}).get("tag")

    def chk(tag):
        if tag == stop_tag:
            raise _Stop()

    def din(name, shape):
        return P.dram(name, shape, F32, kind="ExternalInput")

    def dout(name, shape):
        return P.dram(name, shape, F32, kind="ExternalOutput")

    xp = din("xp", [NPS, SEQ, D])
    xs = din("xs", [NSS * DEC_SEQ, D])
    st_ssm = din("st_ssm", [NSS, DI, DS])
    st_sconv = din("st_sconv", [NSS, 3, CONVD])
    st_fconv = din("st_fconv", [2, NSS, 2, 2 * DFF])
    c_lat = din("c_lat", [NSS, PAST, KVR])
    c_kr = din("c_kr", [NSS, PAST, ROPE])
    g_mix_pre = din("g_mix_pre", [2, D])
    g_mix_post = din("g_mix_post", [2, D])
    g_ffn_pre = din("g_ffn_pre", [2, D])
    g_ffn_post = din("g_ffn_post", [2, D])
    w_in = din("w_in", [D, INP])
    cw_ssm = din("cw_ssm", [4, CONVD])
    cb_ssm = din("cb_ssm", [CONVD])
    dt_bias = din("dt_bias", [NH])
    a_log = din("a_log", [NH])
    d_skip = din("d_skip", [NH])
    g_ssm = din("g_ssm", [DI])
    w_out = din("w_out", [DI, D])
    g_kvin = din("g_kvin", [D])
    w_dkv = din("w_dkv", [D, KVR])
    g_kv = din("g_kv", [KVR])
    w_kr = din("w_kr", [D, ROPE])
    w_uk = din("w_uk", [KVR, MH * NOPE])
    w_uv = din("w_uv", [KVR, MH * VD])
    w_dq = din("w_dq", [D, QR])
    g_q = din("g_q", [QR])
    w_uq = din("w_uq", [QR, MH * (NOPE + ROPE)])
    w_o = din("w_o", [MH * VD, D])
    w_up = din("w_up", [2, D, 2 * DFF])
    cw_ffn = din("cw_ffn", [2, 3, 2 * DFF])
    cb_ffn = din("cb_ffn", [2, 2 * DFF])
    w_dn = din("w_dn", [2, DFF, D])
    k_ident = din("k_ident", [128, 128])
    k_u2 = din("k_u2", [128, 64])
    k_delta = din("k_delta", [128, 64])
    k_pos = din("k_pos", [128, 17])
    k_j = din("k_j", [128, 32])
    k_mrow = din("k_mrow", [2, 128])

    y_p = dout("y_p", [NPS, SEQ, D])
    y_s = dout("y_s", [NSS * DEC_SEQ, D])
    o_ssm = dout("o_ssm", [NSEQ, DI, DS])
    o_sconv = dout("o_sconv", [NSEQ, 3, CONVD])
    o_fconv = dout("o_fconv", [2, NSEQ, 2, 2 * DFF])
    o_lat_p = dout("o_lat_p", [NPS, SEQ, KVR])
    o_kr_p = dout("o_kr_p", [NPS, SEQ, ROPE])
    o_lat_s = dout("o_lat_s", [NSS * DEC_SEQ, KVR])
    o_kr_s = dout("o_kr_s", [NSS * DEC_SEQ, ROPE])

    class Chunked:
        def __init__(self, bufs):
            self.bufs = bufs

        def __getitem__(self, key):
            p, c, t = key
            return self.bufs[c][p, t]

    xres = Chunked([P.sbuf("xres%d" % i, [128, T], F32) for i in range(KC)])
    mixb = Chunked([P.sbuf("mixb%d" % i, [128, T], F32) for i in range(KC)])
    hn = Chunked([P.sbuf("hn%d" % i, [128, T], BF16) for i in range(KC)])
    bufA = Chunked([P.sbuf("bufA%d" % i, [128, T], BF16) for i in range(NXC)])
    G = [P.sbuf("G%d" % i, [128, 2048], F32) for i in range(8)]
    Hs = P.sbuf("Hs", [128, DI], F32)
    Hb = P.sbuf("Hb", [128, DI], BF16)
    kv_fm = P.sbuf("kv_fm", [128, 2, PAST + 64], BF16)
    kpe_fm = P.sbuf("kpe_fm", [128, PAST + 64], BF16)
    kv_tm = P.sbuf("kv_tm", [128, 17, KVR], BF16)
    ring = [P.sbuf("ring%d" % i, [128, SLOT_E], BF16) for i in range(NSLOT)]
    extp = Rot([P.sbuf("ext%d" % i, [128, T + 16], F32) for i in range(2)])
    caccp = Rot([P.sbuf("cacc%d" % i, [128, T], F32) for i in range(2)])
    sqbp = Rot([P.sbuf("sqb%d" % i, [128, T], BF16) for i in range(2)])
    rstdp = Rot([P.sbuf("rstd%d" % i, [128, T], F32) for i in range(2)])
    tmpp = Rot([P.sbuf("tmp%d" % i, [128, 512], F32) for i in range(6)])
    smallp = Rot([P.sbuf("small%d" % i, [128, 64], F32) for i in range(12)])
    ident_f = P.sbuf("ident_f", [128, 128], F32)
    ident_b = P.sbuf("ident_b", [128, 128], BF16)
    ones_b = P.sbuf("ones_b", [128, 128], BF16)
    ones_f = P.sbuf("ones_f", [128, 128], F32)
    u2 = P.sbuf("u2", [128, 64], F32)
    negu2 = P.sbuf("negu2", [128, 64], F32)
    delta = P.sbuf("delta", [128, 64], F32)
    mrow = P.sbuf("mrow", [1, 2, 128], BF16)
    mrow_f = P.sbuf("mrow_f", [1, 2, 128], F32)
    diagD = P.sbuf("diagD", [128, 16, 128], BF16)
    dcol = P.sbuf("dcol", [128, 16], F32)
    cos_tm = P.sbuf("cos_tm", [128, 17, 32], F32)
    sin_tm = P.sbuf("sin_tm", [128, 17, 32], F32)
    gcols = P.sbuf("gcols", [128, 9, KC], F32)
    gq_col = P.sbuf("gq_col", [128, 3], F32)
    gssm_col = P.sbuf("gssm_col", [128, 16], F32)
    gkv_bc = P.sbuf("gkv_bc", [128, KVR], F32)
    dtb_bc = P.sbuf("dtb_bc", [128, NH], F32)
    a_bc = P.sbuf("a_bc", [128, NH], F32)
    cws = P.sbuf("cws", [128, NXC, 4], F32)
    cbs = P.sbuf("cbs", [128, NXC], F32)
    cwf = P.sbuf("cwf", [128, 2, NFC, 3], F32)
    cbf = P.sbuf("cbf", [128, 2, NFC], F32)
    tail_s = P.sbuf("tail_s", [128, NXC, 4, 3], F32)
    tail_f = P.sbuf("tail_f", [128, 2, NFC, 4, 2], F32)
    dt_tm = P.sbuf("dt_tm", [128, ST, NH], F32)
    PS = Rot([P.psum("ps%d" % i, [128, 512], F32) for i in range(8)])

    P.dma("sp", ident_f.v, k_ident.v)
    P.copy("dve", ident_b.v, ident_f.v)
    P.memset("pool", ones_b.v, 1.0)
    P.memset("pool", ones_f.v, 1.0)
    onesH = P.sbuf("onesH", [128, 2, 128], F32)
    P.memset("pool", onesH.v, 0.0)
    P.memset("pool", onesH[0:64, 0, :], 1.0)
    P.memset("pool", onesH[64:128, 1, :], 1.0)
    P.dma("sp", u2.v, k_u2.v)
    P.ts("dve", negu2.v, u2.v, -1.0, ALU.mult)
    P.dma("sp", delta.v, k_delta.v)
    P.dma("sp", mrow_f.v, k_mrow.v.unsq(0))
    P.copy("dve", mrow.v, mrow_f.v)
    for i, gsrc in enumerate((g_mix_pre, g_mix_post, g_ffn_pre, g_ffn_post)):
        for l in range(2):
            P.dma("sp", gcols[:, 2 * i + l, :], gsrc[l].rr("(c p) -> p c", p=128))
    P.dma("sp", gcols[:, 8, :], g_kvin.v.rr("(c p) -> p c", p=128))
    P.dma("sp", gq_col.v, g_q.v.rr("(c p) -> p c", p=128))
    P.dma("sp", gssm_col.v, g_ssm.v.rr("(c p) -> p c", p=128))
    P.dma("sp", gkv_bc.v, g_kv.v.pbc(128))
    P.dma("sp", dtb_bc.v, dt_bias.v.pbc(128))
    P.dma("sp", a_bc.v, a_log.v.pbc(128))
    P.act(a_bc.v, a_bc.v, AF.Exp)
    P.ts("dve", a_bc.v, a_bc.v, -1.0, ALU.mult)
    for i in range(4):
        P.dma("sp", cws[:, :, i], cw_ssm[i].rr("(c p) -> p c", p=128))
    P.dma("sp", cbs.v, cb_ssm.v.rr("(c p) -> p c", p=128))
    for l in range(2):
        for i in range(3):
            P.dma("sp", cwf[:, l, :, i], cw_ffn[l, i].rr("(c p) -> p c", p=128))
        P.dma("sp", cbf[:, l], cb_ffn[l].rr("(c p) -> p c", p=128))
    for hf in range(2):
        P.dma("sp", dcol[hf * 64:(hf + 1) * 64, :], d_skip.v[hf::2].pbc(64))
    for j in range(16):
        P.ts("dve", diagD[:, j, :], ident_f.v, dcol[:, j:j + 1], ALU.mult)
    posb = P.sbuf("posb", [128, 17], F32)
    invb = P.sbuf("invb", [128, 32], F32)
    angb = V(G[2], G[2].t[:, 0:544].rearrange("p (b j) -> p b j", b=17))
    angi = V(G[3], G[3].t[:, 0:544].bitcast(I32).rearrange("p (b j) -> p b j", b=17))
    angk = V(G[4], G[4].t[:, 0:544].rearrange("p (b j) -> p b j", b=17))
    P.dma("sp", posb.v, k_pos.v)
    P.dma("sp", invb.v, k_j.v)
    P.act(invb.v, invb.v, AF.Exp, scale=-math.log(THETA) / 32.0)
    for b in range(17):
        P.ts("dve", angb[:, b, :], invb.v, posb[:, b:b + 1], ALU.mult)
    for tab, shift in ((sin_tm, 0.0), (cos_tm, math.pi / 2)):
        P.ts("dve", angk, angb, shift, ALU.add, 1.0 / (2 * math.pi), ALU.mult)
        P.copy("dve", angi, angk)
        P.copy("dve", angk, angi)
        P.stt(angk, angk, -2 * math.pi, angb, ALU.mult, ALU.add)
        if shift != 0.0:
            P.ts("dve", angk, angk, shift, ALU.add)
        P.ts("dve", angk, angk, -3.1415925, ALU.max, 3.1415925, ALU.min)
        P.act(tab.v, angk, AF.Sin)

    wsc = {}
    wshape = {}

    def wdef(name, nblk, kc, m):
        wsc[name] = P.dram("wsc_" + name, [nblk, 128, kc * m], BF16)
        wshape[name] = (nblk, kc, m)

    wdef("wz", 4, 8, 512)
    wdef("wx", 6, 8, 512)
    wdef("wdt", 1, 8, 32)
    wdef("wout", 4, 16, 256)
    for l in range(2):
        wdef("wup%d" % l, 11, 8, 512)
        wdef("wdn%d" % l, 8, 22, 128)
    wdef("wkv", 1, 8, 320)
    wdef("wdq", 1, 8, 384)
    wdef("wuqn", 4, 3, 512)
    wdef("wuqr", 1, 3, 1024)
    wdef("wo", 4, 16, 256)
    wdef("wuv", 1, 2, 2048)

    stgp = Rot([G[0], G[1], G[2], G[3], G[4], G[5]])
    cast_eng = Rot(["dve", "pool", "act"])
    ring_i = [0]

    def kview(w, c0, c1):
        return w.rr("(c p) m -> p c m", p=128)[:, :, c0:c1]

    def prep(name, j, parts, gain=None):
        nblk, kc, m = wshape[name]
        slot = ring[ring_i[0] % NSLOT]
        ring_i[0] += 1
        bv = slot[:, 0:kc * m].rr("p (c m) -> p c m", c=kc)
        cper = max(1, 2048 // m)
        for c0 in range(0, kc, cper):
            c1 = min(kc, c0 + cper)
            stg = stgp.get()
            sv = stg[:, 0:(c1 - c0) * m].rr("p (c m) -> p c m", c=c1 - c0)
            for src, off, mi in parts:
                P.dma("sp", sv[:, :, off:off + mi], src[:, c0:c1, :])
            if gain is None:
                P.copy(cast_eng.get(), bv[:, c0:c1, :], sv)
            else:
                for c in range(c0, c1):
                    P.ts("dve" if c % 2 else "pool", bv[:, c, :], sv[:, c - c0, :], gain[:, c:c + 1], ALU.mult)
        P.dma("sp", wsc[name][j], slot[:, 0:kc * m])

    w_in_v = w_in.v
    _skip_prologue = (stop_after or {}).get("phase", 99) < 1
    for j in range(4):
        prep("wz", j, [(kview(w_in_v, j * 512, (j + 1) * 512), 0, 512)])
    prep("wdt", 0, [(kview(w_in_v, 5120, 5152), 0, 32)])
    for j in range(6):
        prep("wx", j, [(kview(w_in_v, 2048 + j * 512, 2048 + (j + 1) * 512), 0, 512)])
    for j in range(4):
        prep("wout", j, [(kview(w_out.v, j * 256, (j + 1) * 256), 0, 256)], gain=gssm_col.v)
    prep("wkv", 0, [(kview(w_dkv.v, 0, 256), 0, 256), (kview(w_kr.v, 0, 64), 256, 64)])
    for l in range(2):
        for j in range(11):
            prep("wup%d" % l, j, [(kview(w_up[l], j * 256, (j + 1) * 256), 0, 256),
                                  (kview(w_up[l], DFF + j * 256, DFF + (j + 1) * 256), 256, 256)])
        for j in range(8):
            prep("wdn%d" % l, j, [(kview(w_dn[l], j * 128, (j + 1) * 128), 0, 128)])
    prep("wdq", 0, [(kview(w_dq.v, 0, 384), 0, 384)])
    uq4 = w_uq.v.rr("(c p) (h e) -> p c h e", p=128, e=192)
    for j in range(4):
        stg = stgp.get()
        sv = stg[:, 0:1536].rr("p (c h e) -> p c h e", c=3, h=4)
        for c in range(3):
            P.dma("sp", sv[:, c], uq4[:, c, 4 * j:4 * j + 4, 0:128])
        slot = ring[ring_i[0] % NSLOT]
        ring_i[0] += 1
        P.copy(cast_eng.get(), slot[:, 0:1536], stg[:, 0:1536])
        P.dma("sp", wsc["wuqn"][j], slot[:, 0:1536])
    slot = ring[ring_i[0] % NSLOT]
    ring_i[0] += 1
    for c in range(3):
        stg = stgp.get()
        sv = stg[:, 0:1024].rr("p (h e) -> p h e", h=16)
        P.dma("sp", sv, uq4[:, c, :, 128:192])
        P.copy(cast_eng.get(), slot[:, c * 1024:(c + 1) * 1024], stg[:, 0:1024])
    P.dma("sp", wsc["wuqr"][0], slot[:, 0:3072])
    for j in range(4):
        prep("wo", j, [(kview(w_o.v, j * 256, (j + 1) * 256), 0, 256)])
    stg = stgp.get()
    stg2 = stgp.get()
    P.dma("sp", stg.v, w_uv.v[0:128, :])
    P.dma("sp", stg2.v, w_uv.v[128:256, :])
    slot = ring[ring_i[0] % NSLOT]
    ring_i[0] += 1
    P.copy("dve", slot[:, 0:2048], stg.v)
    P.copy("pool", slot[:, 2048:4096], stg2.v)
    P.dma("sp", wsc["wuv"][0], slot.v)
    stg = stgp.get()
    stg2 = stgp.get()
    P.dma("sp", stg.v, w_uk.v[0:128, :])
    P.dma("sp", stg2.v, w_uk.v[128:256, :])
    wukT_sb = P.sbuf("wukT_sb", [128, 16, 256], BF16)
    sl4 = wukT_sb.v
    for rc, sg in enumerate((stg, stg2)):
        for h4 in range(4):
            ps = PS.get()
            for hh in range(4):
                h = h4 * 4 + hh
                P.tr(ps[:, hh * 128:(hh + 1) * 128], sg[:, h * 128:(h + 1) * 128], ident_f.v, inc=(hh == 3))
            P.copy("act" if h4 % 2 else "dve", sl4[:, h4 * 4:h4 * 4 + 4, rc * 128:(rc + 1) * 128],
                   ps.v.rr("p (h r) -> p h r", h=4))

    class WStream:
        def __init__(self):
            self.order = []
            self.pos = 0
            self.issued = 0
            self.slot_of = {}

        def plan(self, lst):
            self.order = lst

        def _issue(self):
            name, j = self.order[self.issued]
            nblk, kc, m = wshape[name]
            slot = ring[(ring_i[0] + self.issued) % NSLOT]
            P.dma("sp", slot[:, 0:kc * m], wsc[name][j])
            self.slot_of[self.issued] = slot
            self.issued += 1

        def get(self, name, j):
            assert self.order[self.pos] == (name, j), (self.order[self.pos], name, j)
            while self.issued < min(len(self.order), self.pos + NSLOT):
                self._issue()
            slot = self.slot_of.pop(self.pos)
            self.pos += 1
            nblk, kc, m = wshape[name]
            return slot[:, 0:kc * m].rr("p (c m) -> p c m", c=kc)

    WS = WStream()

    def pass_blocks(kind):
        lst = [("wz", j) for j in range(4)] + [("wdt", 0)] + [("wx", j) for j in range(6)]
        lst += [("wout", j) for j in range(4)]
        lst += [("wup0", j) for j in range(11)] + [("wdn0", j) for j in range(8)]
        lst += [("wkv", 0), ("wdq", 0), ("wuqr", 0)]
        lst += [("wuqn", j) for j in range(4)] * (NSS if kind == "s" else 1)
        lst += [("wo", j) for j in range(4)]
        lst += [("wup1", j) for j in range(11)] + [("wdn1", j) for j in range(8)]
        return lst

    tiles = []
    for s in range(NPS):
        for t0 in range(0, SEQ, T):
            tiles.append(dict(kind="p", segs=[(s, t0, T)]))
    tiles.append(dict(kind="s", segs=[(NPS + i, 0, DEC_SEQ) for i in range(NSS)]))
    if stop_after is not None and stop_after.get("ntiles"):
        tiles = tiles[:stop_after["ntiles"]]
    if stop_after is not None and stop_after.get("tiles"):
        tiles = [tiles[i] for i in stop_after["tiles"]]
    order = []
    for tl in tiles:
        order += pass_blocks(tl["kind"])
    WS.plan(order)

    def norm_to_hn(src, gidx, nk=KC, dim=D, out=None, gcol=None):
        out = hn if out is None else out
        ps = PS.get()
        for kc in range(nk):
            sq = sqbp.get()
            P.act(sq.v, src[:, kc, :], AF.Square)
            P.mm(ps[:, 0:T], ones_b.v, sq.v, start=(kc == 0), stop=(kc == nk - 1), inc=True)
        rs = rstdp.get()
        P.act(rs.v, ps[:, 0:T], AF.Sqrt, bias=EPS, scale=1.0 / dim)
        P.recip(rs.v, rs.v)
        for kc in range(nk):
            g = gcols[:, gidx, kc:kc + 1] if gcol is None else gcol[:, kc:kc + 1]
            P.stt(out[:, kc, :], src[:, kc, :], g, rs.v, ALU.mult, ALU.mult)

    def postnorm_add(gidx):
        ps = PS.get()
        for kc in range(KC):
            sq = sqbp.get()
            P.act(sq.v, mixb[:, kc, :], AF.Square)
            P.mm(ps[:, 0:T], ones_b.v, sq.v, start=(kc == 0), stop=(kc == KC - 1), inc=True)
        rs = rstdp.get()
        P.act(rs.v, ps[:, 0:T], AF.Sqrt, bias=EPS, scale=1.0 / D)
        P.recip(rs.v, rs.v)
        for kc in range(KC):
            P.tt("pool" if kc % 2 else "dve", mixb[:, kc, :], mixb[:, kc, :], rs.v, ALU.mult)
            P.stt(xres[:, kc, :], mixb[:, kc, :], gcols[:, gidx, kc:kc + 1], xres[:, kc, :], ALU.mult, ALU.add)

    def conv_chunk(ps, K, wv, bv, tail, segs, acc=None):
        nseg = len(segs)
        L = segs[0][2]
        ext = extp.get()
        ev = ext[:, 0:nseg * (L + K - 1)].rr("p (s l) -> p s l", s=nseg)
        P.copy("pool", ev[:, :, 0:K - 1], tail[:, 0:nseg, :])
        P.copy("act", ev[:, :, K - 1:], ps[:, 0:T].rr("p (s l) -> p s l", s=nseg))
        if acc is None:
            acc = caccp.get()
        av = acc[:, 0:T].rr("p (s l) -> p s l", s=nseg)
        P.act(av, ev[:, :, 0:L], AF.Identity, bias=bv, scale=wv[:, 0:1])
        for i in range(1, K):
            P.stt(av, ev[:, :, i:i + L], wv[:, i:i + 1], av, ALU.mult, ALU.add)
        P.copy("pool", tail[:, 0:nseg, :], ev[:, :, L:L + K - 1])
        return acc

    def tail_init(tile):
        segs = tile["segs"]
        if tile["kind"] == "p":
            if segs[0][1] == 0:
                P.memset("pool", tail_s.v, 0.0)
                P.memset("pool", tail_f.v, 0.0)
        else:
            for i, (seq, t0, L) in enumerate(segs):
                for r in range(3):
                    P.dma("sp", tail_s[:, :, i, r], st_sconv[seq - NPS, r].rr("(c p) -> p c", p=128))
                for l in range(2):
                    for r in range(2):
                        P.dma("sp", tail_f[:, l, :, i, r], st_fconv[l, seq - NPS, r].rr("(c p) -> p c", p=128))

    def tail_out(tile):
        segs = tile["segs"]
        for i, (seq, t0, L) in enumerate(segs):
            if tile["kind"] == "p" and t0 + L < SEQ:
                continue
            for r in range(3):
                P.dma("sp", o_sconv[seq, r].rr("(c p) -> p c", p=128), tail_s[:, :, i, r])
            for l in range(2):
                for r in range(2):
                    P.dma("sp", o_fconv[l, seq, r].rr("(c p) -> p c", p=128), tail_f[:, l, :, i, r])

    def load_x(tile):
        stage = G[0].v[:, 0:ST * D].rr("p (s d) -> p s d", s=ST)
        if tile["kind"] == "p":
            seq, t0, L = tile["segs"][0]
            P.dma("sp", stage, xp[seq, t0:t0 + T, :].rr("(s p) d -> p s d", p=128))
        else:
            P.dma("sp", stage, xs.v.rr("(s p) d -> p s d", p=128))
        for kc in range(KC):
            ps = PS.get()
            for st in range(ST):
                P.tr(ps[:, st * 128:(st + 1) * 128], stage[:, st, kc * 128:(kc + 1) * 128], ident_f.v,
                     inc=(st == ST - 1))
            P.copy("act" if kc % 2 else "dve", xres[:, kc, :], ps[:, 0:T])

    def store_y(tile):
        stage = G[1].v[:, 0:ST * D].rr("p (s d) -> p s d", s=ST)
        for st in range(ST):
            for k4 in range(2):
                ps = PS.get()
                for kk in range(4):
                    kc = k4 * 4 + kk
                    P.tr(ps[:, kk * 128:(kk + 1) * 128], xres[:, kc, st * 128:(st + 1) * 128], ident_f.v,
                         inc=(kk == 3))
                P.copy("act" if k4 % 2 else "dve", stage[:, st, k4 * 512:(k4 + 1) * 512], ps.v)
        if tile["kind"] == "p":
            seq, t0, L = tile["segs"][0]
            P.dma("sp", y_p[seq, t0:t0 + T, :].rr("(s p) d -> p s d", p=128), stage)
        else:
            P.dma("sp", y_s.v.rr("(s p) d -> p s d", p=128), stage)

    zs_tm = G[0].v.bitcast(BF16).rr("p (s c) -> p s c", s=ST)
    yg_fm = G[1].v.bitcast(BF16).rr("p (c t) -> p c t", c=16)

    def state_load(seq):
        if seq < NPS:
            P.memset("pool", Hs.v, 0.0)
            P.memset("pool", Hb.v, 0.0)
            return
        stg = G[2]
        sv = stg.v.rr("p (j n) -> p j n", j=16)
        P.dma("sp", sv, st_ssm[seq - NPS].rr("(j p) n -> p j n", p=128))
        for j4 in range(4):
            ps = PS.get()
            for jj in range(4):
                j = j4 * 4 + jj
                P.tr(ps[:, jj * 128:(jj + 1) * 128], sv[:, j, :], ident_f.v, inc=(jj == 3))
            P.copy("act" if j4 % 2 else "dve", Hs[:, j4 * 512:(j4 + 1) * 512], ps.v)
        P.copy("pool", Hb.v, Hs.v)

    def state_store(seq):
        stg = G[2]
        sv = stg.v.rr("p (j n) -> p j n", j=16)
        for j4 in range(4):
            ps = PS.get()
            for jj in range(4):
                j = j4 * 4 + jj
                P.tr(ps[:, jj * 128:(jj + 1) * 128], Hs[:, j * 128:(j + 1) * 128], ident_f.v, inc=(jj == 3))
            P.copy("act" if j4 % 2 else "dve", sv[:, j4 * 4:j4 * 4 + 4, :], ps.v.rr("p (j n) -> p j n", j=4))
        P.dma("sp", o_ssm[seq].rr("(j p) n -> p j n", p=128), sv)

    def mamba(tile):
        segs = tile["segs"]
        norm_to_hn(xres, 0)
        for zb in range(4):
            wv = WS.get("wz", zb)
            for st in range(ST):
                ps = PS.get()
                for kc in range(KC):
                    P.mm(ps.v, hn[:, kc, st * 128:(st + 1) * 128], wv[:, kc, :], start=(kc == 0), stop=(kc == KC - 1))
                P.act(zs_tm[:, st, zb * 512:(zb + 1) * 512], ps.v, AF.Silu)
        chk("m_z")
        wv = WS.get("wdt", 0)
        for st in range(ST):
            ps = PS.get()
            for kc in range(KC):
                P.mm(ps[:, 0:NH], hn[:, kc, st * 128:(st + 1) * 128], wv[:, kc, :], start=(kc == 0), stop=(kc == KC - 1))
            xr = smallp.get()[:, 0:NH]
            ab = smallp.get()[:, 0:NH]
            P.tt("dve", xr, ps[:, 0:NH], dtb_bc.v, ALU.add)
            P.act(ab, xr, AF.Abs)
            P.act(ab, ab, AF.Exp, scale=-1.0)
            P.act(ab, ab, AF.Ln, bias=1.0)
            P.stt(dt_tm[:, st, :], xr, 0.0, ab, ALU.max, ALU.add)
        chk("m_dt")
        pend_silu = []
        for cb in range(6):
            wv = WS.get("wx", cb)
            for mc in range(4):
                ch = cb * 4 + mc
                ps = PS.get()
                for kc in range(KC):
                    P.mm(ps[:, 0:T], wv[:, kc, mc * 128:(mc + 1) * 128], hn[:, kc, :], start=(kc == 0), stop=(kc == KC - 1))
                acc = conv_chunk(ps, 4, cws[:, ch, :], cbs[:, ch:ch + 1], tail_s[:, ch], segs, None)
                if pend_silu:
                    pa, pch = pend_silu.pop()
                    P.act(bufA[:, pch, :], pa.v, AF.Silu)
                pend_silu.append((acc, ch))
        if pend_silu:
            pa, pch = pend_silu.pop()
            P.act(bufA[:, pch, :], pa.v, AF.Silu)
        chk("m_x")
        for st in range(ST):
            tk = slice(st * 128, (st + 1) * 128)
            x_tm = G[7].v.bitcast(BF16)[:, 0:2048]
            B_tm = G[7].v.bitcast(BF16)[:, 2048:2560]
            xdt = G[6].v.bitcast(BF16)[:, 0:2048]
            xw = G[6].v.bitcast(BF16)[:, 2048:4096]
            dec = G[5].v.bitcast(BF16)[:, 0:2048]
            wts = G[5].v.bitcast(BF16)[:, 2048:4096]
            Rm = G[2].v
            DAB = G[3].v
            y_sb = G[4].v
            for j8 in range(2):
                ps = PS.get()
                pv = ps.v.bitcast(BF16)
                for jj in range(8):
                    j = j8 * 8 + jj
                    P.tr(pv[:, jj * 128:(jj + 1) * 128], bufA[:, j, tk], ident_b.v, inc=(jj == 7))
                P.copy("act" if j8 else "dve", x_tm[:, j8 * 1024:(j8 + 1) * 1024], pv)
            ps = PS.get()
            pv = ps.v.bitcast(BF16)
            for g in range(NG):
                P.tr(pv[:, g * 128:(g + 1) * 128], bufA[:, 16 + g, tk], ident_b.v, inc=(g == NG - 1))
            P.copy("act", B_tm, pv[:, 0:512])
            chk("m_tr")
            da = smallp.get()[:, 0:NH]
            P.tt("dve", da, dt_tm[:, st, :], a_bc.v, ALU.mult)
            P.tt("pool", xdt.rr("p (h e) -> p h e", h=NH), x_tm.rr("p (h e) -> p h e", h=NH),
                 dt_tm[:, st, :].unsq(2).bc([128, NH, HD]), ALU.mult)
            dab3 = da.unsq(2).bc([128, NH, 64])
            P.tt("dve", Rm.rr("p (h l) -> p h l", h=NH), dab3, u2.v.unsq(1).bc([128, NH, 64]), ALU.mult)
            P.tt("pool", DAB.rr("p (h l) -> p h l", h=NH), dab3, delta.v.unsq(1).bc([128, NH, 64]), ALU.add)
            chk("m_rd")
            for g in range(NG):
                ps = PS.get()
                for hf in range(2):
                    rows = slice(hf * 64, (hf + 1) * 64)
                    P.mm(ps[rows, :], ones_f[rows, 0:64], Rm[rows, g * 512:(g + 1) * 512], start=True, stop=False)
                    P.mm(ps[rows, :], negu2[rows, :], DAB[rows, g * 512:(g + 1) * 512], start=False, stop=True,
                         inc=(hf == 1))
                P.act(dec[:, g * 512:(g + 1) * 512], ps.v, AF.Exp)
            P.tt("pool", xw.rr("p (h e) -> p h e", h=NH), xdt.rr("p (h e) -> p h e", h=NH),
                 dec.rr("p (h l) -> p h l", h=NH)[:, :, 63:64].bc([128, NH, HD]), ALU.mult)
            chk("m_seg")
            ps = PS.get()
            for hf in range(2):
                rows = slice(hf * 64, (hf + 1) * 64)
                P.mm(ps[rows, 0:NH], u2[rows, :], da[rows, :], inc=False)
            for hf in range(2):
                rows = slice(hf * 64, (hf + 1) * 64)
                P.mm(ps[:, 64 + hf * NH:64 + (hf + 1) * NH], onesH[:, hf, :], da, inc=(hf == 1))
            eac = smallp.get()
            P.act(eac[:, 0:NH], ps[:, 0:NH], AF.Exp)
            cd = smallp.get()
            P.act(cd.v, ps[:, 64:128], AF.Exp)
            chk("m_eac")
            psc = PS.get()
            for hf in range(2):
                rows = slice(hf * 64, (hf + 1) * 64)
                tok = slice(st * 128 + hf * 64, st * 128 + (hf + 1) * 64)
                for g in range(NG):
                    P.mm(psc[rows, g * 64:(g + 1) * 64], bufA[:, 16 + g, tok], bufA[:, 20 + g, tok],
                         inc=(hf == 1 and g == NG - 1))
            P.tt("dve", wts.rr("p (g e l) -> p g e l", g=NG, e=8), dec.rr("p (g e l) -> p g e l", g=NG, e=8),
                 psc[:, 0:256].rr("p (g l) -> p g l", g=NG).unsq(2).bc([128, NG, 8, 64]), ALU.mult)
            chk("m_cb")
            for hf in range(2):
                c = st * 2 + hf
                rows = slice(hf * 64, (hf + 1) * 64)
                tok = slice(c * 64, (c + 1) * 64)
                if tile["kind"] == "p":
                    seq, t0, L = segs[0]
                    first = (t0 == 0 and c == 0)
                    last = (t0 + T == SEQ and c == NCH - 1)
                else:
                    seq = segs[c][0]
                    first = last = True
                if first:
                    state_load(seq)
                for g in range(NG):
                    pso = PS.get()
                    P.mm(pso[rows, :], bufA[:, 20 + g, tok], Hb[:, g * 512:(g + 1) * 512])
                    psd = PS.get()
                    for e in range(8):
                        h = g * 8 + e
                        P.mm(psd[rows, e * 64:(e + 1) * 64], wts[rows, h * 64:(h + 1) * 64], xdt[rows, h * 64:(h + 1) * 64],
                             start=(e == 0), stop=False, inc=False)
                    for jj in range(4):
                        j = g * 4 + jj
                        P.mm(psd[rows, jj * 128:(jj + 1) * 128], bufA[:, j, tok], diagD[:, j, :], start=False,
                             stop=(jj == 3), inc=(jj == 3))
                    tmp = tmpp.get()
                    P.tt("dve", tmp[rows, :].rr("p (e d) -> p e d", e=8), pso[rows, :].rr("p (e d) -> p e d", e=8),
                         eac[rows, g * 8:(g + 1) * 8].unsq(2).bc([64, 8, HD]), ALU.mult)
                    P.tt("dve", y_sb[rows, g * 512:(g + 1) * 512], psd[rows, :], tmp[rows, :], ALU.add)
                hh2 = NH // 2
                P.tt("pool", Hs[:, 0:1024].rr("p (h e) -> p h e", h=hh2), Hs[:, 0:1024].rr("p (h e) -> p h e", h=hh2),
                     cd[:, hf * NH:hf * NH + hh2].unsq(2).bc([128, hh2, HD]), ALU.mult)
                P.tt("dve", Hs[:, 1024:2048].rr("p (h e) -> p h e", h=hh2), Hs[:, 1024:2048].rr("p (h e) -> p h e", h=hh2),
                     cd[:, hf * NH + hh2:(hf + 1) * NH].unsq(2).bc([128, hh2, HD]), ALU.mult)
                for g in range(NG):
                    pss = PS.get()
                    P.mm(pss.v, B_tm[rows, g * 128:(g + 1) * 128], xw[rows, g * 512:(g + 1) * 512])
                    P.tt("dve", Hs[:, g * 512:(g + 1) * 512], Hs[:, g * 512:(g + 1) * 512], pss.v, ALU.add)
                P.copy("act", Hb.v, Hs.v)
                if last:
                    state_store(seq)
            chk("m_chunk")
            P.tt("dve", y_sb, y_sb, zs_tm[:, st, :], ALU.mult)
            ssq = smallp.get()
            for g in range(NG):
                junk = tmpp.get()
                P.act(junk.v, y_sb[:, g * 512:(g + 1) * 512], AF.Square, accum=ssq[:, g:g + 1])
            P.act(ssq[:, 0:NG], ssq[:, 0:NG], AF.Sqrt, bias=EPS, scale=1.0 / 512)
            P.recip(ssq[:, 0:NG], ssq[:, 0:NG])
            ygn = G[2].v.bitcast(BF16)[:, 0:2048]
            for g in range(NG):
                P.act(ygn[:, g * 512:(g + 1) * 512], y_sb[:, g * 512:(g + 1) * 512], AF.Identity, scale=ssq[:, g:g + 1])
            for j8 in range(2):
                ps = PS.get()
                pv = ps.v.bitcast(BF16)
                for jj in range(8):
                    j = j8 * 8 + jj
                    P.tr(pv[:, jj * 128:(jj + 1) * 128], ygn[:, j * 128:(j + 1) * 128], ident_b.v, inc=(jj == 7))
                P.copy("act" if j8 else "dve", yg_fm[:, j8 * 8:(j8 + 1) * 8, tk], pv.rr("p (j t) -> p j t", j=8))
        chk("m_gate")
        for mb in range(4):
            wv = WS.get("wout", mb)
            for mc in range(2):
                ps = PS.get()
                for kc in range(16):
                    P.mm(ps[:, 0:T], wv[:, kc, mc * 128:(mc + 1) * 128], yg_fm[:, kc, :], start=(kc == 0), stop=(kc == 15))
                P.copy("act", mixb[:, mb * 2 + mc, :], ps[:, 0:T])
        postnorm_add(2)

    def ffn(tile, l):
        segs = tile["segs"]
        norm_to_hn(xres, 4 + l)
        pend_gate = []
        for blk in range(11):
            wv = WS.get("wup%d" % l, blk)
            accs = []
            for q in range(4):
                ch = (blk * 2 + q) if q < 2 else (22 + blk * 2 + q - 2)
                ps = PS.get()
                for kc in range(KC):
                    P.mm(ps[:, 0:T], wv[:, kc, q * 128:(q + 1) * 128], hn[:, kc, :], start=(kc == 0), stop=(kc == KC - 1))
                acc = conv_chunk(ps, 3, cwf[:, l, ch, :], cbf[:, l, ch:ch + 1], tail_f[:, l, ch], segs,
                                 acc=(tmpp.get() if q < 2 else None))
                while pend_gate:
                    ga, gv, gidx = pend_gate.pop(0)
                    P.act(ga.v, ga.v, AF.Gelu_apprx_tanh)
                    P.tt("dve", bufA[:, gidx, :], ga[:, 0:T], gv[:, 0:T], ALU.mult)
                if q < 2:
                    accs.append(acc)
                else:
                    pend_gate.append((acc, accs[q - 2], blk * 2 + q - 2))
        while pend_gate:
            ga, gv, gidx = pend_gate.pop(0)
            P.act(ga.v, ga.v, AF.Gelu_apprx_tanh)
            P.tt("dve", bufA[:, gidx, :], ga[:, 0:T], gv[:, 0:T], ALU.mult)
        chk("f_up")
        for mb in range(8):
            wv = WS.get("wdn%d" % l, mb)
            ps = PS.get()
            for kc in range(22):
                P.mm(ps[:, 0:T], wv[:, kc, :], bufA[:, kc, :], start=(kc == 0), stop=(kc == 21))
            P.copy("act", mixb[:, mb, :], ps[:, 0:T])
        chk("f_dn")
        postnorm_add(6 + l)

    def rowgroups(tile):
        if tile["kind"] == "p":
            seq, t0, L = tile["segs"][0]
            return [(st * 128, 128, seq, t0 + st * 128) for st in range(ST)]
        return [(i * 64, 64, seq, PAST) for i, (seq, t0, L) in enumerate(tile["segs"])]

    def cache_load(seq):
        s = seq - NPS
        stg = G[2]
        for half in range(2):
            sv = stg.v.rr("p (k r) -> p k r", k=8)
            P.dma("sp", sv, c_lat[s, half * 1024:(half + 1) * 1024, :].rr("(k p) r -> p k r", p=128))
            P.copy("pool", kv_tm[:, half * 8:(half + 1) * 8, :], sv)
        sk = G[3].v[:, 0:1024].rr("p (k r) -> p k r", k=16)
        P.dma("sp", sk, c_kr[s].rr("(k p) r -> p k r", p=128))
        kb2 = G[3].v.bitcast(BF16)[:, 2048:4096].rr("p (k r) -> p k r", k=16)
        P.copy("pool", kb2[:, :, 0:64], sk)
        P.copy("pool", kb2[:, :, 64:128], sk)
        for k4 in range(4):
            for rc in range(2):
                ps = PS.get()
                pv = ps.v.bitcast(BF16)
                for kk in range(4):
                    kb = k4 * 4 + kk
                    P.tr(pv[:, kk * 128:(kk + 1) * 128], kv_tm[:, kb, rc * 128:(rc + 1) * 128], ident_b.v, inc=(kk == 3))
                P.copy("act" if rc else "dve", kv_fm[:, rc, k4 * 512:(k4 + 1) * 512], pv[:, 0:512])
            ps = PS.get()
            pv = ps.v.bitcast(BF16)
            for kk in range(4):
                kb = k4 * 4 + kk
                P.tr(pv[:, kk * 128:(kk + 1) * 128], kb2[:, kb, :], ident_b.v, inc=(kk == 3))
            P.copy("act", kpe_fm[:, k4 * 512:(k4 + 1) * 512], pv[:, 0:512])

    nkv = P.sbuf("nkv", [64, NSS, 384], BF16)

    def shared_kv(tile):
        norm_to_hn(xres, 8)
        wv = WS.get("wkv", 0)
        for (c0, nr, seq, pos0) in rowgroups(tile):
            ps = PS.get()
            for kc in range(KC):
                P.mm(ps[0:nr, 0:320], hn[:, kc, c0:c0 + nr], wv[:, kc, :], start=(kc == 0), stop=(kc == KC - 1))
            junk = tmpp.get()
            ssq = smallp.get()
            P.act(junk[0:nr, 0:KVR], ps[0:nr, 0:KVR], AF.Square, accum=ssq[0:nr, 0:1])
            P.act(ssq[0:nr, 0:1], ssq[0:nr, 0:1], AF.Sqrt, bias=EPS, scale=1.0 / KVR)
            P.recip(ssq[0:nr, 0:1], ssq[0:nr, 0:1])
            ck = tmpp.get()
            P.stt(ck[0:nr, 0:KVR], ps[0:nr, 0:KVR], ssq[0:nr, 0:1], gkv_bc[0:nr, :], ALU.mult, ALU.mult)
            chk("k_norm")
            kr = tmpp.get()
            P.copy("act", kr[0:nr, 0:64], ps[0:nr, 256:320])
            rope_any(ck[:, 256:320], kr[:, 0:64], slice(0, nr), cos_tm[0:nr, pos0 // 128, :], sin_tm[0:nr, pos0 // 128, :], 1)
            chk("k_rope")
            if seq < NPS:
                P.dma("sp", o_lat_p[seq, pos0:pos0 + nr, :], ck[0:nr, 0:KVR])
                P.dma("sp", o_kr_p[seq, pos0:pos0 + nr, :], ck[0:nr, 256:320])
            else:
                r0 = (seq - NPS) * DEC_SEQ
                P.dma("sp", o_lat_s[r0:r0 + nr, :], ck[0:nr, 0:KVR])
                P.dma("sp", o_kr_s[r0:r0 + nr, :], ck[0:nr, 256:320])
            chk("k_out")
            if seq < NPS:
                append_keys(ck, nr, pos0)
            else:
                P.copy("pool", nkv[0:nr, seq - NPS, 0:320], ck[0:nr, 0:320])
                P.copy("pool", nkv[0:nr, seq - NPS, 320:384], ck[0:nr, 256:320])

    def append_keys(ck, nr, pos0, bf=None):
        kb = pos0 // 128
        if bf is None:
            t = tmpp.get()
            bf = t.v.bitcast(BF16)
            P.copy("pool", bf[0:nr, 0:320], ck[0:nr, 0:320])
            P.copy("pool", bf[0:nr, 320:384], ck[0:nr, 256:320])
        P.copy("pool", kv_tm[0:nr, kb, :], bf[0:nr, 0:KVR])
        chk("a_cp")
        ps = PS.get()
        pv = ps.v.bitcast(BF16)
        for rc in range(2):
            P.tr(pv[:, rc * 128:rc * 128 + nr], bf[0:nr, rc * 128:(rc + 1) * 128], ident_b[0:nr, 0:nr], inc=False)
        P.tr(pv[:, 256:256 + nr], bf[0:nr, 256:384], ident_b[0:nr, 0:nr], inc=True)
        chk("a_tr")
        for rc in range(2):
            P.copy("dve", kv_fm[:, rc, pos0:pos0 + nr], pv[:, rc * 128:rc * 128 + nr])
        P.copy("dve", kpe_fm[:, pos0:pos0 + nr], pv[:, 256:256 + nr])

    def mla(tile):
        segs = tile["segs"]
        norm_to_hn(xres, 1)
        cq_raw = G[7].v[:, 0:3 * T].rr("p (c t) -> p c t", c=3)
        cq = G[7].v.bitcast(BF16)[:, 2048:2048 + 3 * T].rr("p (c t) -> p c t", c=3)
        wv = WS.get("wdq", 0)
        for mc in range(3):
            ps = PS.get()
            for kc in range(KC):
                P.mm(ps[:, 0:T], wv[:, kc, mc * 128:(mc + 1) * 128], hn[:, kc, :], start=(kc == 0), stop=(kc == KC - 1))
            P.copy("act", cq_raw[:, mc, :], ps[:, 0:T])
        norm_to_hn(cq_raw, None, nk=3, dim=QR, out=cq, gcol=gq_col.v)
        qpe_fm = G[6].v.bitcast(BF16)[:, 0:8 * T].rr("p (j t) -> p j t", j=8)
        wv = WS.get("wuqr", 0)
        for st in range(ST):
            qr = G[2].v[:, 0:1024]
            for half in range(2):
                ps = PS.get()
                for c in range(3):
                    P.mm(ps.v, cq[:, c, st * 128:(st + 1) * 128], wv[:, c, half * 512:(half + 1) * 512],
                         start=(c == 0), stop=(c == 2))
                P.copy("act", qr[:, half * 512:(half + 1) * 512], ps.v)
            qrr = G[3].v.bitcast(BF16)[:, 0:1024]
            if tile["kind"] == "p":
                pos0 = segs[0][1] + st * 128
                rope_any(qrr, qr, slice(0, 128), cos_tm[:, pos0 // 128, :], sin_tm[:, pos0 // 128, :], MH)
            else:
                for hf in range(2):
                    rows = slice(hf * 64, (hf + 1) * 64)
                    rope_any(qrr, qr, rows, cos_hi[rows, :], sin_hi[rows, :], MH)
            ps = PS.get()
            pv = ps.v.bitcast(BF16)
            for j in range(8):
                P.tr(pv[:, j * 128:(j + 1) * 128], qrr[:, j * 128:(j + 1) * 128], ident_b.v, inc=(j == 7))
            P.copy("dve", qpe_fm[:, :, st * 128:(st + 1) * 128], pv.rr("p (j t) -> p j t", j=8))
        wuv = G[1].v.bitcast(BF16).rr("p (c m) -> p c m", c=2)
        P.dma("sp", G[1].v.bitcast(BF16), wsc["wuv"][0])
        wukT = wukT_sb.v
        if tile["kind"] == "p":
            seq, t0, L = segs[0]
            groups = [(0, T, seq, [(st * 128, 128, t0 + (st + 1) * 128, True) for st in range(ST)])]
        else:
            groups = [(i * 64, 64, sq, [(i * 64, 64, PAST + 64, False)]) for i, (sq, _, _) in enumerate(segs)]
        for (g0, gn, seq, qgs) in groups:
            gc = slice(g0, g0 + gn)
            if tile["kind"] == "s":
                cache_load(seq)
                append_keys(None, 64, PAST, bf=nkv[:, seq - NPS, :])
            qsets = [(G[2].v.bitcast(BF16)[:, 0:T], G[2].v.bitcast(BF16)[:, 1024:1024 + 2 * T].rr("p (c t) -> p c t", c=2)),
                     (G[7].v.bitcast(BF16)[:, 2048 + 3 * T:2048 + 4 * T],
                      G[7].v.bitcast(BF16)[:, 2048 + 4 * T:2048 + 6 * T].rr("p (c t) -> p c t", c=2))]
            wqs = {}
            s0_done = set()

            def att_s0(h):
                if h in s0_done or h >= MH:
                    return
                s0_done.add(h)
                h4, hh = divmod(h, 4)
                if hh == 0:
                    wqs[h4] = WS.get("wuqn", h4)
                wq = wqs[h4]
                qn, qlat = qsets[h % 2]
                ps = MISC
                for c in range(3):
                    P.mm(ps[:, 0:gn], wq[:, c, hh * 128:(hh + 1) * 128], cq[:, c, gc], start=(c == 0), stop=(c == 2))
                P.copy("act", qn[:, gc], ps[:, 0:gn])
                for rc in range(2):
                    P.mm(ps[:, 256 * rc:256 * rc + gn], wukT[:, h, rc * 128:(rc + 1) * 128], qn[:, gc])
                P.copy("act", qlat[:, :, gc], ps.v.rr("p (c t) -> p c t", c=2)[:, :, 0:gn])

            units = []
            gi = 0
            for h in range(MH):
                pr = slice((h % 2) * 64, (h % 2) * 64 + 64)
                for qi, (c0, nq, nk, diag) in enumerate(qgs):
                    nsb = (nk + 511) // 512
                    ob = OBS if tile["kind"] == "s" else (OBA if gi % 2 == 0 else OBB)
                    gi += 1
                    grp = dict(h=h, c0=c0, nq=nq, nk=nk, nsb=nsb, ob=ob)
                    for j in range(nsb):
                        units.append(dict(grp=grp, j=j, k0=j * 512, kn=min(512, nk - j * 512), last=(j == nsb - 1),
                                          diag=(diag and j == nsb - 1), qlat=qsets[h % 2][1], qpe=qpe_fm[pr, h // 2, :],
                                          pr=pr, h=h, head_first=(qi == 0 and j == 0)))

            def emit_s1(u):
                att_s0(u["h"])
                g = u["grp"]
                if "b1" not in g:
                    g["b1"], g["b2"], g["b3"] = smallp.get(), smallp.get(), smallp.get()
                att_s1(u)
                if u["head_first"]:
                    att_s0(u["h"] + 1)

            n = len(units)
            pend_b = [None]
            for t in range(n + 3):
                if 0 <= t - 2 < n:
                    att_s3a(units[t - 2])
                if t < n:
                    emit_s1(units[t])
                fin = None
                if 0 <= t - 3 < n:
                    att_s3b(units[t - 3])
                    if pend_b[0] is not None:
                        att_combine_b(pend_b[0], wuv)
                        pend_b[0] = None
                    if units[t - 3]["last"]:
                        fin = units[t - 3]["grp"]
                if 0 <= t - 1 < n:
                    att_s2(units[t - 1])
                if 0 <= t - 2 < n:
                    att_s3e(units[t - 2])
                if fin is not None:
                    att_combine_a(fin)
                    pend_b[0] = fin
            if pend_b[0] is not None:
                att_combine_b(pend_b[0], wuv)
                pend_b[0] = None
        for mb in range(4):
            wv = WS.get("wo", mb)
            for mc in range(2):
                ps = PS.get()
                for kc in range(16):
                    P.mm(ps[:, 0:T], wv[:, kc, mc * 128:(mc + 1) * 128], bufA[:, kc, :], start=(kc == 0), stop=(kc == 15))
                P.copy("act", mixb[:, mb * 2 + mc, :], ps[:, 0:T])
        postnorm_add(3)

    def rope_any(dst, srcv, rows, cs_t, sn_t, nh):
        nr = rows.stop - rows.start
        src = srcv[rows, :].rr("p (h e) -> p h e", h=nh)
        out = dst[rows, :].rr("p (h e) -> p h e", h=nh)
        cs = cs_t.unsq(1).bc([nr, nh, 32])
        sn = sn_t.unsq(1).bc([nr, nh, 32])
        t1 = tmpp.get()[rows, 0:nh * 32].rr("p (h e) -> p h e", h=nh)
        t2 = tmpp.get()[rows, 0:nh * 32].rr("p (h e) -> p h e", h=nh)
        x1 = src[:, :, 0:32]
        x2 = src[:, :, 32:64]
        P.tt("dve", t1, x1, cs, ALU.mult)
        P.tt("dve", t2, x2, sn, ALU.mult)
        P.tt("dve", out[:, :, 0:32], t1, t2, ALU.subtract)
        P.tt("dve", t1, x2, cs, ALU.mult)
        P.tt("dve", t2, x1, sn, ALU.mult)
        P.tt("dve", out[:, :, 32:64], t1, t2, ALU.add)

    cos_hi = P.sbuf("cos_hi", [128, 32], F32)
    sin_hi = P.sbuf("sin_hi", [128, 32], F32)
    P.dma("sp", cos_hi[0:64, :], cos_tm[0:64, 16, :])
    P.dma("sp", cos_hi[64:128, :], cos_tm[0:64, 16, :])
    P.dma("sp", sin_hi[0:64, :], sin_tm[0:64, 16, :])
    P.dma("sp", sin_hi[64:128, :], sin_tm[0:64, 16, :])

    PSB = PS.bufs
    SCp = Rot([PSB[0], PSB[1]])
    TRB = PSB[2]
    OBA = [PSB[3], PSB[4]]
    OBB = [PSB[5], PSB[6]]
    OBS = [PSB[3], PSB[4], PSB[5]]
    MISC = PSB[7]
    pbp = Rot([G[3], G[4]])
    ptp = Rot([G[5], G[0]])

    def att_s1(u):
        g = u["grp"]
        nq, c0, k0, kn, pr = g["nq"], g["c0"], u["k0"], u["kn"], u["pr"]
        ps = SCp.get()
        u["bank"] = ps
        qlat = u["qlat"]
        P.mm(ps[0:nq, 0:kn], qlat[:, 0, c0:c0 + nq], kv_fm[:, 0, k0:k0 + kn], start=True, stop=False)
        P.mm(ps[0:nq, 0:kn], qlat[:, 1, c0:c0 + nq], kv_fm[:, 1, k0:k0 + kn], start=False, stop=False)
        P.mm(ps[0:nq, 0:kn], u["qpe"][:, c0:c0 + nq], kpe_fm[pr, k0:k0 + kn], start=False, stop=not u["diag"],
             inc=not u["diag"])
        if u["diag"]:
            P.mm(ps[0:nq, kn - 128:kn], mrow[0:1, 0, :], mrow[0:1, 1, :], start=False, stop=True)

    def att_s2(u):
        g = u["grp"]
        nq, j, kn = g["nq"], u["j"], u["kn"]
        ps = u["bank"]
        mx = g["b1"][0:nq, j:j + 1]
        nb = g["b1"][0:nq, 8 + j:9 + j]
        P.reduce(mx, ps[0:nq, 0:kn], ALU.max)
        P.ts("dve", nb, mx, -SCALE, ALU.mult)
        pb = pbp.get().v.bitcast(BF16)
        u["pb"] = pb
        P.act(pb[0:nq, 0:kn], ps[0:nq, 0:kn], AF.Exp, bias=nb, scale=SCALE, accum=g["b2"][0:nq, j:j + 1])

    def att_s3a(u):
        g = u["grp"]
        nq, j, k0, kn = g["nq"], u["j"], u["k0"], u["kn"]
        pb = u["pb"]
        nkb = (kn + 127) // 128
        pv = TRB.v.bitcast(BF16).rr("p (k q) -> p k q", k=8)
        pt = ptp.get().v.bitcast(BF16)[:, 0:512].rr("p (k q) -> p k q", k=4)
        for kk in range(nkb):
            kc = min(128, kn - kk * 128)
            P.tr(pv[0:kc, kk, 0:nq], pb[0:nq, kk * 128:kk * 128 + kc], ident_b[0:nq, 0:nq], inc=(kk == nkb - 1))
        u["pv"] = pv
        u["pt"] = pt

    def att_s3e(u):
        g = u["grp"]
        nq, kn = g["nq"], u["kn"]
        nkb = (kn + 127) // 128
        pv, pt = u["pv"], u["pt"]
        full = kn // 128
        if full > 0:
            P.copy("dve", pt[:, 0:full, 0:nq], pv[:, 0:full, 0:nq])
        if full < nkb:
            kc = kn - full * 128
            P.copy("dve", pt[0:kc, full, 0:nq], pv[0:kc, full, 0:nq])
        u["pt"] = pt

    def att_s3b(u):
        g = u["grp"]
        nq, j, k0, kn = g["nq"], u["j"], u["k0"], u["kn"]
        nkb = (kn + 127) // 128
        pt = u["pt"]
        ob = g["ob"][j // 2]
        oc = (j % 2) * 256
        for kk in range(nkb):
            kc = min(128, kn - kk * 128)
            kb = k0 // 128 + kk
            P.mm(ob[0:nq, oc:oc + KVR], pt[0:kc, kk, 0:nq], kv_tm[0:kc, kb, :], start=(kk == 0), stop=(kk == nkb - 1))

    def att_combine_a(g):
        nq, nsb, c0, h = g["nq"], g["nsb"], g["c0"], g["h"]
        b1, b2, b3 = g["b1"], g["b2"], g["b3"]
        on = tmpp.get().v.bitcast(BF16)
        rl = b3[0:nq, 40:41]
        if nsb == 1:
            P.recip(rl, b2[0:nq, 0:1])
            P.act(on[0:nq, 0:KVR], g["ob"][0][0:nq, 0:KVR], AF.Identity, scale=rl)
        else:
            m = b3[0:nq, 41:42]
            P.reduce(m, b1[0:nq, 0:nsb], ALU.max)
            P.ts("dve", m, m, -SCALE, ALU.mult)
            ws = b3[0:nq, 0:nsb]
            P.act(ws, b1[0:nq, 0:nsb], AF.Exp, bias=m, scale=SCALE)
            lw = b3[0:nq, 16:16 + nsb]
            P.tt("dve", lw, ws, b2[0:nq, 0:nsb], ALU.mult)
            P.reduce(rl, lw, ALU.add)
            P.recip(rl, rl)
            acc = tmpp.get()
            for j in range(nsb):
                oj = g["ob"][j // 2][0:nq, (j % 2) * 256:(j % 2) * 256 + KVR]
                if j == 0:
                    P.ts("dve", acc[0:nq, 0:KVR], oj, ws[:, 0:1], ALU.mult)
                else:
                    P.stt(acc[0:nq, 0:KVR], oj, ws[:, j:j + 1], acc[0:nq, 0:KVR], ALU.mult, ALU.add)
            P.act(on[0:nq, 0:KVR], acc[0:nq, 0:KVR], AF.Identity, scale=rl)
        g["on"] = on

    def att_combine_b(g, wuv):
        nq, c0, h = g["nq"], g["c0"], g["h"]
        on = g["on"]
        ps = TRB
        pv = ps.v.bitcast(BF16)[:, 512:768]
        for rc in range(2):
            P.tr(pv[:, rc * 128:rc * 128 + nq], on[0:nq, rc * 128:(rc + 1) * 128], ident_b[0:nq, 0:nq], inc=(rc == 1))
        ol = tmpp.get().v.bitcast(BF16)
        P.copy("dve", ol[:, 0:256].rr("p (c q) -> p c q", c=2)[:, :, 0:nq], pv.rr("p (c q) -> p c q", c=2)[:, :, 0:nq])
        for rc in range(2):
            P.mm(ps[:, 384:384 + nq], wuv[:, rc, h * 128:(h + 1) * 128], ol[:, rc * 128:rc * 128 + nq], start=(rc == 0), stop=(rc == 1))
        P.copy("act", bufA[:, h, c0:c0 + nq], ps[:, 384:384 + nq])

    try:
        phase = (stop_after or {}).get("phase", 99)
        for ti, tile in enumerate(tiles):
            if phase < 2:
                break
            load_x(tile)
            tail_init(tile)
            if phase < 3:
                break
            mamba(tile)
            if phase < 4:
                break
            ffn(tile, 0)
            if phase < 5:
                break
            shared_kv(tile)
            if phase < 6:
                break
            mla(tile)
            if phase < 7:
                break
            ffn(tile, 1)
            tail_out(tile)
            store_y(tile)
    except _Stop:
        pass
    P.finish()
    return P


_CACHE = {}


def _consts():
    k = np.arange(128)
    l = np.arange(64)
    u2 = ((k[:, None] % 64) <= l[None, :]).astype(np.float32)
    delta = np.where((k[:, None] % 64) == (l[None, :] + 1), -NEGBIG, 0.0).astype(np.float32)
    pos = (np.arange(17)[None, :] * 128 + k[:, None]).astype(np.float32)
    jj = np.broadcast_to(np.arange(32, dtype=np.float32)[None, :], (128, 32)).copy()
    mrow = np.zeros((2, 128), np.float32)
    mrow[0, :64] = 1.0
    mrow[1, 64:] = NEGBIG
    return dict(k_ident=np.eye(128, dtype=np.float32), k_u2=u2, k_delta=delta, k_pos=pos, k_j=jj, k_mrow=mrow)


def kernel(x_prompt, x_sample, state_ssm, state_ssm_conv, state_ffn_conv, cache_kv_latent, cache_k_rope,
           norm_mix_pre, norm_mix_post, norm_ffn_pre, norm_ffn_post,
           ssm_w_in, ssm_conv_w, ssm_conv_b, ssm_dt_bias, ssm_a_log, ssm_d, ssm_norm, ssm_w_out,
           kv_norm_in, kv_w_dkv, kv_norm, kv_w_kr, kv_w_uk, kv_w_uv,
           mla_w_dq, mla_q_norm, mla_w_uq, mla_w_o,
           ffn_w_up, ffn_conv_w, ffn_conv_b, ffn_w_down, _stop_after=None):
    f = lambda a: np.ascontiguousarray(np.asarray(a, dtype=np.float32))
    if "prog" not in _CACHE or _stop_after is not None:
        _CACHE["prog"] = build_program(_stop_after)
    prog = _CACHE["prog"]
    shared = dict(
        g_mix_pre=f(norm_mix_pre), g_mix_post=f(norm_mix_post), g_ffn_pre=f(norm_ffn_pre), g_ffn_post=f(norm_ffn_post),
        w_in=f(ssm_w_in)[0], cw_ssm=f(ssm_conv_w)[0], cb_ssm=f(ssm_conv_b)[0], dt_bias=f(ssm_dt_bias)[0],
        a_log=f(ssm_a_log)[0], d_skip=f(ssm_d)[0], g_ssm=f(ssm_norm)[0], w_out=f(ssm_w_out)[0],
        g_kvin=f(kv_norm_in), w_dkv=f(kv_w_dkv), g_kv=f(kv_norm), w_kr=f(kv_w_kr),
        w_uk=f(kv_w_uk).reshape(KVR, MH * NOPE), w_uv=f(kv_w_uv).reshape(KVR, MH * VD),
        w_dq=f(mla_w_dq)[0], g_q=f(mla_q_norm)[0], w_uq=f(mla_w_uq)[0], w_o=f(mla_w_o)[0],
        w_up=f(ffn_w_up), cw_ffn=f(ffn_conv_w), cb_ffn=f(ffn_conv_b), w_dn=f(ffn_w_down),
    )
    shared.update(_consts())
    xp = f(x_prompt)
    xs_ = f(x_sample)
    sst = f(state_ssm)[0]
    ssc = f(state_ssm_conv)[0]
    sfc = f(state_ffn_conv)
    cl = f(cache_kv_latent)
    ck = f(cache_k_rope)
    in_maps = []
    for c in range(8):
        m = dict(shared)
        m["xp"] = xp[NPS * c:NPS * (c + 1)]
        m["xs"] = xs_[NSS * c:NSS * (c + 1)].reshape(NSS * DEC_SEQ, D)
        m["st_ssm"] = sst[NSS * c:NSS * (c + 1)].reshape(NSS, DI, DS)
        m["st_sconv"] = ssc[NSS * c:NSS * (c + 1)]
        m["st_fconv"] = np.ascontiguousarray(sfc[:, NSS * c:NSS * (c + 1)])
        m["c_lat"] = cl[NSS * c:NSS * (c + 1)]
        m["c_kr"] = ck[NSS * c:NSS * (c + 1)]
        in_maps.append(m)
    res = run_bass_kernel_spmd(prog.nc, in_maps, core_ids=list(range(8)))
    R = res.results
    cat = lambda name: [np.asarray(r[name]) for r in R]
    y_prompt = np.concatenate(cat("y_p"), axis=0)
    y_sample = np.concatenate([a.reshape(NSS, DEC_SEQ, D) for a in cat("y_s")], axis=0)
    ossm = cat("o_ssm")
    osc = cat("o_sconv")
    ofc = cat("o_fconv")
    p_ssm = np.concatenate([a[:NPS] for a in ossm], axis=0).reshape(1, 8 * NPS, NH, HD, DS)
    s_ssm = np.concatenate([a[NPS:] for a in ossm], axis=0).reshape(1, 8 * NSS, NH, HD, DS)
    p_sc = np.concatenate([a[:NPS] for a in osc], axis=0)[None]
    s_sc = np.concatenate([a[NPS:] for a in osc], axis=0)[None]
    p_fc = np.concatenate([a[:, :NPS] for a in ofc], axis=1)
    s_fc = np.concatenate([a[:, NPS:] for a in ofc], axis=1)
    p_lat = np.concatenate(cat("o_lat_p"), axis=0)
    p_kr = np.concatenate(cat("o_kr_p"), axis=0)
    s_lat = np.concatenate([a.reshape(NSS, DEC_SEQ, KVR) for a in cat("o_lat_s")], axis=0)
    s_kr = np.concatenate([a.reshape(NSS, DEC_SEQ, ROPE) for a in cat("o_kr_s")], axis=0)
    outs = (y_prompt, y_sample, p_ssm, p_sc, p_fc, p_lat, p_kr, s_ssm, s_sc, s_fc, s_lat, s_kr)
    return tuple(np.ascontiguousarray(o, dtype=np.float32) for o in outs)
```

```python
import math
import numpy as np
import concourse.bass as bass
import concourse.mybir as mybir
from concourse.bass_utils import run_bass_kernel_spmd

F32 = mybir.dt.float32
BF16 = mybir.dt.bfloat16
I32 = mybir.dt.int32
ALU = mybir.AluOpType
AF = mybir.ActivationFunctionType
AX = mybir.AxisListType


class V:
    __slots__ = ("buf", "ap")

    def __init__(self, buf, ap):
        self.buf = buf
        self.ap = ap

    def __getitem__(self, key):
        return V(self.buf, self.ap[key])

    def bitcast(self, dt):
        return V(self.buf, self.ap.bitcast(dt))

    def rr(self, pattern, **kw):
        return V(self.buf, self.ap.rearrange(pattern, **kw))

    def bc(self, shape):
        return V(self.buf, self.ap.broadcast_to(list(shape)))

    def unsq(self, axis):
        return V(self.buf, self.ap.unsqueeze(axis))

    def pbc(self, n):
        return V(self.buf, self.ap.partition_broadcast(n))


class Buf:
    def __init__(self, name, t, tracked=True):
        self.name = name
        self.t = t
        self.tracked = tracked
        self.w = {}
        self.r = {}
        self.dsem = None
        self.psum = False

    def __getitem__(self, key):
        return V(self, self.t[key])

    @property
    def v(self):
        return V(self, self.t.ap())


class Eng:
    def __init__(self, name, handle, sem):
        self.name = name
        self.h = handle
        self.sem = sem
        self.count = 0
        self.items = []
        self.waited = {}


class Prog:
    def __init__(self):
        self.nc = bass.Bass("TRN2", target_bir_lowering=False)
        nc = self.nc
        self.sems = {}
        self.E = {}
        for name, h in (("pe", nc.tensor), ("act", nc.scalar), ("dve", nc.vector),
                        ("pool", nc.gpsimd), ("sp", nc.sync)):
            self.E[name] = Eng(name, h, self._sem("e_" + name))
        self.dma_tot = {}
        self.nbuf = 0

    def _sem(self, name):
        s = self.nc.alloc_semaphore(name)
        self.sems[name] = s
        return s

    def sbuf(self, name, shape, dt):
        return Buf(name, self.nc.alloc_sbuf_tensor(name, list(shape), dt))

    def psum(self, name, shape, dt=F32):
        b = Buf(name, self.nc.alloc_psum_tensor(name, list(shape), dt))
        b.psum = True
        return b

    def dram(self, name, shape, dt, kind="Internal"):
        t = self.nc.dram_tensor(name, list(shape), dt, kind=kind)
        return Buf(name, t, tracked=(kind == "Internal"))

    def _need(self, eng, deps):
        for key, val in deps.items():
            if eng.waited.get(key, 0) >= val:
                continue
            if key == "e_" + eng.name:
                assert val <= eng.count, "same-engine wait on a pending (non-inc) op"
            eng.waited[key] = val
            eng.items.append(("wait", key, val))

    def op(self, en, fn, reads=(), writes=(), inc=True):
        eng = self.E[en]
        me = "e_" + en
        deps = {}

        def add(d, skip=None):
            for k, v in d.items():
                if k != skip and v > deps.get(k, 0):
                    deps[k] = v
        wset = set(id(b) for b in writes)
        for b in reads:
            if b.tracked:
                add(b.w)
                if b.psum:
                    add(b.r, me)
        skip = me if en == "pe" else None
        for b in writes:
            if b.tracked:
                add(b.w, skip)
                add(b.r, skip)
        self._need(eng, deps)
        if inc:
            eng.count += 1
            tok = eng.count
        else:
            tok = eng.count + 1
        eng.items.append(("ins", fn, inc))
        for b in reads:
            if b.tracked and id(b) not in wset:
                b.r[me] = tok
        for b in writes:
            if b.tracked:
                b.w = {me: tok}
                b.r = {}
        return tok

    def dma(self, qn, out, in_):
        eng = self.E[qn]
        src, dst = in_.buf, out.buf
        deps = {}

        def add(d):
            for k, v in d.items():
                if v > deps.get(k, 0):
                    deps[k] = v
        if src.tracked:
            add(src.w)
        if dst.tracked:
            add(dst.w)
            add(dst.r)
        self._need(eng, deps)
        owner = dst if dst.tracked else src
        if owner.dsem is None:
            owner.dsem = "d_%d" % self.nbuf
            self.nbuf += 1
            self._sem(owner.dsem)
            self.dma_tot[owner.dsem] = 0
        key = owner.dsem
        self.dma_tot[key] += 16
        val = self.dma_tot[key]
        eng.items.append(("dma", out.ap, in_.ap, key))
        if src.tracked:
            src.r[key] = val
        if dst.tracked:
            dst.w = {key: val}
            dst.r = {}

    def finish(self):
        sp = self.E["sp"]
        deps = dict(self.dma_tot)
        for n, e in self.E.items():
            if n != "sp" and e.count:
                deps["e_" + n] = e.count
        self._need(sp, deps)
        nc = self.nc
        with nc.allow_non_contiguous_dma(reason="small strided parameter/state loads"), nc.Block() as block:
            for n, deco in (("pe", block.tensor), ("act", block.scalar), ("dve", block.vector),
                            ("pool", block.gpsimd), ("sp", block.sync)):
                eng = self.E[n]

                def body(e, eng=eng):
                    for it in eng.items:
                        if it[0] == "wait":
                            e.wait_ge(self.sems[it[1]], it[2])
                        elif it[0] == "ins":
                            ins = it[1](e)
                            if it[2]:
                                ins.then_inc(eng.sem, 1)
                        else:
                            _, oap, iap, key = it
                            e.dma_start(out=oap, in_=iap).then_inc(self.sems[key], 16)
                deco(body)
        return nc

    def mm(self, out, lhsT, rhs, start=True, stop=True, inc=None):
        if inc is None:
            inc = stop
        return self.op("pe", lambda e: e.matmul(out.ap, lhsT=lhsT.ap, rhs=rhs.ap, start=start, stop=stop),
                       reads=[lhsT.buf, rhs.buf], writes=[out.buf], inc=inc)

    def tr(self, out, in_, ident, inc=True):
        return self.op("pe", lambda e: e.transpose(out.ap, in_.ap, ident.ap),
                       reads=[in_.buf, ident.buf], writes=[out.buf], inc=inc)

    def act(self, out, in_, func, bias=None, scale=None, accum=None):
        reads = [in_.buf]
        writes = [out.buf]
        kw = {}
        if bias is not None:
            if isinstance(bias, V):
                reads.append(bias.buf)
                kw["bias"] = bias.ap
            else:
                kw["bias"] = bias
        if scale is not None:
            if isinstance(scale, V):
                reads.append(scale.buf)
                kw["scale"] = scale.ap
            else:
                kw["scale"] = scale
        if accum is not None:
            writes.append(accum.buf)
            kw["accum_out"] = accum.ap
        return self.op("act", lambda e: e.activation(out.ap, in_.ap, func, **kw), reads=reads, writes=writes)

    def tt(self, en, out, a, b, op):
        return self.op(en, lambda e: e.tensor_tensor(out.ap, a.ap, b.ap, op),
                       reads=[a.buf, b.buf], writes=[out.buf])

    def ts(self, en, out, a, s1, op0, s2=None, op1=None):
        reads = [a.buf]
        s1a = s1.ap if isinstance(s1, V) else s1
        s2a = s2.ap if isinstance(s2, V) else s2
        if isinstance(s1, V):
            reads.append(s1.buf)
        if isinstance(s2, V):
            reads.append(s2.buf)
        kw = {}
        if op1 is not None:
            kw["op1"] = op1
        return self.op(en, lambda e: e.tensor_scalar(out.ap, a.ap, s1a, s2a, op0, **kw),
                       reads=reads, writes=[out.buf])

    def stt(self, out, a, s, b, op0, op1):
        reads = [a.buf, b.buf]
        sa = s.ap if isinstance(s, V) else s
        if isinstance(s, V):
            reads.append(s.buf)
        return self.op("dve", lambda e: e.scalar_tensor_tensor(out.ap, a.ap, sa, b.ap, op0, op1),
                       reads=reads, writes=[out.buf])

    def copy(self, en, out, in_):
        if en == "act":
            return self.op(en, lambda e: e.copy(out.ap, in_.ap), reads=[in_.buf], writes=[out.buf])
        return self.op(en, lambda e: e.tensor_copy(out.ap, in_.ap), reads=[in_.buf], writes=[out.buf])

    def memset(self, en, out, val):
        return self.op(en, lambda e: e.memset(out.ap, val), reads=[], writes=[out.buf])

    def recip(self, out, in_):
        return self.op("dve", lambda e: e.reciprocal(out.ap, in_.ap), reads=[in_.buf], writes=[out.buf])

    def reduce(self, out, in_, op):
        return self.op("dve", lambda e: e.tensor_reduce(out.ap, in_.ap, AX.X, op), reads=[in_.buf], writes=[out.buf])


class Rot:
    def __init__(self, bufs):
        self.bufs = bufs
        self.i = 0

    def get(self):
        b = self.bufs[self.i % len(self.bufs)]
        self.i += 1
        return b


D = 1024
KC = 8
SEQ = 2048
DEC_SEQ = 64
PAST = 2048
DI = 2048
NH = 32
HD = 64
NG = 4
DS = 128
CONVD = 3072
NXC = 24
INP = 5152
DFF = 2816
NFC = 44
MH = 16
QR = 384
KVR = 256
NOPE = 128
ROPE = 64
VD = 128
EPS = 1e-6
THETA = 10000.0
T = 256
ST = T // 128
NCH = T // 64
NPS = 2
NSS = 4
NSEQ = NPS + NSS
NEGBIG = -30000.0
SCALE = (NOPE + ROPE) ** -0.5
NSLOT = 3
SLOT_E = 4096


class _Stop(Exception):
    pass


def build_program(stop_after=None):
    P = Prog()
    nc = P.nc
    stop_tag = (stop_after or {}).get("tag")

    def chk(tag):
        if tag == stop_tag:
            raise _Stop()

    def din(name, shape):
        return P.dram(name, shape, F32, kind="ExternalInput")

    def dout(name, shape):
        return P.dram(name, shape, F32, kind="ExternalOutput")

    xp = din("xp", [NPS, SEQ, D])
    xs = din("xs", [NSS * DEC_SEQ, D])
    st_ssm = din("st_ssm", [NSS, DI, DS])
    st_sconv = din("st_sconv", [NSS, 3, CONVD])
    st_fconv = din("st_fconv", [2, NSS, 2, 2 * DFF])
    c_lat = din("c_lat", [NSS, PAST, KVR])
    c_kr = din("c_kr", [NSS, PAST, ROPE])
    g_mix_pre = din("g_mix_pre", [2, D])
    g_mix_post = din("g_mix_post", [2, D])
    g_ffn_pre = din("g_ffn_pre", [2, D])
    g_ffn_post = din("g_ffn_post", [2, D])
    w_in = din("w_in", [D, INP])
    cw_ssm = din("cw_ssm", [4, CONVD])
    cb_ssm = din("cb_ssm", [CONVD])
    dt_bias = din("dt_bias", [NH])
    a_log = din("a_log", [NH])
    d_skip = din("d_skip", [NH])
    g_ssm = din("g_ssm", [DI])
    w_out = din("w_out", [DI, D])
    g_kvin = din("g_kvin", [D])
    w_dkv = din("w_dkv", [D, KVR])
    g_kv = din("g_kv", [KVR])
    w_kr = din("w_kr", [D, ROPE])
    w_uk = din("w_uk", [KVR, MH * NOPE])
    w_uv = din("w_uv", [KVR, MH * VD])
    w_dq = din("w_dq", [D, QR])
    g_q = din("g_q", [QR])
    w_uq = din("w_uq", [QR, MH * (NOPE + ROPE)])
    w_o = din("w_o", [MH * VD, D])
    w_up = din("w_up", [2, D, 2 * DFF])
    cw_ffn = din("cw_ffn", [2, 3, 2 * DFF])
    cb_ffn = din("cb_ffn", [2, 2 * DFF])
    w_dn = din("w_dn", [2, DFF, D])
    k_ident = din("k_ident", [128, 128])
    k_u2 = din("k_u2", [128, 64])
    k_delta = din("k_delta", [128, 64])
    k_pos = din("k_pos", [128, 17])
    k_j = din("k_j", [128, 32])
    k_mrow = din("k_mrow", [2, 128])

    y_p = dout("y_p", [NPS, SEQ, D])
    y_s = dout("y_s", [NSS * DEC_SEQ, D])
    o_ssm = dout("o_ssm", [NSEQ, DI, DS])
    o_sconv = dout("o_sconv", [NSEQ, 3, CONVD])
    o_fconv = dout("o_fconv", [2, NSEQ, 2, 2 * DFF])
    o_lat_p = dout("o_lat_p", [NPS, SEQ, KVR])
    o_kr_p = dout("o_kr_p", [NPS, SEQ, ROPE])
    o_lat_s = dout("o_lat_s", [NSS * DEC_SEQ, KVR])
    o_kr_s = dout("o_kr_s", [NSS * DEC_SEQ, ROPE])

    class Chunked:
        def __init__(self, bufs):
            self.bufs = bufs

        def __getitem__(self, key):
            p, c, t = key
            return self.bufs[c][p, t]

    xres = Chunked([P.sbuf("xres%d" % i, [128, T], F32) for i in range(KC)])
    mixb = Chunked([P.sbuf("mixb%d" % i, [128, T], F32) for i in range(KC)])
    hn = Chunked([P.sbuf("hn%d" % i, [128, T], BF16) for i in range(KC)])
    bufA = Chunked([P.sbuf("bufA%d" % i, [128, T], BF16) for i in range(NXC)])
    G = [P.sbuf("G%d" % i, [128, 2048], F32) for i in range(8)]
    Hs = P.sbuf("Hs", [128, DI], F32)
    Hb = P.sbuf("Hb", [128, DI], BF16)
    kv_fm = P.sbuf("kv_fm", [128, 2, PAST + 64], BF16)
    kpe_fm = P.sbuf("kpe_fm", [128, PAST + 64], BF16)
    kv_tm = P.sbuf("kv_tm", [128, 17, KVR], BF16)
    ring = [P.sbuf("ring%d" % i, [128, SLOT_E], BF16) for i in range(NSLOT)]
    extp = Rot([P.sbuf("ext%d" % i, [128, T + 16], F32) for i in range(2)])
    caccp = Rot([P.sbuf("cacc%d" % i, [128, T], F32) for i in range(2)])
    sqbp = Rot([P.sbuf("sqb%d" % i, [128, T], BF16) for i in range(2)])
    rstdp = Rot([P.sbuf("rstd%d" % i, [128, T], F32) for i in range(2)])
    tmpp = Rot([P.sbuf("tmp%d" % i, [128, 512], F32) for i in range(6)])
    smallp = Rot([P.sbuf("small%d" % i, [128, 64], F32) for i in range(12)])
    ident_f = P.sbuf("ident_f", [128, 128], F32)
    ident_b = P.sbuf("ident_b", [128, 128], BF16)
    ones_b = P.sbuf("ones_b", [128, 128], BF16)
    ones_f = P.sbuf("ones_f", [128, 128], F32)
    u2 = P.sbuf("u2", [128, 64], F32)
    negu2 = P.sbuf("negu2", [128, 64], F32)
    delta = P.sbuf("delta", [128, 64], F32)
    mrow = P.sbuf("mrow", [1, 2, 128], BF16)
    mrow_f = P.sbuf("mrow_f", [1, 2, 128], F32)
    diagD = P.sbuf("diagD", [128, 16, 128], BF16)
    dcol = P.sbuf("dcol", [128, 16], F32)
    cos_tm = P.sbuf("cos_tm", [128, 17, 32], F32)
    sin_tm = P.sbuf("sin_tm", [128, 17, 32], F32)
    gcols = P.sbuf("gcols", [128, 9, KC], F32)
    gq_col = P.sbuf("gq_col", [128, 3], F32)
    gssm_col = P.sbuf("gssm_col", [128, 16], F32)
    gkv_bc = P.sbuf("gkv_bc", [128, KVR], F32)
    dtb_bc = P.sbuf("dtb_bc", [128, NH], F32)
    a_bc = P.sbuf("a_bc", [128, NH], F32)
    cws = P.sbuf("cws", [128, NXC, 4], F32)
    cbs = P.sbuf("cbs", [128, NXC], F32)
    cwf = P.sbuf("cwf", [128, 2, NFC, 3], F32)
    cbf = P.sbuf("cbf", [128, 2, NFC], F32)
    tail_s = P.sbuf("tail_s", [128, NXC, 4, 3], F32)
    tail_f = P.sbuf("tail_f", [128, 2, NFC, 4, 2], F32)
    dt_tm = P.sbuf("dt_tm", [128, ST, NH], F32)
    PS = Rot([P.psum("ps%d" % i, [128, 512], F32) for i in range(8)])

    P.dma("sp", ident_f.v, k_ident.v)
    P.copy("dve", ident_b.v, ident_f.v)
    P.memset("pool", ones_b.v, 1.0)
    P.memset("pool", ones_f.v, 1.0)
    onesH = P.sbuf("onesH", [128, 2, 128], F32)
    P.memset("pool", onesH.v, 0.0)
    P.memset("pool", onesH[0:64, 0, :], 1.0)
    P.memset("pool", onesH[64:128, 1, :], 1.0)
    P.dma("sp", u2.v, k_u2.v)
    P.ts("dve", negu2.v, u2.v, -1.0, ALU.mult)
    P.dma("sp", delta.v, k_delta.v)
    P.dma("sp", mrow_f.v, k_mrow.v.unsq(0))
    P.copy("dve", mrow.v, mrow_f.v)
    for i, gsrc in enumerate((g_mix_pre, g_mix_post, g_ffn_pre, g_ffn_post)):
        for l in range(2):
            P.dma("sp", gcols[:, 2 * i + l, :], gsrc[l].rr("(c p) -> p c", p=128))
    P.dma("sp", gcols[:, 8, :], g_kvin.v.rr("(c p) -> p c", p=128))
    P.dma("sp", gq_col.v, g_q.v.rr("(c p) -> p c", p=128))
    P.dma("sp", gssm_col.v, g_ssm.v.rr("(c p) -> p c", p=128))
    P.dma("sp", gkv_bc.v, g_kv.v.pbc(128))
    P.dma("sp", dtb_bc.v, dt_bias.v.pbc(128))
    P.dma("sp", a_bc.v, a_log.v.pbc(128))
    P.act(a_bc.v, a_bc.v, AF.Exp)
    P.ts("dve", a_bc.v, a_bc.v, -1.0, ALU.mult)
    for i in range(4):
        P.dma("sp", cws[:, :, i], cw_ssm[i].rr("(c p) -> p c", p=128))
    P.dma("sp", cbs.v, cb_ssm.v.rr("(c p) -> p c", p=128))
    for l in range(2):
        for i in range(3):
            P.dma("sp", cwf[:, l, :, i], cw_ffn[l, i].rr("(c p) -> p c", p=128))
        P.dma("sp", cbf[:, l], cb_ffn[l].rr("(c p) -> p c", p=128))
    for hf in range(2):
        P.dma("sp", dcol[hf * 64:(hf + 1) * 64, :], d_skip.v[hf::2].pbc(64))
    for j in range(16):
        P.ts("dve", diagD[:, j, :], ident_f.v, dcol[:, j:j + 1], ALU.mult)
    posb = P.sbuf("posb", [128, 17], F32)
    invb = P.sbuf("invb", [128, 32], F32)
    angb = V(G[2], G[2].t[:, 0:544].rearrange("p (b j) -> p b j", b=17))
    angi = V(G[3], G[3].t[:, 0:544].bitcast(I32).rearrange("p (b j) -> p b j", b=17))
    angk = V(G[4], G[4].t[:, 0:544].rearrange("p (b j) -> p b j", b=17))
    P.dma("sp", posb.v, k_pos.v)
    P.dma("sp", invb.v, k_j.v)
    P.act(invb.v, invb.v, AF.Exp, scale=-math.log(THETA) / 32.0)
    for b in range(17):
        P.ts("dve", angb[:, b, :], invb.v, posb[:, b:b + 1], ALU.mult)
    for tab, shift in ((sin_tm, 0.0), (cos_tm, math.pi / 2)):
        P.ts("dve", angk, angb, shift, ALU.add, 1.0 / (2 * math.pi), ALU.mult)
        P.copy("dve", angi, angk)
        P.copy("dve", angk, angi)
        P.stt(angk, angk, -2 * math.pi, angb, ALU.mult, ALU.add)
        if shift != 0.0:
            P.ts("dve", angk, angk, shift, ALU.add)
        P.ts("dve", angk, angk, -3.1415925, ALU.max, 3.1415925, ALU.min)
        P.act(tab.v, angk, AF.Sin)

    wsc = {}
    wshape = {}

    def wdef(name, nblk, kc, m):
        wsc[name] = P.dram("wsc_" + name, [nblk, 128, kc * m], BF16)
        wshape[name] = (nblk, kc, m)

    wdef("wz", 4, 8, 512)
    wdef("wx", 6, 8, 512)
    wdef("wdt", 1, 8, 32)
    wdef("wout", 4, 16, 256)
    for l in range(2):
        wdef("wup%d" % l, 11, 8, 512)
        wdef("wdn%d" % l, 8, 22, 128)
    wdef("wkv", 1, 8, 320)
    wdef("wdq", 1, 8, 384)
    wdef("wuqn", 4, 3, 512)
    wdef("wuqr", 1, 3, 1024)
    wdef("wo", 4, 16, 256)
    wdef("wuv", 1, 2, 2048)

    stgp = Rot([G[0], G[1], G[2], G[3], G[4], G[5]])
    cast_eng = Rot(["dve", "pool", "act"])
    ring_i = [0]

    def kview(w, c0, c1):
        return w.rr("(c p) m -> p c m", p=128)[:, :, c0:c1]

    def prep(name, j, parts, gain=None):
        nblk, kc, m = wshape[name]
        slot = ring[ring_i[0] % NSLOT]
        ring_i[0] += 1
        bv = slot[:, 0:kc * m].rr("p (c m) -> p c m", c=kc)
        cper = max(1, 2048 // m)
        for c0 in range(0, kc, cper):
            c1 = min(kc, c0 + cper)
            stg = stgp.get()
            sv = stg[:, 0:(c1 - c0) * m].rr("p (c m) -> p c m", c=c1 - c0)
            for src, off, mi in parts:
                P.dma("sp", sv[:, :, off:off + mi], src[:, c0:c1, :])
            if gain is None:
                P.copy(cast_eng.get(), bv[:, c0:c1, :], sv)
            else:
                for c in range(c0, c1):
                    P.ts("dve" if c % 2 else "pool", bv[:, c, :], sv[:, c - c0, :], gain[:, c:c + 1], ALU.mult)
        P.dma("sp", wsc[name][j], slot[:, 0:kc * m])

    w_in_v = w_in.v
    _skip_prologue = (stop_after or {}).get("phase", 99) < 1
    for j in range(4):
        prep("wz", j, [(kview(w_in_v, j * 512, (j + 1) * 512), 0, 512)])
    prep("wdt", 0, [(kview(w_in_v, 5120, 5152), 0, 32)])
    for j in range(6):
        prep("wx", j, [(kview(w_in_v, 2048 + j * 512, 2048 + (j + 1) * 512), 0, 512)])
    for j in range(4):
        prep("wout", j, [(kview(w_out.v, j * 256, (j + 1) * 256), 0, 256)], gain=gssm_col.v)
    prep("wkv", 0, [(kview(w_dkv.v, 0, 256), 0, 256), (kview(w_kr.v, 0, 64), 256, 64)])
    for l in range(2):
        for j in range(11):
            prep("wup%d" % l, j, [(kview(w_up[l], j * 256, (j + 1) * 256), 0, 256),
                                  (kview(w_up[l], DFF + j * 256, DFF + (j + 1) * 256), 256, 256)])
        for j in range(8):
            prep("wdn%d" % l, j, [(kview(w_dn[l], j * 128, (j + 1) * 128), 0, 128)])
    prep("wdq", 0, [(kview(w_dq.v, 0, 384), 0, 384)])
    uq4 = w_uq.v.rr("(c p) (h e) -> p c h e", p=128, e=192)
    for j in range(4):
        stg = stgp.get()
        sv = stg[:, 0:1536].rr("p (c h e) -> p c h e", c=3, h=4)
        for c in range(3):
            P.dma("sp", sv[:, c], uq4[:, c, 4 * j:4 * j + 4, 0:128])
        slot = ring[ring_i[0] % NSLOT]
        ring_i[0] += 1
        P.copy(cast_eng.get(), slot[:, 0:1536], stg[:, 0:1536])
        P.dma("sp", wsc["wuqn"][j], slot[:, 0:1536])
    slot = ring[ring_i[0] % NSLOT]
    ring_i[0] += 1
    for c in range(3):
        stg = stgp.get()
        sv = stg[:, 0:1024].rr("p (h e) -> p h e", h=16)
        P.dma("sp", sv, uq4[:, c, :, 128:192])
        P.copy(cast_eng.get(), slot[:, c * 1024:(c + 1) * 1024], stg[:, 0:1024])
    P.dma("sp", wsc["wuqr"][0], slot[:, 0:3072])
    for j in range(4):
        prep("wo", j, [(kview(w_o.v, j * 256, (j + 1) * 256), 0, 256)])
    stg = stgp.get()
    stg2 = stgp.get()
    P.dma("sp", stg.v, w_uv.v[0:128, :])
    P.dma("sp", stg2.v, w_uv.v[128:256, :])
    slot = ring[ring_i[0] % NSLOT]
    ring_i[0] += 1
    P.copy("dve", slot[:, 0:2048], stg.v)
    P.copy("pool", slot[:, 2048:4096], stg2.v)
    P.dma("sp", wsc["wuv"][0], slot.v)
    stg = stgp.get()
    stg2 = stgp.get()
    P.dma("sp", stg.v, w_uk.v[0:128, :])
    P.dma("sp", stg2.v, w_uk.v[128:256, :])
    wukT_sb = P.sbuf("wukT_sb", [128, 16, 256], BF16)
    sl4 = wukT_sb.v
    for rc, sg in enumerate((stg, stg2)):
        for h4 in range(4):
            ps = PS.get()
            for hh in range(4):
                h = h4 * 4 + hh
                P.tr(ps[:, hh * 128:(hh + 1) * 128], sg[:, h * 128:(h + 1) * 128], ident_f.v, inc=(hh == 3))
            P.copy("act" if h4 % 2 else "dve", sl4[:, h4 * 4:h4 * 4 + 4, rc * 128:(rc + 1) * 128],
                   ps.v.rr("p (h r) -> p h r", h=4))

    class WStream:
        def __init__(self):
            self.order = []
            self.pos = 0
            self.issued = 0
            self.slot_of = {}

        def plan(self, lst):
            self.order = lst

        def _issue(self):
            name, j = self.order[self.issued]
            nblk, kc, m = wshape[name]
            slot = ring[(ring_i[0] + self.issued) % NSLOT]
            P.dma("sp", slot[:, 0:kc * m], wsc[name][j])
            self.slot_of[self.issued] = slot
            self.issued += 1

        def get(self, name, j):
            assert self.order[self.pos] == (name, j), (self.order[self.pos], name, j)
            while self.issued < min(len(self.order), self.pos + NSLOT):
                self._issue()
            slot = self.slot_of.pop(self.pos)
            self.pos += 1
            nblk, kc, m = wshape[name]
            return slot[:, 0:kc * m].rr("p (c m) -> p c m", c=kc)

    WS = WStream()

    def pass_blocks(kind):
        lst = [("wz", j) for j in range(4)] + [("wdt", 0)] + [("wx", j) for j in range(6)]
        lst += [("wout", j) for j in range(4)]
        lst += [("wup0", j) for j in range(11)] + [("wdn0", j) for j in range(8)]
        lst += [("wkv", 0), ("wdq", 0), ("wuqr", 0)]
        lst += [("wuqn", j) for j in range(4)] * (NSS if kind == "s" else 1)
        lst += [("wo", j) for j in range(4)]
        lst += [("wup1", j) for j in range(11)] + [("wdn1", j) for j in range(8)]
        return lst

    tiles = []
    for s in range(NPS):
        for t0 in range(0, SEQ, T):
            tiles.append(dict(kind="p", segs=[(s, t0, T)]))
    tiles.append(dict(kind="s", segs=[(NPS + i, 0, DEC_SEQ) for i in range(NSS)]))
    if stop_after is not None and stop_after.get("ntiles"):
        tiles = tiles[:stop_after["ntiles"]]
    if stop_after is not None and stop_after.get("tiles"):
        tiles = [tiles[i] for i in stop_after["tiles"]]
    order = []
    for tl in tiles:
        order += pass_blocks(tl["kind"])
    WS.plan(order)

    last_rs = [None]

    def norm_to_hn(src, gidx, nk=KC, dim=D, out=None, gcol=None, reuse_rs=False):
        out = hn if out is None else out
        if reuse_rs:
            rs = last_rs[0]
        else:
            ps = PS.get()
            for kc in range(nk):
                sq = sqbp.get()
                P.act(sq.v, src[:, kc, :], AF.Square)
                P.mm(ps[:, 0:T], ones_b.v, sq.v, start=(kc == 0), stop=(kc == nk - 1), inc=True)
            rs = rstdp.get()
            P.act(rs.v, ps[:, 0:T], AF.Sqrt, bias=EPS, scale=1.0 / dim)
            P.recip(rs.v, rs.v)
            last_rs[0] = rs
        for kc in range(nk):
            g = gcols[:, gidx, kc:kc + 1] if gcol is None else gcol[:, kc:kc + 1]
            P.stt(out[:, kc, :], src[:, kc, :], g, rs.v, ALU.mult, ALU.mult)

    def postnorm_add(gidx):
        ps = PS.get()
        for kc in range(KC):
            sq = sqbp.get()
            P.act(sq.v, mixb[:, kc, :], AF.Square)
            P.mm(ps[:, 0:T], ones_b.v, sq.v, start=(kc == 0), stop=(kc == KC - 1), inc=True)
        rs = rstdp.get()
        P.act(rs.v, ps[:, 0:T], AF.Sqrt, bias=EPS, scale=1.0 / D)
        P.recip(rs.v, rs.v)
        for kc in range(KC):
            P.tt("pool" if kc % 2 else "dve", mixb[:, kc, :], mixb[:, kc, :], rs.v, ALU.mult)
            P.stt(xres[:, kc, :], mixb[:, kc, :], gcols[:, gidx, kc:kc + 1], xres[:, kc, :], ALU.mult, ALU.add)

    def conv_chunk(ps, K, wv, bv, tail, segs, acc=None):
        nseg = len(segs)
        L = segs[0][2]
        ext = extp.get()
        ev = ext[:, 0:nseg * (L + K - 1)].rr("p (s l) -> p s l", s=nseg)
        P.copy("pool", ev[:, :, 0:K - 1], tail[:, 0:nseg, :])
        P.copy("act", ev[:, :, K - 1:], ps[:, 0:T].rr("p (s l) -> p s l", s=nseg))
        if acc is None:
            acc = caccp.get()
        av = acc[:, 0:T].rr("p (s l) -> p s l", s=nseg)
        P.act(av, ev[:, :, 0:L], AF.Identity, bias=bv, scale=wv[:, 0:1])
        for i in range(1, K):
            P.stt(av, ev[:, :, i:i + L], wv[:, i:i + 1], av, ALU.mult, ALU.add)
        P.copy("pool", tail[:, 0:nseg, :], ev[:, :, L:L + K - 1])
        return acc

    def tail_init(tile):
        segs = tile["segs"]
        if tile["kind"] == "p":
            if segs[0][1] == 0:
                P.memset("pool", tail_s.v, 0.0)
                P.memset("pool", tail_f.v, 0.0)
        else:
            for i, (seq, t0, L) in enumerate(segs):
                for r in range(3):
                    P.dma("sp", tail_s[:, :, i, r], st_sconv[seq - NPS, r].rr("(c p) -> p c", p=128))
                for l in range(2):
                    for r in range(2):
                        P.dma("sp", tail_f[:, l, :, i, r], st_fconv[l, seq - NPS, r].rr("(c p) -> p c", p=128))

    def tail_out(tile):
        segs = tile["segs"]
        for i, (seq, t0, L) in enumerate(segs):
            if tile["kind"] == "p" and t0 + L < SEQ:
                continue
            for r in range(3):
                P.dma("sp", o_sconv[seq, r].rr("(c p) -> p c", p=128), tail_s[:, :, i, r])
            for l in range(2):
                for r in range(2):
                    P.dma("sp", o_fconv[l, seq, r].rr("(c p) -> p c", p=128), tail_f[:, l, :, i, r])

    x_prefetched = set()

    def fetch_x(tile):
        stage = G[0].v[:, 0:ST * D].rr("p (s d) -> p s d", s=ST)
        if tile["kind"] == "p":
            seq, t0, L = tile["segs"][0]
            P.dma("sp", stage, xp[seq, t0:t0 + T, :].rr("(s p) d -> p s d", p=128))
        else:
            P.dma("sp", stage, xs.v.rr("(s p) d -> p s d", p=128))
        x_prefetched.add(id(tile))

    def load_x(tile):
        stage = G[0].v[:, 0:ST * D].rr("p (s d) -> p s d", s=ST)
        if id(tile) not in x_prefetched:
            fetch_x(tile)
        for kc in range(KC):
            ps = PS.get()
            for st in range(ST):
                P.tr(ps[:, st * 128:(st + 1) * 128], stage[:, st, kc * 128:(kc + 1) * 128], ident_f.v,
                     inc=(st == ST - 1))
            P.copy("act" if kc % 2 else "dve", xres[:, kc, :], ps[:, 0:T])

    def store_y(tile):
        stage = G[1].v[:, 0:ST * D].rr("p (s d) -> p s d", s=ST)
        for st in range(ST):
            for k4 in range(2):
                ps = PS.get()
                for kk in range(4):
                    kc = k4 * 4 + kk
                    P.tr(ps[:, kk * 128:(kk + 1) * 128], xres[:, kc, st * 128:(st + 1) * 128], ident_f.v,
                         inc=(kk == 3))
                P.copy("act" if k4 % 2 else "dve", stage[:, st, k4 * 512:(k4 + 1) * 512], ps.v)
        if tile["kind"] == "p":
            seq, t0, L = tile["segs"][0]
            P.dma("sp", y_p[seq, t0:t0 + T, :].rr("(s p) d -> p s d", p=128), stage)
        else:
            P.dma("sp", y_s.v.rr("(s p) d -> p s d", p=128), stage)

    zs_tm = G[0].v.bitcast(BF16).rr("p (s c) -> p s c", s=ST)
    yg_fm = G[1].v.bitcast(BF16).rr("p (c t) -> p c t", c=16)

    def state_load(seq):
        if seq < NPS:
            P.memset("pool", Hs.v, 0.0)
            P.memset("pool", Hb.v, 0.0)
            return
        stg = G[2]
        sv = stg.v.rr("p (j n) -> p j n", j=16)
        P.dma("sp", sv, st_ssm[seq - NPS].rr("(j p) n -> p j n", p=128))
        for j4 in range(4):
            ps = PS.get()
            for jj in range(4):
                j = j4 * 4 + jj
                P.tr(ps[:, jj * 128:(jj + 1) * 128], sv[:, j, :], ident_f.v, inc=(jj == 3))
            P.copy("act" if j4 % 2 else "dve", Hs[:, j4 * 512:(j4 + 1) * 512], ps.v)
        P.copy("pool", Hb.v, Hs.v)

    def state_store(seq):
        stg = G[2]
        sv = stg.v.rr("p (j n) -> p j n", j=16)
        for j4 in range(4):
            ps = PS.get()
            for jj in range(4):
                j = j4 * 4 + jj
                P.tr(ps[:, jj * 128:(jj + 1) * 128], Hs[:, j * 128:(j + 1) * 128], ident_f.v, inc=(jj == 3))
            P.copy("act" if j4 % 2 else "dve", sv[:, j4 * 4:j4 * 4 + 4, :], ps.v.rr("p (j n) -> p j n", j=4))
        P.dma("sp", o_ssm[seq].rr("(j p) n -> p j n", p=128), sv)

    def mamba(tile):
        segs = tile["segs"]
        norm_to_hn(xres, 0)
        for zb in range(4):
            wv = WS.get("wz", zb)
            for st in range(ST):
                ps = PS.get()
                for kc in range(KC):
                    P.mm(ps.v, hn[:, kc, st * 128:(st + 1) * 128], wv[:, kc, :], start=(kc == 0), stop=(kc == KC - 1))
                P.act(zs_tm[:, st, zb * 512:(zb + 1) * 512], ps.v, AF.Silu)
        chk("m_z")
        wv = WS.get("wdt", 0)
        for st in range(ST):
            ps = PS.get()
            for kc in range(KC):
                P.mm(ps[:, 0:NH], hn[:, kc, st * 128:(st + 1) * 128], wv[:, kc, :], start=(kc == 0), stop=(kc == KC - 1))
            xr = smallp.get()[:, 0:NH]
            ab = smallp.get()[:, 0:NH]
            P.tt("dve", xr, ps[:, 0:NH], dtb_bc.v, ALU.add)
            P.act(ab, xr, AF.Abs)
            P.act(ab, ab, AF.Exp, scale=-1.0)
            P.act(ab, ab, AF.Ln, bias=1.0)
            P.stt(dt_tm[:, st, :], xr, 0.0, ab, ALU.max, ALU.add)
        chk("m_dt")
        pend_silu = []
        for cb in range(6):
            wv = WS.get("wx", cb)
            for mc in range(4):
                ch = cb * 4 + mc
                ps = PS.get()
                for kc in range(KC):
                    P.mm(ps[:, 0:T], wv[:, kc, mc * 128:(mc + 1) * 128], hn[:, kc, :], start=(kc == 0), stop=(kc == KC - 1))
                acc = conv_chunk(ps, 4, cws[:, ch, :], cbs[:, ch:ch + 1], tail_s[:, ch], segs, None)
                if pend_silu:
                    pa, pch = pend_silu.pop()
                    P.act(bufA[:, pch, :], pa.v, AF.Silu)
                pend_silu.append((acc, ch))
        if pend_silu:
            pa, pch = pend_silu.pop()
            P.act(bufA[:, pch, :], pa.v, AF.Silu)
        chk("m_x")
        for st in range(ST):
            tk = slice(st * 128, (st + 1) * 128)
            x_tm = G[7].v.bitcast(BF16)[:, 0:2048]
            B_tm = G[7].v.bitcast(BF16)[:, 2048:2560]
            xdt = G[6].v.bitcast(BF16)[:, 0:2048]
            xw = G[6].v.bitcast(BF16)[:, 2048:4096]
            dec = G[5].v.bitcast(BF16)[:, 0:2048]
            wts = G[5].v.bitcast(BF16)[:, 2048:4096]
            Rm = G[2].v
            DAB = G[3].v
            y_sb = G[4].v
            for j8 in range(2):
                ps = PS.get()
                pv = ps.v.bitcast(BF16)
                for jj in range(8):
                    j = j8 * 8 + jj
                    P.tr(pv[:, jj * 128:(jj + 1) * 128], bufA[:, j, tk], ident_b.v, inc=(jj == 7))
                P.copy("act" if j8 else "dve", x_tm[:, j8 * 1024:(j8 + 1) * 1024], pv)
            ps = PS.get()
            pv = ps.v.bitcast(BF16)
            for g in range(NG):
                P.tr(pv[:, g * 128:(g + 1) * 128], bufA[:, 16 + g, tk], ident_b.v, inc=(g == NG - 1))
            P.copy("act", B_tm, pv[:, 0:512])
            chk("m_tr")
            da = smallp.get()[:, 0:NH]
            P.tt("dve", da, dt_tm[:, st, :], a_bc.v, ALU.mult)
            P.tt("pool", xdt.rr("p (h e) -> p h e", h=NH), x_tm.rr("p (h e) -> p h e", h=NH),
                 dt_tm[:, st, :].unsq(2).bc([128, NH, HD]), ALU.mult)
            dab3 = da.unsq(2).bc([128, NH, 64])
            P.tt("dve", Rm.rr("p (h l) -> p h l", h=NH), dab3, u2.v.unsq(1).bc([128, NH, 64]), ALU.mult)
            P.tt("pool", DAB.rr("p (h l) -> p h l", h=NH), dab3, delta.v.unsq(1).bc([128, NH, 64]), ALU.add)
            chk("m_rd")
            for g in range(NG):
                ps = PS.get()
                for hf in range(2):
                    rows = slice(hf * 64, (hf + 1) * 64)
                    P.mm(ps[rows, :], ones_f[rows, 0:64], Rm[rows, g * 512:(g + 1) * 512], start=True, stop=False)
                    P.mm(ps[rows, :], negu2[rows, :], DAB[rows, g * 512:(g + 1) * 512], start=False, stop=True,
                         inc=(hf == 1))
                P.act(dec[:, g * 512:(g + 1) * 512], ps.v, AF.Exp)
            P.tt("pool", xw.rr("p (h e) -> p h e", h=NH), xdt.rr("p (h e) -> p h e", h=NH),
                 dec.rr("p (h l) -> p h l", h=NH)[:, :, 63:64].bc([128, NH, HD]), ALU.mult)
            chk("m_seg")
            ps = PS.get()
            for hf in range(2):
                rows = slice(hf * 64, (hf + 1) * 64)
                P.mm(ps[rows, 0:NH], u2[rows, :], da[rows, :], inc=False)
            for hf in range(2):
                rows = slice(hf * 64, (hf + 1) * 64)
                P.mm(ps[:, 64 + hf * NH:64 + (hf + 1) * NH], onesH[:, hf, :], da, inc=(hf == 1))
            eac = smallp.get()
            P.act(eac[:, 0:NH], ps[:, 0:NH], AF.Exp)
            cd = smallp.get()
            P.act(cd.v, ps[:, 64:128], AF.Exp)
            chk("m_eac")
            psc = PS.get()
            for hf in range(2):
                rows = slice(hf * 64, (hf + 1) * 64)
                tok = slice(st * 128 + hf * 64, st * 128 + (hf + 1) * 64)
                for g in range(NG):
                    P.mm(psc[rows, g * 64:(g + 1) * 64], bufA[:, 16 + g, tok], bufA[:, 20 + g, tok],
                         inc=(hf == 1 and g == NG - 1))
            P.tt("dve", wts.rr("p (g e l) -> p g e l", g=NG, e=8), dec.rr("p (g e l) -> p g e l", g=NG, e=8),
                 psc[:, 0:256].rr("p (g l) -> p g l", g=NG).unsq(2).bc([128, NG, 8, 64]), ALU.mult)
            chk("m_cb")
            for hf in range(2):
                c = st * 2 + hf
                rows = slice(hf * 64, (hf + 1) * 64)
                tok = slice(c * 64, (c + 1) * 64)
                if tile["kind"] == "p":
                    seq, t0, L = segs[0]
                    first = (t0 == 0 and c == 0)
                    last = (t0 + T == SEQ and c == NCH - 1)
                else:
                    seq = segs[c][0]
                    first = last = True
                if first:
                    state_load(seq)
                for g in range(NG):
                    pso = PS.get()
                    P.mm(pso[rows, :], bufA[:, 20 + g, tok], Hb[:, g * 512:(g + 1) * 512])
                    psd = PS.get()
                    for e in range(8):
                        h = g * 8 + e
                        P.mm(psd[rows, e * 64:(e + 1) * 64], wts[rows, h * 64:(h + 1) * 64], xdt[rows, h * 64:(h + 1) * 64],
                             start=(e == 0), stop=False, inc=False)
                    for jj in range(4):
                        j = g * 4 + jj
                        P.mm(psd[rows, jj * 128:(jj + 1) * 128], bufA[:, j, tok], diagD[:, j, :], start=False,
                             stop=(jj == 3), inc=(jj == 3))
                    tmp = tmpp.get()
                    P.tt("dve", tmp[rows, :].rr("p (e d) -> p e d", e=8), pso[rows, :].rr("p (e d) -> p e d", e=8),
                         eac[rows, g * 8:(g + 1) * 8].unsq(2).bc([64, 8, HD]), ALU.mult)
                    P.tt("dve", y_sb[rows, g * 512:(g + 1) * 512], psd[rows, :], tmp[rows, :], ALU.add)
                hh2 = NH // 2
                P.tt("pool", Hs[:, 0:1024].rr("p (h e) -> p h e", h=hh2), Hs[:, 0:1024].rr("p (h e) -> p h e", h=hh2),
                     cd[:, hf * NH:hf * NH + hh2].unsq(2).bc([128, hh2, HD]), ALU.mult)
                P.tt("dve", Hs[:, 1024:2048].rr("p (h e) -> p h e", h=hh2), Hs[:, 1024:2048].rr("p (h e) -> p h e", h=hh2),
                     cd[:, hf * NH + hh2:(hf + 1) * NH].unsq(2).bc([128, hh2, HD]), ALU.mult)
                for g in range(NG):
                    pss = PS.get()
                    P.mm(pss.v, B_tm[rows, g * 128:(g + 1) * 128], xw[rows, g * 512:(g + 1) * 512])
                    P.tt("dve", Hs[:, g * 512:(g + 1) * 512], Hs[:, g * 512:(g + 1) * 512], pss.v, ALU.add)
                P.copy("act", Hb.v, Hs.v)
                if last:
                    state_store(seq)
            chk("m_chunk")
            P.tt("dve", y_sb, y_sb, zs_tm[:, st, :], ALU.mult)
            ssq = smallp.get()
            for g in range(NG):
                junk = tmpp.get()
                P.act(junk.v, y_sb[:, g * 512:(g + 1) * 512], AF.Square, accum=ssq[:, g:g + 1])
            P.act(ssq[:, 0:NG], ssq[:, 0:NG], AF.Sqrt, bias=EPS, scale=1.0 / 512)
            P.recip(ssq[:, 0:NG], ssq[:, 0:NG])
            ygn = G[2].v.bitcast(BF16)[:, 0:2048]
            for g in range(NG):
                P.act(ygn[:, g * 512:(g + 1) * 512], y_sb[:, g * 512:(g + 1) * 512], AF.Identity, scale=ssq[:, g:g + 1])
            for j8 in range(2):
                ps = PS.get()
                pv = ps.v.bitcast(BF16)
                for jj in range(8):
                    j = j8 * 8 + jj
                    P.tr(pv[:, jj * 128:(jj + 1) * 128], ygn[:, j * 128:(j + 1) * 128], ident_b.v, inc=(jj == 7))
                P.copy("act" if j8 else "dve", yg_fm[:, j8 * 8:(j8 + 1) * 8, tk], pv.rr("p (j t) -> p j t", j=8))
        chk("m_gate")
        for mb in range(4):
            wv = WS.get("wout", mb)
            for mc in range(2):
                ps = PS.get()
                for kc in range(16):
                    P.mm(ps[:, 0:T], wv[:, kc, mc * 128:(mc + 1) * 128], yg_fm[:, kc, :], start=(kc == 0), stop=(kc == 15))
                P.copy("act", mixb[:, mb * 2 + mc, :], ps[:, 0:T])
        postnorm_add(2)

    def ffn(tile, l):
        segs = tile["segs"]
        norm_to_hn(xres, 4 + l)
        pend_gate = []
        for blk in range(11):
            wv = WS.get("wup%d" % l, blk)
            accs = []
            for q in range(4):
                ch = (blk * 2 + q) if q < 2 else (22 + blk * 2 + q - 2)
                ps = PS.get()
                for kc in range(KC):
                    P.mm(ps[:, 0:T], wv[:, kc, q * 128:(q + 1) * 128], hn[:, kc, :], start=(kc == 0), stop=(kc == KC - 1))
                acc = conv_chunk(ps, 3, cwf[:, l, ch, :], cbf[:, l, ch:ch + 1], tail_f[:, l, ch], segs,
                                 acc=(tmpp.get() if q < 2 else None))
                while pend_gate:
                    ga, gv, gidx = pend_gate.pop(0)
                    P.act(ga.v, ga.v, AF.Gelu_apprx_tanh)
                    P.tt("dve", bufA[:, gidx, :], ga[:, 0:T], gv[:, 0:T], ALU.mult)
                if q < 2:
                    accs.append(acc)
                else:
                    pend_gate.append((acc, accs[q - 2], blk * 2 + q - 2))
        while pend_gate:
            ga, gv, gidx = pend_gate.pop(0)
            P.act(ga.v, ga.v, AF.Gelu_apprx_tanh)
            P.tt("dve", bufA[:, gidx, :], ga[:, 0:T], gv[:, 0:T], ALU.mult)
        chk("f_up")
        for mb in range(8):
            wv = WS.get("wdn%d" % l, mb)
            ps = PS.get()
            for kc in range(22):
                P.mm(ps[:, 0:T], wv[:, kc, :], bufA[:, kc, :], start=(kc == 0), stop=(kc == 21))
            P.copy("act", mixb[:, mb, :], ps[:, 0:T])
        chk("f_dn")
        postnorm_add(6 + l)

    def rowgroups(tile):
        if tile["kind"] == "p":
            seq, t0, L = tile["segs"][0]
            return [(st * 128, 128, seq, t0 + st * 128) for st in range(ST)]
        return [(i * 64, 64, seq, PAST) for i, (seq, t0, L) in enumerate(tile["segs"])]

    def cache_load(seq):
        s = seq - NPS
        stg = G[2]
        for half in range(2):
            sv = stg.v.rr("p (k r) -> p k r", k=8)
            P.dma("sp", sv, c_lat[s, half * 1024:(half + 1) * 1024, :].rr("(k p) r -> p k r", p=128))
            P.copy("pool", kv_tm[:, half * 8:(half + 1) * 8, :], sv)
        sk = G[3].v[:, 0:1024].rr("p (k r) -> p k r", k=16)
        P.dma("sp", sk, c_kr[s].rr("(k p) r -> p k r", p=128))
        kb2 = G[3].v.bitcast(BF16)[:, 2048:4096].rr("p (k r) -> p k r", k=16)
        P.copy("pool", kb2[:, :, 0:64], sk)
        P.copy("pool", kb2[:, :, 64:128], sk)
        for k4 in range(4):
            for rc in range(2):
                ps = PS.get()
                pv = ps.v.bitcast(BF16)
                for kk in range(4):
                    kb = k4 * 4 + kk
                    P.tr(pv[:, kk * 128:(kk + 1) * 128], kv_tm[:, kb, rc * 128:(rc + 1) * 128], ident_b.v, inc=(kk == 3))
                P.copy("act" if rc else "dve", kv_fm[:, rc, k4 * 512:(k4 + 1) * 512], pv[:, 0:512])
            ps = PS.get()
            pv = ps.v.bitcast(BF16)
            for kk in range(4):
                kb = k4 * 4 + kk
                P.tr(pv[:, kk * 128:(kk + 1) * 128], kb2[:, kb, :], ident_b.v, inc=(kk == 3))
            P.copy("act", kpe_fm[:, k4 * 512:(k4 + 1) * 512], pv[:, 0:512])

    nkv = P.sbuf("nkv", [64, NSS, 384], BF16)

    def shared_kv(tile):
        norm_to_hn(xres, 8)
        wv = WS.get("wkv", 0)
        for (c0, nr, seq, pos0) in rowgroups(tile):
            ps = PS.get()
            for kc in range(KC):
                P.mm(ps[0:nr, 0:320], hn[:, kc, c0:c0 + nr], wv[:, kc, :], start=(kc == 0), stop=(kc == KC - 1))
            junk = tmpp.get()
            ssq = smallp.get()
            P.act(junk[0:nr, 0:KVR], ps[0:nr, 0:KVR], AF.Square, accum=ssq[0:nr, 0:1])
            P.act(ssq[0:nr, 0:1], ssq[0:nr, 0:1], AF.Sqrt, bias=EPS, scale=1.0 / KVR)
            P.recip(ssq[0:nr, 0:1], ssq[0:nr, 0:1])
            ck = tmpp.get()
            P.stt(ck[0:nr, 0:KVR], ps[0:nr, 0:KVR], ssq[0:nr, 0:1], gkv_bc[0:nr, :], ALU.mult, ALU.mult)
            chk("k_norm")
            kr = tmpp.get()
            P.copy("act", kr[0:nr, 0:64], ps[0:nr, 256:320])
            rope_any(ck[:, 256:320], kr[:, 0:64], slice(0, nr), cos_tm[0:nr, pos0 // 128, :], sin_tm[0:nr, pos0 // 128, :], 1)
            chk("k_rope")
            if seq < NPS:
                P.dma("sp", o_lat_p[seq, pos0:pos0 + nr, :], ck[0:nr, 0:KVR])
                P.dma("sp", o_kr_p[seq, pos0:pos0 + nr, :], ck[0:nr, 256:320])
            else:
                r0 = (seq - NPS) * DEC_SEQ
                P.dma("sp", o_lat_s[r0:r0 + nr, :], ck[0:nr, 0:KVR])
                P.dma("sp", o_kr_s[r0:r0 + nr, :], ck[0:nr, 256:320])
            chk("k_out")
            if seq < NPS:
                append_keys(ck, nr, pos0)
            else:
                P.copy("pool", nkv[0:nr, seq - NPS, 0:320], ck[0:nr, 0:320])
                P.copy("pool", nkv[0:nr, seq - NPS, 320:384], ck[0:nr, 256:320])

    def append_keys(ck, nr, pos0, bf=None):
        kb = pos0 // 128
        if bf is None:
            t = tmpp.get()
            bf = t.v.bitcast(BF16)
            P.copy("pool", bf[0:nr, 0:320], ck[0:nr, 0:320])
            P.copy("pool", bf[0:nr, 320:384], ck[0:nr, 256:320])
        P.copy("pool", kv_tm[0:nr, kb, :], bf[0:nr, 0:KVR])
        chk("a_cp")
        ps = PS.get()
        pv = ps.v.bitcast(BF16)
        for rc in range(2):
            P.tr(pv[:, rc * 128:rc * 128 + nr], bf[0:nr, rc * 128:(rc + 1) * 128], ident_b[0:nr, 0:nr], inc=False)
        P.tr(pv[:, 256:256 + nr], bf[0:nr, 256:384], ident_b[0:nr, 0:nr], inc=True)
        chk("a_tr")
        for rc in range(2):
            P.copy("dve", kv_fm[:, rc, pos0:pos0 + nr], pv[:, rc * 128:rc * 128 + nr])
        P.copy("dve", kpe_fm[:, pos0:pos0 + nr], pv[:, 256:256 + nr])

    def mla(tile):
        segs = tile["segs"]
        norm_to_hn(xres, 1, reuse_rs=True)
        cq_raw = G[7].v[:, 0:3 * T].rr("p (c t) -> p c t", c=3)
        cq = G[7].v.bitcast(BF16)[:, 2048:2048 + 3 * T].rr("p (c t) -> p c t", c=3)
        wv = WS.get("wdq", 0)
        for mc in range(3):
            ps = PS.get()
            for kc in range(KC):
                P.mm(ps[:, 0:T], wv[:, kc, mc * 128:(mc + 1) * 128], hn[:, kc, :], start=(kc == 0), stop=(kc == KC - 1))
            P.copy("act", cq_raw[:, mc, :], ps[:, 0:T])
        norm_to_hn(cq_raw, None, nk=3, dim=QR, out=cq, gcol=gq_col.v)
        qpe_fm = G[6].v.bitcast(BF16)[:, 0:8 * T].rr("p (j t) -> p j t", j=8)
        wv = WS.get("wuqr", 0)
        for st in range(ST):
            qr = G[2].v[:, 0:1024]
            for half in range(2):
                ps = PS.get()
                for c in range(3):
                    P.mm(ps.v, cq[:, c, st * 128:(st + 1) * 128], wv[:, c, half * 512:(half + 1) * 512],
                         start=(c == 0), stop=(c == 2))
                P.copy("act", qr[:, half * 512:(half + 1) * 512], ps.v)
            qrr = G[3].v.bitcast(BF16)[:, 0:1024]
            if tile["kind"] == "p":
                pos0 = segs[0][1] + st * 128
                rope_any(qrr, qr, slice(0, 128), cos_tm[:, pos0 // 128, :], sin_tm[:, pos0 // 128, :], MH)
            else:
                for hf in range(2):
                    rows = slice(hf * 64, (hf + 1) * 64)
                    rope_any(qrr, qr, rows, cos_hi[rows, :], sin_hi[rows, :], MH)
            ps = PS.get()
            pv = ps.v.bitcast(BF16)
            for j in range(8):
                P.tr(pv[:, j * 128:(j + 1) * 128], qrr[:, j * 128:(j + 1) * 128], ident_b.v, inc=(j == 7))
            P.copy("dve", qpe_fm[:, :, st * 128:(st + 1) * 128], pv.rr("p (j t) -> p j t", j=8))
        wuv = G[1].v.bitcast(BF16).rr("p (c m) -> p c m", c=2)
        P.dma("sp", G[1].v.bitcast(BF16), wsc["wuv"][0])
        wukT = wukT_sb.v
        if tile["kind"] == "p":
            seq, t0, L = segs[0]
            groups = [(0, T, seq, [(st * 128, 128, t0 + (st + 1) * 128, True) for st in range(ST)])]
        else:
            groups = [(i * 64, 64, sq, [(i * 64, 64, PAST + 64, False)]) for i, (sq, _, _) in enumerate(segs)]
        for (g0, gn, seq, qgs) in groups:
            gc = slice(g0, g0 + gn)
            if tile["kind"] == "s":
                cache_load(seq)
                append_keys(None, 64, PAST, bf=nkv[:, seq - NPS, :])
            qsets = [(G[2].v.bitcast(BF16)[:, 0:T], G[2].v.bitcast(BF16)[:, 1024:1024 + 2 * T].rr("p (c t) -> p c t", c=2)),
                     (G[7].v.bitcast(BF16)[:, 2048 + 3 * T:2048 + 4 * T],
                      G[7].v.bitcast(BF16)[:, 2048 + 4 * T:2048 + 6 * T].rr("p (c t) -> p c t", c=2))]
            wqs = {}
            s0_done = set()

            def att_s0(h):
                if h in s0_done or h >= MH:
                    return
                s0_done.add(h)
                h4, hh = divmod(h, 4)
                if hh == 0:
                    wqs[h4] = WS.get("wuqn", h4)
                wq = wqs[h4]
                qn, qlat = qsets[h % 2]
                ps = MISC
                for c in range(3):
                    P.mm(ps[:, 0:gn], wq[:, c, hh * 128:(hh + 1) * 128], cq[:, c, gc], start=(c == 0), stop=(c == 2))
                P.copy("act", qn[:, gc], ps[:, 0:gn])
                for rc in range(2):
                    P.mm(ps[:, 256 * rc:256 * rc + gn], wukT[:, h, rc * 128:(rc + 1) * 128], qn[:, gc])
                P.copy("act", qlat[:, :, gc], ps.v.rr("p (c t) -> p c t", c=2)[:, :, 0:gn])

            units = []
            gi = 0
            for h in range(MH):
                pr = slice((h % 2) * 64, (h % 2) * 64 + 64)
                for qi, (c0, nq, nk, diag) in enumerate(qgs):
                    nsb = (nk + 511) // 512
                    ob = OBS if tile["kind"] == "s" else (OBA if gi % 2 == 0 else OBB)
                    gi += 1
                    grp = dict(h=h, c0=c0, nq=nq, nk=nk, nsb=nsb, ob=ob)
                    for j in range(nsb):
                        units.append(dict(grp=grp, j=j, k0=j * 512, kn=min(512, nk - j * 512), last=(j == nsb - 1),
                                          diag=(diag and j == nsb - 1), qlat=qsets[h % 2][1], qpe=qpe_fm[pr, h // 2, :],
                                          pr=pr, h=h, head_first=(qi == 0 and j == 0)))

            def emit_s1(u):
                att_s0(u["h"])
                g = u["grp"]
                if "b1" not in g:
                    g["b1"], g["b2"], g["b3"] = smallp.get(), smallp.get(), smallp.get()
                att_s1(u)
                if u["head_first"]:
                    att_s0(u["h"] + 1)

            n = len(units)
            pend_b = [None]
            for t in range(n + 3):
                if 0 <= t - 2 < n:
                    att_s3a(units[t - 2])
                if t < n:
                    emit_s1(units[t])
                fin = None
                if 0 <= t - 3 < n:
                    att_s3b(units[t - 3])
                    if pend_b[0] is not None:
                        att_combine_b(pend_b[0], wuv)
                        pend_b[0] = None
                    if units[t - 3]["last"]:
                        fin = units[t - 3]["grp"]
                if 0 <= t - 1 < n:
                    att_s2(units[t - 1])
                if 0 <= t - 2 < n:
                    att_s3e(units[t - 2])
                if fin is not None:
                    att_combine_a(fin)
                    pend_b[0] = fin
            if pend_b[0] is not None:
                att_combine_b(pend_b[0], wuv)
                pend_b[0] = None
        for mb in range(4):
            wv = WS.get("wo", mb)
            for mc in range(2):
                ps = PS.get()
                for kc in range(16):
                    P.mm(ps[:, 0:T], wv[:, kc, mc * 128:(mc + 1) * 128], bufA[:, kc, :], start=(kc == 0), stop=(kc == 15))
                P.copy("act", mixb[:, mb * 2 + mc, :], ps[:, 0:T])
        postnorm_add(3)

    def rope_any(dst, srcv, rows, cs_t, sn_t, nh):
        nr = rows.stop - rows.start
        src = srcv[rows, :].rr("p (h e) -> p h e", h=nh)
        out = dst[rows, :].rr("p (h e) -> p h e", h=nh)
        cs = cs_t.unsq(1).bc([nr, nh, 32])
        sn = sn_t.unsq(1).bc([nr, nh, 32])
        t1 = tmpp.get()[rows, 0:nh * 32].rr("p (h e) -> p h e", h=nh)
        t2 = tmpp.get()[rows, 0:nh * 32].rr("p (h e) -> p h e", h=nh)
        x1 = src[:, :, 0:32]
        x2 = src[:, :, 32:64]
        P.tt("dve", t1, x1, cs, ALU.mult)
        P.tt("dve", t2, x2, sn, ALU.mult)
        P.tt("dve", out[:, :, 0:32], t1, t2, ALU.subtract)
        P.tt("dve", t1, x2, cs, ALU.mult)
        P.tt("dve", t2, x1, sn, ALU.mult)
        P.tt("dve", out[:, :, 32:64], t1, t2, ALU.add)

    cos_hi = P.sbuf("cos_hi", [128, 32], F32)
    sin_hi = P.sbuf("sin_hi", [128, 32], F32)
    P.dma("sp", cos_hi[0:64, :], cos_tm[0:64, 16, :])
    P.dma("sp", cos_hi[64:128, :], cos_tm[0:64, 16, :])
    P.dma("sp", sin_hi[0:64, :], sin_tm[0:64, 16, :])
    P.dma("sp", sin_hi[64:128, :], sin_tm[0:64, 16, :])

    PSB = PS.bufs
    SCp = Rot([PSB[0], PSB[1]])
    TRB = PSB[2]
    OBA = [PSB[3], PSB[4]]
    OBB = [PSB[5], PSB[6]]
    OBS = [PSB[3], PSB[4], PSB[5]]
    MISC = PSB[7]
    pbp = Rot([G[3], G[4]])
    ptp = Rot([G[5], G[0]])

    def att_s1(u):
        g = u["grp"]
        nq, c0, k0, kn, pr = g["nq"], g["c0"], u["k0"], u["kn"], u["pr"]
        ps = SCp.get()
        u["bank"] = ps
        qlat = u["qlat"]
        P.mm(ps[0:nq, 0:kn], qlat[:, 0, c0:c0 + nq], kv_fm[:, 0, k0:k0 + kn], start=True, stop=False)
        P.mm(ps[0:nq, 0:kn], qlat[:, 1, c0:c0 + nq], kv_fm[:, 1, k0:k0 + kn], start=False, stop=False)
        P.mm(ps[0:nq, 0:kn], u["qpe"][:, c0:c0 + nq], kpe_fm[pr, k0:k0 + kn], start=False, stop=not u["diag"],
             inc=not u["diag"])
        if u["diag"]:
            P.mm(ps[0:nq, kn - 128:kn], mrow[0:1, 0, :], mrow[0:1, 1, :], start=False, stop=True)

    def att_s2(u):
        g = u["grp"]
        nq, j, kn = g["nq"], u["j"], u["kn"]
        ps = u["bank"]
        mx = g["b1"][0:nq, j:j + 1]
        nb = g["b1"][0:nq, 8 + j:9 + j]
        P.reduce(mx, ps[0:nq, 0:kn], ALU.max)
        P.ts("dve", nb, mx, -SCALE, ALU.mult)
        pb = pbp.get().v.bitcast(BF16)
        u["pb"] = pb
        P.act(pb[0:nq, 0:kn], ps[0:nq, 0:kn], AF.Exp, bias=nb, scale=SCALE, accum=g["b2"][0:nq, j:j + 1])

    def att_s3a(u):
        g = u["grp"]
        nq, j, k0, kn = g["nq"], u["j"], u["k0"], u["kn"]
        pb = u["pb"]
        nkb = (kn + 127) // 128
        pv = TRB.v.bitcast(BF16).rr("p (k q) -> p k q", k=8)
        pt = ptp.get().v.bitcast(BF16)[:, 0:512].rr("p (k q) -> p k q", k=4)
        for kk in range(nkb):
            kc = min(128, kn - kk * 128)
            P.tr(pv[0:kc, kk, 0:nq], pb[0:nq, kk * 128:kk * 128 + kc], ident_b[0:nq, 0:nq], inc=(kk == nkb - 1))
        u["pv"] = pv
        u["pt"] = pt

    def att_s3e(u):
        g = u["grp"]
        nq, kn = g["nq"], u["kn"]
        nkb = (kn + 127) // 128
        pv, pt = u["pv"], u["pt"]
        full = kn // 128
        if full > 0:
            P.copy("dve", pt[:, 0:full, 0:nq], pv[:, 0:full, 0:nq])
        if full < nkb:
            kc = kn - full * 128
            P.copy("dve", pt[0:kc, full, 0:nq], pv[0:kc, full, 0:nq])
        u["pt"] = pt

    def att_s3b(u):
        g = u["grp"]
        nq, j, k0, kn = g["nq"], u["j"], u["k0"], u["kn"]
        nkb = (kn + 127) // 128
        pt = u["pt"]
        ob = g["ob"][j // 2]
        oc = (j % 2) * 256
        for kk in range(nkb):
            kc = min(128, kn - kk * 128)
            kb = k0 // 128 + kk
            P.mm(ob[0:nq, oc:oc + KVR], pt[0:kc, kk, 0:nq], kv_tm[0:kc, kb, :], start=(kk == 0), stop=(kk == nkb - 1))

    def att_combine_a(g):
        nq, nsb, c0, h = g["nq"], g["nsb"], g["c0"], g["h"]
        b1, b2, b3 = g["b1"], g["b2"], g["b3"]
        on = tmpp.get().v.bitcast(BF16)
        rl = b3[0:nq, 40:41]
        if nsb == 1:
            P.recip(rl, b2[0:nq, 0:1])
            P.act(on[0:nq, 0:KVR], g["ob"][0][0:nq, 0:KVR], AF.Identity, scale=rl)
        else:
            m = b3[0:nq, 41:42]
            P.reduce(m, b1[0:nq, 0:nsb], ALU.max)
            P.ts("dve", m, m, -SCALE, ALU.mult)
            ws = b3[0:nq, 0:nsb]
            P.act(ws, b1[0:nq, 0:nsb], AF.Exp, bias=m, scale=SCALE)
            lw = b3[0:nq, 16:16 + nsb]
            P.tt("dve", lw, ws, b2[0:nq, 0:nsb], ALU.mult)
            P.reduce(rl, lw, ALU.add)
            P.recip(rl, rl)
            acc = tmpp.get()
            for j in range(nsb):
                oj = g["ob"][j // 2][0:nq, (j % 2) * 256:(j % 2) * 256 + KVR]
                if j == 0:
                    P.ts("dve", acc[0:nq, 0:KVR], oj, ws[:, 0:1], ALU.mult)
                else:
                    P.stt(acc[0:nq, 0:KVR], oj, ws[:, j:j + 1], acc[0:nq, 0:KVR], ALU.mult, ALU.add)
            P.act(on[0:nq, 0:KVR], acc[0:nq, 0:KVR], AF.Identity, scale=rl)
        g["on"] = on

    def att_combine_b(g, wuv):
        nq, c0, h = g["nq"], g["c0"], g["h"]
        on = g["on"]
        ps = TRB
        pv = ps.v.bitcast(BF16)[:, 512:768]
        for rc in range(2):
            P.tr(pv[:, rc * 128:rc * 128 + nq], on[0:nq, rc * 128:(rc + 1) * 128], ident_b[0:nq, 0:nq], inc=(rc == 1))
        ol = tmpp.get().v.bitcast(BF16)
        P.copy("dve", ol[:, 0:256].rr("p (c q) -> p c q", c=2)[:, :, 0:nq], pv.rr("p (c q) -> p c q", c=2)[:, :, 0:nq])
        for rc in range(2):
            P.mm(ps[:, 384:384 + nq], wuv[:, rc, h * 128:(h + 1) * 128], ol[:, rc * 128:rc * 128 + nq], start=(rc == 0), stop=(rc == 1))
        P.copy("act", bufA[:, h, c0:c0 + nq], ps[:, 384:384 + nq])

    try:
        phase = (stop_after or {}).get("phase", 99)
        for ti, tile in enumerate(tiles):
            if phase < 2:
                break
            load_x(tile)
            tail_init(tile)
            if phase < 3:
                break
            mamba(tile)
            if phase < 4:
                break
            ffn(tile, 0)
            if phase < 5:
                break
            shared_kv(tile)
            if phase < 6:
                break
            mla(tile)
            if ti + 1 < len(tiles):
                fetch_x(tiles[ti + 1])
            if phase < 7:
                break
            ffn(tile, 1)
            tail_out(tile)
            store_y(tile)
    except _Stop:
        pass
    P.finish()
    return P


_CACHE = {}


def _consts():
    k = np.arange(128)
    l = np.arange(64)
    u2 = ((k[:, None] % 64) <= l[None, :]).astype(np.float32)
    delta = np.where((k[:, None] % 64) == (l[None, :] + 1), -NEGBIG, 0.0).astype(np.float32)
    pos = (np.arange(17)[None, :] * 128 + k[:, None]).astype(np.float32)
    jj = np.broadcast_to(np.arange(32, dtype=np.float32)[None, :], (128, 32)).copy()
    mrow = np.zeros((2, 128), np.float32)
    mrow[0, :64] = 1.0
    mrow[1, 64:] = NEGBIG
    return dict(k_ident=np.eye(128, dtype=np.float32), k_u2=u2, k_delta=delta, k_pos=pos, k_j=jj, k_mrow=mrow)


def kernel(x_prompt, x_sample, state_ssm, state_ssm_conv, state_ffn_conv, cache_kv_latent, cache_k_rope,
           norm_mix_pre, norm_mix_post, norm_ffn_pre, norm_ffn_post,
           ssm_w_in, ssm_conv_w, ssm_conv_b, ssm_dt_bias, ssm_a_log, ssm_d, ssm_norm, ssm_w_out,
           kv_norm_in, kv_w_dkv, kv_norm, kv_w_kr, kv_w_uk, kv_w_uv,
           mla_w_dq, mla_q_norm, mla_w_uq, mla_w_o,
           ffn_w_up, ffn_conv_w, ffn_conv_b, ffn_w_down, _stop_after=None):
    f = lambda a: np.ascontiguousarray(np.asarray(a, dtype=np.float32))
    if "prog" not in _CACHE or _stop_after is not None:
        _CACHE["prog"] = build_program(_stop_after)
    prog = _CACHE["prog"]
    shared = dict(
        g_mix_pre=f(norm_mix_pre), g_mix_post=f(norm_mix_post), g_ffn_pre=f(norm_ffn_pre), g_ffn_post=f(norm_ffn_post),
        w_in=f(ssm_w_in)[0], cw_ssm=f(ssm_conv_w)[0], cb_ssm=f(ssm_conv_b)[0], dt_bias=f(ssm_dt_bias)[0],
        a_log=f(ssm_a_log)[0], d_skip=f(ssm_d)[0], g_ssm=f(ssm_norm)[0], w_out=f(ssm_w_out)[0],
        g_kvin=f(kv_norm_in), w_dkv=f(kv_w_dkv), g_kv=f(kv_norm), w_kr=f(kv_w_kr),
        w_uk=f(kv_w_uk).reshape(KVR, MH * NOPE), w_uv=f(kv_w_uv).reshape(KVR, MH * VD),
        w_dq=f(mla_w_dq)[0], g_q=f(mla_q_norm)[0], w_uq=f(mla_w_uq)[0], w_o=f(mla_w_o)[0],
        w_up=f(ffn_w_up), cw_ffn=f(ffn_conv_w), cb_ffn=f(ffn_conv_b), w_dn=f(ffn_w_down),
    )
    shared.update(_consts())
    xp = f(x_prompt)
    xs_ = f(x_sample)
    sst = f(state_ssm)[0]
    ssc = f(state_ssm_conv)[0]
    sfc = f(state_ffn_conv)
    cl = f(cache_kv_latent)
    ck = f(cache_k_rope)
    in_maps = []
    for c in range(8):
        m = dict(shared)
        m["xp"] = xp[NPS * c:NPS * (c + 1)]
        m["xs"] = xs_[NSS * c:NSS * (c + 1)].reshape(NSS * DEC_SEQ, D)
        m["st_ssm"] = sst[NSS * c:NSS * (c + 1)].reshape(NSS, DI, DS)
        m["st_sconv"] = ssc[NSS * c:NSS * (c + 1)]
        m["st_fconv"] = np.ascontiguousarray(sfc[:, NSS * c:NSS * (c + 1)])
        m["c_lat"] = cl[NSS * c:NSS * (c + 1)]
        m["c_kr"] = ck[NSS * c:NSS * (c + 1)]
        in_maps.append(m)
    res = run_bass_kernel_spmd(prog.nc, in_maps, core_ids=list(range(8)))
    R = res.results
    cat = lambda name: [np.asarray(r[name]) for r in R]
    y_prompt = np.concatenate(cat("y_p"), axis=0)
    y_sample = np.concatenate([a.reshape(NSS, DEC_SEQ, D) for a in cat("y_s")], axis=0)
    ossm = cat("o_ssm")
    osc = cat("o_sconv")
    ofc = cat("o_fconv")
    p_ssm = np.concatenate([a[:NPS] for a in ossm], axis=0).reshape(1, 8 * NPS, NH, HD, DS)
    s_ssm = np.concatenate([a[NPS:] for a in ossm], axis=0).reshape(1, 8 * NSS, NH, HD, DS)
    p_sc = np.concatenate([a[:NPS] for a in osc], axis=0)[None]
    s_sc = np.concatenate([a[NPS:] for a in osc], axis=0)[None]
    p_fc = np.concatenate([a[:, :NPS] for a in ofc], axis=1)
    s_fc = np.concatenate([a[:, NPS:] for a in ofc], axis=1)
    p_lat = np.concatenate(cat("o_lat_p"), axis=0)
    p_kr = np.concatenate(cat("o_kr_p"), axis=0)
    s_lat = np.concatenate([a.reshape(NSS, DEC_SEQ, KVR) for a in cat("o_lat_s")], axis=0)
    s_kr = np.concatenate([a.reshape(NSS, DEC_SEQ, ROPE) for a in cat("o_kr_s")], axis=0)
    outs = (y_prompt, y_sample, p_ssm, p_sc, p_fc, p_lat, p_kr, s_ssm, s_sc, s_fc, s_lat, s_kr)
    return tuple(np.ascontiguousarray(o, dtype=np.float32) for o in outs)
```
